# Optimizing a Trainium2 kernel written in Bass

```python
import math
import jax
import jax.numpy as jnp
from jax import lax
import numpy as np

D_MODEL = 1024
BATCH = 16
SEQ = 2048
DEPTH = 4

GRID_W = 64
CTX_LEN = 256

HEAD_DIM = 64
ATT_Q_HEADS = 8
ATT_KV_HEADS = 2
ATT_GROUP = ATT_Q_HEADS // ATT_KV_HEADS
ATT_WIDTH = ATT_Q_HEADS * HEAD_DIM
KV_WIDTH = ATT_KV_HEADS * HEAD_DIM
WINDOW = 128
BLOCK = 128
ROPE_BASE = 10000.0
ROPE_PAIRS = HEAD_DIM // 4

CONV_WIDTH = 512
CONV_TAPS = 31

DN_HEADS = 4
DN_HEAD_DIM = 128
DN_WIDTH = DN_HEADS * DN_HEAD_DIM
SHORT_TAPS = 3
CHUNK = 64

BRANCH_WIDTH = 512
N_BRANCH = 3
DEEPNORM_ALPHA = (2 * DEPTH) ** 0.25
DEEPNORM_BETA = (8 * DEPTH) ** -0.25
LN_EPS = 1e-5
RMS_EPS = 1e-6
NEG_INF = -1e30

COL_NAMES = ('a_q', 'a_k', 'a_v', 'a_z', 'b_glu', 'b_z', 'c_qkv', 'c_z', 'c_a', 'c_b', 'gate')
COL_SIZES = (ATT_WIDTH, KV_WIDTH, KV_WIDTH, ATT_WIDTH, 2 * CONV_WIDTH, CONV_WIDTH,
             3 * DN_WIDTH, DN_WIDTH, 2 * DN_HEADS, 2 * DN_HEADS, N_BRANCH * D_MODEL)
IN_WIDTH = sum(COL_SIZES)

kernel_name = 'hybrid_diffusion_trunk'


def _layer_norm(t):
    tf = t.astype(jnp.float32)
    mu = jnp.mean(tf, -1, keepdims=True)
    var = jnp.mean(jnp.square(tf - mu), -1, keepdims=True)
    return ((tf - mu) * lax.rsqrt(var + LN_EPS)).astype(t.dtype)


def _split_cols(p):
    idx = np.cumsum(COL_SIZES)[:-1].tolist()
    return dict(zip(COL_NAMES, jnp.split(p, idx, axis=-1)))


def _axial_rope_tables(n):
    rows = n // GRID_W
    row = jnp.repeat(jnp.arange(rows, dtype=jnp.float32), GRID_W)
    col = jnp.tile(jnp.arange(GRID_W, dtype=jnp.float32), rows)
    inv = ROPE_BASE ** (-jnp.arange(ROPE_PAIRS, dtype=jnp.float32) / ROPE_PAIRS)
    ang = jnp.stack([row[:, None] * inv, col[:, None] * inv], axis=1)
    return jnp.cos(ang), jnp.sin(ang)


def _apply_axial_rope(t, cos, sin):
    t4 = t.reshape(*t.shape[:-1], 2, 2, ROPE_PAIRS)
    x1, x2 = t4[..., 0, :], t4[..., 1, :]
    cs = cos[None, :, None].astype(t.dtype)
    sn = sin[None, :, None].astype(t.dtype)
    out = jnp.stack([x1 * cs - x2 * sn, x2 * cs + x1 * sn], axis=-2)
    return out.reshape(t.shape)


def _band_ctx_attention(q, k, v, kc, vc, sink):
    B, n = q.shape[:2]
    nb = n // BLOCK
    L = kc.shape[1]
    scale = HEAD_DIM ** -0.5
    qb = q.reshape(B, nb, BLOCK, ATT_KV_HEADS, ATT_GROUP, HEAD_DIM)

    def band(t):
        tp = jnp.pad(t, ((0, 0), (BLOCK, BLOCK), (0, 0), (0, 0)))
        tp = tp.reshape(B, nb + 2, BLOCK, ATT_KV_HEADS, HEAD_DIM)
        return jnp.concatenate([tp[:, :-2], tp[:, 1:-1], tp[:, 2:]], axis=2)

    kb, vb = band(k), band(v)
    s_band = jnp.einsum('bnqhgd,bnkhd->bnhgqk', qb, kb).astype(jnp.float32) * scale
    qpos = jnp.arange(nb)[:, None] * BLOCK + jnp.arange(BLOCK)[None, :]
    kpos = (jnp.arange(nb)[:, None] - 1) * BLOCK + jnp.arange(3 * BLOCK)[None, :]
    kp = kpos[:, None, :]
    valid = (jnp.abs(kp - qpos[:, :, None]) <= WINDOW) & (kp >= 0) & (kp < n)
    s_band = jnp.where(valid[None, :, None, None], s_band, NEG_INF)
    s_ctx = jnp.einsum('bnqhgd,blhd->bnhgql', qb, kc).astype(jnp.float32) * scale
    s_sink = jnp.broadcast_to(sink.astype(jnp.float32).reshape(1, 1, ATT_KV_HEADS, ATT_GROUP, 1, 1),
                              s_band.shape[:-1] + (1,))
    p = jax.nn.softmax(jnp.concatenate([s_band, s_ctx, s_sink], axis=-1), axis=-1).astype(v.dtype)
    nk = 3 * BLOCK
    o = (jnp.einsum('bnhgqk,bnkhd->bnqhgd', p[..., :nk], vb)
         + jnp.einsum('bnhgql,blhd->bnqhgd', p[..., nk:nk + L], vc))
    return o.reshape(B, n, ATT_WIDTH)


def _ctx_attention(qc, kc, vc, sink):
    B, L = qc.shape[:2]
    scale = HEAD_DIM ** -0.5
    qg = qc.reshape(B, L, ATT_KV_HEADS, ATT_GROUP, HEAD_DIM)
    s = jnp.einsum('blhgd,bmhd->bhglm', qg, kc).astype(jnp.float32) * scale
    s_sink = jnp.broadcast_to(sink.astype(jnp.float32).reshape(1, ATT_KV_HEADS, ATT_GROUP, 1, 1),
                              s.shape[:-1] + (1,))
    p = jax.nn.softmax(jnp.concatenate([s, s_sink], axis=-1), axis=-1).astype(vc.dtype)
    o = jnp.einsum('bhglm,bmhd->blhgd', p[..., :L], vc)
    return o.reshape(B, L, ATT_WIDTH)


def _depthwise_conv(t, w):
    K = w.shape[0]
    return lax.conv_general_dilated(t, w[:, None, :].astype(t.dtype), window_strides=(1,),
                                    padding=[(K // 2, K // 2)],
                                    dimension_numbers=('NWC', 'WIO', 'NWC'),
                                    feature_group_count=t.shape[-1])


def _conformer_conv(glu_in, z, conv_w, conv_b, norm_g, norm_b):
    a, b = jnp.split(glu_in, 2, axis=-1)
    h = a * jax.nn.sigmoid(b)
    h = _depthwise_conv(h, conv_w) + conv_b
    h = _layer_norm(h) * norm_g + norm_b
    return jax.nn.silu(h) * jax.nn.silu(z)


def _l2norm(t):
    return t * lax.rsqrt(jnp.sum(t * t, axis=-1, keepdims=True) + RMS_EPS)


def _deltanet_qkv(qkv, conv_w):
    h = jax.nn.silu(_depthwise_conv(qkv, conv_w)).astype(jnp.float32)
    B, n = h.shape[:2]
    q, k, v = jnp.split(h, 3, axis=-1)
    shp = (B, n, DN_HEADS, DN_HEAD_DIM)
    q = _l2norm(q.reshape(shp)) * (DN_HEAD_DIM ** -0.5)
    k = _l2norm(k.reshape(shp))
    return q, k, v.reshape(shp)


def _decay_beta(a_cols, b_cols, a_log, dt_bias, d):
    sl = slice(d * DN_HEADS, (d + 1) * DN_HEADS)
    g = -jnp.exp(a_log[d].astype(jnp.float32)) * jax.nn.softplus(
        a_cols[..., sl].astype(jnp.float32) + dt_bias[d].astype(jnp.float32))
    beta = jax.nn.sigmoid(b_cols[..., sl].astype(jnp.float32))
    return g, beta


def _rev(t, d):
    return jnp.flip(t, axis=1) if d == 1 else t


def _chunk_gated_delta(q, k, v, g, beta, state, want_out):
    B, n, H, _ = k.shape
    dv = v.shape[-1]
    nc = n // CHUNK

    def to_chunks(t):
        t = t.reshape(B, nc, CHUNK, H, *t.shape[3:])
        return jnp.moveaxis(t, (1, 3), (0, 2))

    q, k, v, g, beta = [to_chunks(t) for t in (q, k, v, g, beta)]
    gc = jnp.cumsum(g, axis=-1)
    idx = jnp.arange(CHUNK)
    incl = idx[:, None] >= idx[None, :]
    strict = idx[:, None] > idx[None, :]
    dmask = jnp.exp(jnp.where(incl, gc[..., :, None] - gc[..., None, :], NEG_INF))
    kb = k * beta[..., None]
    A = jnp.where(strict, jnp.einsum('...id,...jd->...ij', kb, k) * dmask, 0.0)
    eye = jnp.eye(CHUNK, dtype=jnp.float32)
    T = lax.linalg.triangular_solve(eye + A, jnp.broadcast_to(eye, A.shape),
                                    left_side=True, lower=True, unit_diagonal=True)
    u = T @ (v * beta[..., None])
    w = T @ (kb * jnp.exp(gc)[..., None])
    gl = gc[..., -1:]
    k_dec = k * jnp.exp(gl - gc)[..., None]
    c_dec = jnp.exp(gl)[..., None]
    if want_out:
        q_dec = q * jnp.exp(gc)[..., None]
        aqk = jnp.where(incl, jnp.einsum('...id,...jd->...ij', q, k) * dmask, 0.0)

        def step(S, xs):
            u_c, w_c, kd_c, cd_c, qd_c, a_c = xs
            v_new = u_c - w_c @ S
            o = qd_c @ S + a_c @ v_new
            S = S * cd_c + jnp.swapaxes(kd_c, -1, -2) @ v_new
            return S, o

        S, o = lax.scan(step, state, (u, w, k_dec, c_dec, q_dec, aqk))
        return jnp.moveaxis(o, (0, 2), (1, 3)).reshape(B, n, H, dv), S

    def step_state(S, xs):
        u_c, w_c, kd_c, cd_c = xs
        v_new = u_c - w_c @ S
        return S * cd_c + jnp.swapaxes(kd_c, -1, -2) @ v_new, None

    S, _ = lax.scan(step_state, state, (u, w, k_dec, c_dec))
    return None, S


def _gated_rmsnorm(o, z, gain):
    o = o * lax.rsqrt(jnp.mean(o * o, axis=-1, keepdims=True) + RMS_EPS) * gain.astype(jnp.float32)
    return o.reshape(*o.shape[:2], DN_WIDTH).astype(z.dtype) * jax.nn.silu(z)


def _merge(ys, gate_logits, w_branch, w_out):
    gl = gate_logits.reshape(*gate_logits.shape[:-1], N_BRANCH, D_MODEL)
    merged = jax.nn.sigmoid(gl[..., 0, :]) * (ys[0] @ w_branch[0])
    for r in range(1, N_BRANCH):
        merged = merged + jax.nn.sigmoid(gl[..., r, :]) * (ys[r] @ w_branch[r])
    return merged @ w_out


def _layer(x, ctx, c, c_ctx, w_ada, b_ada, w_in, a_sink, b_conv_w, b_conv_b, b_norm_g, b_norm_b,
           c_conv_w, c_a_log, c_dt_bias, c_norm_g, w_branch, w_out, ln_g, ln_b, rope, need_ctx_out):
    B, n = x.shape[:2]
    L = ctx.shape[1]
    cos, sin = rope
    shift, scale, gate = jnp.split(jax.nn.silu(c) @ w_ada + b_ada, 3, axis=-1)
    shift_c, scale_c, gate_c = jnp.split(jax.nn.silu(c_ctx) @ w_ada + b_ada, 3, axis=-1)
    u = x * (1 + scale[:, None]) + shift[:, None]
    uc = ctx * (1 + scale_c) + shift_c
    P = _split_cols(u @ w_in)
    Pc = _split_cols(uc @ w_in)

    q = _apply_axial_rope(P['a_q'].reshape(B, n, ATT_Q_HEADS, HEAD_DIM), cos, sin)
    k = _apply_axial_rope(P['a_k'].reshape(B, n, ATT_KV_HEADS, HEAD_DIM), cos, sin)
    v = P['a_v'].reshape(B, n, ATT_KV_HEADS, HEAD_DIM)
    kc = Pc['a_k'].reshape(B, L, ATT_KV_HEADS, HEAD_DIM)
    vc = Pc['a_v'].reshape(B, L, ATT_KV_HEADS, HEAD_DIM)
    y_a = _band_ctx_attention(q, k, v, kc, vc, a_sink) * jax.nn.silu(P['a_z'])

    y_b = _conformer_conv(P['b_glu'], P['b_z'], b_conv_w, b_conv_b, b_norm_g, b_norm_b)

    qkv_l = _deltanet_qkv(P['c_qkv'], c_conv_w)
    qkv_c = _deltanet_qkv(Pc['c_qkv'], c_conv_w)
    state0 = jnp.zeros((B, DN_HEADS, DN_HEAD_DIM, DN_HEAD_DIM), jnp.float32)
    o_lat = None
    o_ctx = None
    for d in range(2):
        g_l, be_l = _decay_beta(P['c_a'], P['c_b'], c_a_log, c_dt_bias, d)
        g_c, be_c = _decay_beta(Pc['c_a'], Pc['c_b'], c_a_log, c_dt_bias, d)
        oc, s_ctx = _chunk_gated_delta(*[_rev(t, d) for t in (*qkv_c, g_c, be_c)], state0, need_ctx_out)
        ol, _ = _chunk_gated_delta(*[_rev(t, d) for t in (*qkv_l, g_l, be_l)], s_ctx, True)
        ol = _rev(ol, d)
        o_lat = ol if o_lat is None else o_lat + ol
        if need_ctx_out:
            oc = _rev(oc, d)
            o_ctx = oc if o_ctx is None else o_ctx + oc
    y_c = _gated_rmsnorm(o_lat, P['c_z'], c_norm_g)

    out = _merge((y_a, y_b, y_c), P['gate'], w_branch, w_out)
    x_new = _layer_norm(DEEPNORM_ALPHA * x + gate[:, None] * out) * ln_g + ln_b

    if need_ctx_out:
        qc = Pc['a_q'].reshape(B, L, ATT_Q_HEADS, HEAD_DIM)
        yc_a = _ctx_attention(qc, kc, vc, a_sink) * jax.nn.silu(Pc['a_z'])
        yc_b = _conformer_conv(Pc['b_glu'], Pc['b_z'], b_conv_w, b_conv_b, b_norm_g, b_norm_b)
        yc_c = _gated_rmsnorm(o_ctx, Pc['c_z'], c_norm_g)
        out_c = _merge((yc_a, yc_b, yc_c), Pc['gate'], w_branch, w_out)
        ctx = _layer_norm(DEEPNORM_ALPHA * ctx + gate_c * out_c) * ln_g + ln_b
    return x_new, ctx


def setup_inputs(seed: int = 0) -> dict:
    key = jax.random.key(seed)
    ks = jax.random.split(key, 24)

    def nrm(k, shape, s):
        return jax.random.normal(k, shape, jnp.float32) * s

    x = nrm(ks[0], (BATCH, SEQ, D_MODEL), 1.0)
    c = nrm(ks[1], (BATCH, D_MODEL), 1.0)
    ctx = nrm(ks[2], (BATCH, CTX_LEN, D_MODEL), 1.0)
    c_ctx = nrm(ks[3], (D_MODEL,), 1.0)
    w_ada = nrm(ks[4], (DEPTH, D_MODEL, 3 * D_MODEL), D_MODEL ** -0.5)
    b_ada = nrm(ks[5], (DEPTH, 3 * D_MODEL), 0.02)
    w_in = nrm(ks[6], (DEPTH, D_MODEL, IN_WIDTH), D_MODEL ** -0.5)
    a_sink = nrm(ks[7], (DEPTH, ATT_Q_HEADS), 0.5)
    b_conv_w = nrm(ks[8], (DEPTH, CONV_TAPS, CONV_WIDTH), CONV_TAPS ** -0.5)
    b_conv_b = nrm(ks[9], (DEPTH, CONV_WIDTH), 0.02)
    b_norm_g = 1.0 + nrm(ks[10], (DEPTH, CONV_WIDTH), 0.02)
    b_norm_b = nrm(ks[11], (DEPTH, CONV_WIDTH), 0.02)
    c_conv_w = nrm(ks[12], (DEPTH, SHORT_TAPS, 3 * DN_WIDTH), SHORT_TAPS ** -0.5)
    c_a_log = jnp.log(jax.random.uniform(ks[13], (DEPTH, 2, DN_HEADS), jnp.float32, 1.0, 16.0))
    dt = jnp.exp(jax.random.uniform(ks[14], (DEPTH, 2, DN_HEADS), jnp.float32,
                                    math.log(1e-3), math.log(1e-1)))
    c_dt_bias = dt + jnp.log(-jnp.expm1(-dt))
    c_norm_g = 1.0 + nrm(ks[15], (DEPTH, DN_HEAD_DIM), 0.02)
    w_branch = nrm(ks[16], (DEPTH, N_BRANCH, BRANCH_WIDTH, D_MODEL), BRANCH_WIDTH ** -0.5 * DEEPNORM_BETA)
    w_out = nrm(ks[17], (DEPTH, D_MODEL, D_MODEL), D_MODEL ** -0.5 * DEEPNORM_BETA)
    ln_g = 1.0 + nrm(ks[18], (DEPTH, D_MODEL), 0.02)
    ln_b = nrm(ks[19], (DEPTH, D_MODEL), 0.02)
    return {'x': x, 'c': c, 'ctx': ctx, 'c_ctx': c_ctx, 'w_ada': w_ada, 'b_ada': b_ada, 'w_in': w_in,
            'a_sink': a_sink, 'b_conv_w': b_conv_w, 'b_conv_b': b_conv_b, 'b_norm_g': b_norm_g,
            'b_norm_b': b_norm_b, 'c_conv_w': c_conv_w, 'c_a_log': c_a_log, 'c_dt_bias': c_dt_bias,
            'c_norm_g': c_norm_g, 'w_branch': w_branch, 'w_out': w_out, 'ln_g': ln_g, 'ln_b': ln_b}


def reference(x, c, ctx, c_ctx, w_ada, b_ada, w_in, a_sink, b_conv_w, b_conv_b, b_norm_g, b_norm_b,
              c_conv_w, c_a_log, c_dt_bias, c_norm_g, w_branch, w_out, ln_g, ln_b):
    rope = _axial_rope_tables(x.shape[1])
    for l in range(DEPTH):
        x, ctx = _layer(x, ctx, c, c_ctx, w_ada[l], b_ada[l], w_in[l], a_sink[l], b_conv_w[l],
                        b_conv_b[l], b_norm_g[l], b_norm_b[l], c_conv_w[l], c_a_log[l], c_dt_bias[l],
                        c_norm_g[l], w_branch[l], w_out[l], ln_g[l], ln_b[l], rope, l < DEPTH - 1)
    return x
```

```python
import numpy as np
import concourse.bass as bass
import concourse.mybir as mybir

F32 = mybir.dt.float32
BF16 = mybir.dt.bfloat16
AF = mybir.ActivationFunctionType
ALU = mybir.AluOpType
AX = mybir.AxisListType

_DT_SIZE = {F32: 4, BF16: 2}


def _dsize(dt):
    if dt in _DT_SIZE:
        return _DT_SIZE[dt]
    s = str(dt)
    if '32' in s:
        return 4
    if '16' in s:
        return 2
    if '8' in s:
        return 1
    if '64' in s:
        return 8
    raise ValueError(s)


def _prod(xs):
    r = 1
    for x in xs:
        r *= int(x)
    return r


def box(ap):
    t = ap.tensor
    name = t.name
    esz = _dsize(ap.dtype)
    off = int(ap.offset) * esz
    aps = ap.ap
    sp = str(ap.space)
    if sp == 'PSUM':
        return (name, 0, 128, 0, 2048)
    if sp in ('SB', 'PSUM'):
        pbytes = _prod(t.shape[1:]) * _dsize(t.dtype)
        plo = off // pbytes
        flo = off % pbytes
        pcnt = aps[0][1] if aps[0][0] != 0 else 1
        ext = sum((c - 1) * abs(s) for s, c in aps[1:]) + 1
        return (name, plo, plo + pcnt, flo, flo + ext * esz)
    ext = sum((c - 1) * abs(s) for s, c in aps) + 1
    return (name, 0, 1, off, off + ext * esz)


def _ovl(a, b):
    return a[1] < b[2] and b[1] < a[2] and a[3] < b[4] and b[3] < a[4]


def _covers(a, b):
    return a[1] <= b[1] and a[2] >= b[2] and a[3] <= b[3] and a[4] >= b[4]


ENGINES = ('pe', 'act', 'dve', 'pool', 'sp')
SEM_EPOCH = 30000
NDMA_SEMS = 12
SAME_ENGINE_DIST = 3


class Prog:
    def __init__(self, nc):
        self.nc = nc
        self.streams = {e: [] for e in ENGINES}
        self.count = {e: 0 for e in ENGINES}
        self.known = {e: {} for e in ENGINES}
        self.wr = {}
        self.rd = {}
        self.dma_val = {}
        self.dma_next = {e: 0 for e in ENGINES}
        self.n_waits = 0
        self.sem_handles = {}
        self.needed_sems = set()
        self.ninst = 0

    def _deps_for(self, reads, writes):
        deps = {}

        def add(key, val):
            if deps.get(key, -1) < val:
                deps[key] = val
        rboxes = [box(a) for a in reads]
        wboxes = [box(a) for a in writes]
        for b in rboxes:
            w = self.wr.get(b[0])
            if w:
                for ob, (k, v) in w.items():
                    if _ovl(b, ob):
                        add(k, v)
        for b in wboxes:
            w = self.wr.get(b[0])
            if w:
                for ob, (k, v) in w.items():
                    if _ovl(b, ob):
                        add(k, v)
            r = self.rd.get(b[0])
            if r:
                for (ob, k), v in r.items():
                    if _ovl(b, ob):
                        add(k, v)
        return deps, rboxes, wboxes

    def _commit(self, rboxes, wboxes, key, val):
        for b in wboxes:
            w = self.wr.setdefault(b[0], {})
            for ob in [ob for ob in w if _covers(b, ob)]:
                del w[ob]
            w[b] = (key, val)
            r = self.rd.get(b[0])
            if r:
                for rk in [rk for rk in r if _covers(b, rk[0])]:
                    del r[rk]
        for b in rboxes:
            self.rd.setdefault(b[0], {})[(b, key)] = val

    def _emit_waits(self, eng, deps, self_seq=None):
        kn = self.known[eng]
        for key, val in deps.items():
            if key == ('e', eng):
                if eng == 'pe':
                    continue
            if kn.get(key, 0) >= val:
                continue
            kn[key] = val
            self.n_waits += 1
            self.needed_sems.add(self._semname(key, val))
            self.streams[eng].append(('wait', key, val))

    def _semname(self, key, val):
        if key[0] == 'e':
            return ('e', key[1], (val - 1) // SEM_EPOCH)
        return key

    def op(self, eng, fn, reads=(), writes=()):
        deps, rb, wb = self._deps_for(reads, writes)
        seq = self.count[eng] + 1
        self._emit_waits(eng, deps, seq)
        self.count[eng] = seq
        self.ninst += 1
        self.needed_sems.add(('e', eng, (seq - 1) // SEM_EPOCH))
        self.streams[eng].append(('op', fn, seq))
        self._commit(rb, wb, ('e', eng), seq)

    def dma(self, out, in_, queue='sp', **kw):
        deps, rb, wb = self._deps_for([in_], [out])
        i = self.dma_next[queue]
        self.dma_next[queue] = (i + 1) % NDMA_SEMS
        key = ('d', queue, i)
        prev = self.dma_val.get(key, 0)
        if prev:
            deps[key] = max(deps.get(key, 0), prev)
        self._emit_waits(queue, deps)
        val = prev + 16
        self.dma_val[key] = val
        self.ninst += 1
        self.needed_sems.add(key)
        self.streams[queue].append(('dma', out, in_, key, kw))
        self._commit(rb, wb, key, val)
        return key, val

    def wait_all(self, eng):
        deps = {}
        for key, val in self.dma_val.items():
            deps[key] = val
        for e in ENGINES:
            if e != eng and self.count[e]:
                deps[('e', e)] = self.count[e]
        self._emit_waits(eng, deps)

    def emit(self):
        nc = self.nc
        from contextlib import ExitStack
        with ExitStack() as st:
            sems = {}
            for sn in sorted(self.needed_sems, key=str):
                sems[sn] = st.enter_context(nc.semaphore("s_" + "_".join(str(x) for x in sn)))
            block = st.enter_context(nc.Block())
            streams = self.streams

            def run(eng_name):
                def body(e):
                    for item in streams[eng_name]:
                        if item[0] == 'wait':
                            _, key, val = item
                            if key[0] == 'e':
                                ep = (val - 1) // SEM_EPOCH
                                e.wait_ge(sems[('e', key[1], ep)], val - ep * SEM_EPOCH)
                            else:
                                e.wait_ge(sems[key], val)
                        elif item[0] == 'op':
                            _, fn, seq = item
                            ins = fn(e)
                            ep = (seq - 1) // SEM_EPOCH
                            ins.then_inc(sems[('e', eng_name, ep)], 1)
                        else:
                            _, out, in_, key, kw = item
                            e.dma_start(out=out, in_=in_, **kw).then_inc(sems[key], 16)
                return body
            if streams['sp']:
                block.sync(run('sp'))
            if streams['pe']:
                block.tensor(run('pe'))
            if streams['act']:
                block.scalar(run('act'))
            if streams['dve']:
                block.vector(run('dve'))
            if streams['pool']:
                block.gpsimd(run('pool'))

    def mm(self, out, lhsT, rhs, start=True, stop=True):
        self.op('pe', lambda e: e.matmul(out, lhsT, rhs, start=start, stop=stop),
                reads=[lhsT, rhs] + ([] if start else [out]), writes=[out])

    def transpose(self, out, in_, ident):
        self.op('pe', lambda e: e.transpose(out, in_, ident), reads=[in_, ident], writes=[out])

    def act(self, out, in_, func, bias=None, scale=1.0, accum_out=None):
        reads = [in_]
        writes = [out]
        kw = {}
        if bias is not None:
            kw['bias'] = bias
            if not isinstance(bias, (int, float)):
                reads.append(bias)
        if not isinstance(scale, (int, float)):
            reads.append(scale)
        kw['scale'] = scale
        if accum_out is not None:
            kw['accum_out'] = accum_out
            writes.append(accum_out)
        self.op('act', lambda e: e.activation(out, in_, func, **kw), reads=reads, writes=writes)

    def tt(self, eng, out, in0, in1, op):
        self.op(eng, lambda e: e.tensor_tensor(out, in0, in1, op), reads=[in0, in1], writes=[out])

    def ts(self, eng, out, in0, s1, s2, op0, op1=None):
        reads = [in0] + [s for s in (s1, s2) if s is not None and not isinstance(s, (int, float))]
        if op1 is None:
            self.op(eng, lambda e: e.tensor_scalar(out, in0, s1, None, op0), reads=reads, writes=[out])
        else:
            self.op(eng, lambda e: e.tensor_scalar(out, in0, s1, s2, op0, op1), reads=reads, writes=[out])

    def stt(self, eng, out, in0, scalar, in1, op0, op1):
        reads = [in0, in1] + ([] if isinstance(scalar, (int, float)) else [scalar])
        self.op(eng, lambda e: e.scalar_tensor_tensor(out, in0, scalar, in1, op0, op1), reads=reads, writes=[out])

    def copy(self, eng, out, in_):
        if eng == 'act':
            self.op('act', lambda e: e.copy(out, in_), reads=[in_], writes=[out])
        else:
            self.op(eng, lambda e: e.tensor_copy(out, in_), reads=[in_], writes=[out])

    def memset(self, eng, ap, val):
        self.op(eng, lambda e: e.memset(ap, val), reads=[], writes=[ap])

    def recip(self, out, in_):
        self.op('dve', lambda e: e.reciprocal(out, in_), reads=[in_], writes=[out])

    def reduce(self, eng, out, in_, op, axis=None):
        axis = axis if axis is not None else AX.X
        self.op(eng, lambda e: e.tensor_reduce(out, in_, axis, op), reads=[in_], writes=[out])

import numpy as np
from concourse.bass_utils import run_bass_kernel_spmd

from contextlib import ExitStack
import math

D = 1024
T = 2304
NBLK = 18
NLAT = 2048
NCTX = 256
INW = 7952
OFF = {'a_q': 0, 'a_k': 512, 'a_v': 640, 'a_z': 768, 'b_a': 1280, 'b_b': 1792, 'b_z': 2304,
       'c_qkv': 2816, 'c_z': 4352, 'c_a': 4864, 'c_b': 4872, 'gate': 4880}
TTILES = [(0, 256)] + [(256 + 512 * i, 512) for i in range(4)]
ALPHA = 8 ** 0.25
LN_EPS = 1e-5
RMS_EPS = 1e-6
NSLOT = 4


def host_consts():
    c = {}
    c['identf'] = np.eye(128, dtype=np.float32)
    inv = 10000.0 ** (-np.arange(16, dtype=np.float32) / 16)
    pos = np.arange(NLAT)
    row = (pos // 64).astype(np.float32)
    col = (pos % 64).astype(np.float32)
    cosT = np.zeros((128, NLAT), np.float32)
    sinT = np.zeros((128, NLAT), np.float32)
    rmat = np.zeros((128, 128), np.float32)
    for p in range(128):
        d = p % 64
        axis = d // 32
        half = (d % 32) // 16
        pr = d % 16
        ang = (row if axis == 0 else col) * inv[pr]
        cosT[p] = np.cos(ang)
        sinT[p] = np.sin(ang) * (-1.0 if half == 0 else 1.0)
        partner = p + 16 if half == 0 else p - 16
        rmat[partner, p] = 1.0
    c['cosT'] = cosT
    c['sinT'] = sinT
    c['rmat'] = rmat
    k = np.arange(128)[:, None]
    q = np.arange(128)[None, :]
    bm = np.zeros((128, 3, 128), np.float32)
    bm[:, 0, :] = (k >= q)
    bm[:, 1, :] = 1.0
    bm[:, 2, :] = (k <= q)
    c['bandmask'] = bm.reshape(128, 384)
    j = np.arange(128)[:, None]
    i = np.arange(128)[None, :]
    same = np.ones((128, 128), bool)
    dn = np.zeros((7, 128, 128), np.float32)
    dn[0] = same & (j <= i)
    dn[1] = same & (j >= i)
    dn[2] = same
    dn[3] = same & (i > j)
    dn[4] = same & (i >= j)
    dn[5] = same & (i < j)
    dn[6] = same & (i <= j)
    c['dnmask'] = np.ascontiguousarray(dn.transpose(1, 0, 2)).reshape(128, 7 * 128)
    lv = np.zeros((128, 7, 128), np.float32)
    for li, sz in enumerate((1, 2, 4, 8, 16, 32, 64)):
        lv[:, li, :] = ((j // (2 * sz)) == (i // (2 * sz))) & ((j // sz) != (i // sz))
    c['lvlmask'] = lv.reshape(128, 7 * 128)
    return c


def build(n_layers=4, nseq=2, dbg=None, stop=None):
    nc = bass.Bass("TRN2", target_bir_lowering=False)
    P = Prog(nc)

    def din(name, shape, dt=F32):
        return nc.dram_tensor(name, list(shape), dt, kind="ExternalInput").ap()

    x_in = din("x", [2, NLAT, D])
    ctx_in = din("ctx", [2, NCTX, D])
    c_in = din("c", [2, D])
    cctx_in = din("c_ctx", [1, D])
    w_ada = din("w_ada", [4, D, 3 * D])
    b_ada = din("b_ada", [4, 3 * D])
    w_in = din("w_in", [4, D, INW])
    a_sink = din("a_sink", [4, 8])
    b_conv_w = din("b_conv_w", [4, 31, 512])
    b_conv_b = din("b_conv_b", [4, 512])
    b_norm_g = din("b_norm_g", [4, 512])
    b_norm_b = din("b_norm_b", [4, 512])
    c_conv_w = din("c_conv_w", [4, 3, 1536])
    c_a_log = din("c_a_log", [4, 8])
    c_dt_bias = din("c_dt_bias", [4, 8])
    c_norm_g = din("c_norm_g", [4, 128])
    w_branch = din("w_branch", [4, 1536, D])
    w_out = din("w_out", [4, D, D])
    ln_g = din("ln_g", [4, D])
    ln_b = din("ln_b", [4, D])
    k_identf = din("identf", [128, 128])
    k_cosT = din("cosT", [128, NLAT])
    k_sinT = din("sinT", [128, NLAT])
    k_rmat = din("rmat", [128, 128])
    k_bandmask = din("bandmask", [128, 384])
    k_dnmask = din("dnmask", [128, 7 * 128])
    k_lvl = din("lvlmask", [128, 7 * 128])
    y_out = nc.dram_tensor("y", [2, NLAT, D], F32, kind="ExternalOutput").ap()

    def dscr(name, shape, dt):
        return nc.dram_tensor(name, list(shape), dt, kind="Internal").ap()

    xres = dscr("xres", [2, T, D], F32)
    wbf_in = dscr("wbf_in", [4, D, INW], BF16)
    wbf_br = dscr("wbf_br", [4, 1536, D], BF16)
    wbf_out = dscr("wbf_out", [4, D, D], BF16)
    modrow_d = dscr("modrow", [4, 3, 3 * D], F32)
    yT_d = dscr("yT", [3, 512, T], BF16)
    wbf_g = dscr("wbf_g", [4, 8, 128, 8, 3, 128], BF16)

    _uq = {'n': 0}

    def uniq(name):
        _uq['n'] += 1
        return "%s_u%d" % (name, _uq['n'])

    dbg_out = {}
    if dbg:
        for name, (shape, dt) in dbg.items():
            dbg_out[name] = nc.dram_tensor("dbg_" + name, list(shape), dt, kind="ExternalOutput").ap()

    with ExitStack() as gst:
        def gsb(name, shape, dt):
            return gst.enter_context(nc.sbuf_tensor(uniq(name), list(shape), dt))

        psums = [gst.enter_context(nc.psum_tensor("ps%d" % i, [128, 512], F32)) for i in range(8)]
        ps_state = {'i': 0, 'pool': list(range(8))}

        def set_pool(pool):
            ps_state['pool'] = list(pool)
            ps_state['i'] = 0

        def psum():
            pool = ps_state['pool']
            t = psums[pool[ps_state['i'] % len(pool)]]
            ps_state['i'] += 1
            return t

        identf = gsb("identf", [128, 128], F32)
        identb = gsb("identb", [128, 128], BF16)
        onesf = gsb("onesf", [128, 128], F32)
        modT = gsb("modT", [128, 4, 24, 3], F32)
        uT = gsb("uT", [128, 8, T], BF16)

        P.dma(identf[:], k_identf)
        P.copy('dve', identb[:], identf[:])
        P.memset('dve', onesf[:], 1.0)

        P.marks = []

        def barrier():
            P.marks.append(dict(P.count))
            for e in ('sp', 'pe', 'act', 'dve', 'pool'):
                P.wait_all(e)

        rr = {'i': 0}

        def rot(engs=('dve', 'act', 'pool')):
            rr['i'] += 1
            return engs[rr['i'] % len(engs)]

        for s in range(nseq):
            P.dma(xres[s, 0:NCTX, :], ctx_in[s])
            P.dma(xres[s, NCTX:T, :], x_in[s])

        with ExitStack() as st:
            def sb(name, shape, dt):
                return st.enter_context(nc.sbuf_tensor(uniq(name), list(shape), dt))
            stg = [sb("stg%d" % i, [128, 2048], F32) for i in range(3)]
            o16 = [sb("o16_%d" % i, [128, 2048], BF16) for i in range(3)]
            engs = ['dve', 'act', 'pool']
            cnt = 0
            gstg = [sb("gstg%d" % i, [128, 3072], F32) for i in range(2)]
            g16 = [sb("g16_%d" % i, [128, 3072], BF16) for i in range(2)]

            def cast_rows(src, dst, ncols):
                nonlocal cnt
                c0 = 0
                while c0 < ncols:
                    n = min(2048, ncols - c0)
                    i = cnt % 3
                    cnt += 1
                    P.dma(stg[i][:, 0:n], src[:, c0:c0 + n])
                    P.copy(engs[i], o16[i][:, 0:n], stg[i][:, 0:n])
                    P.dma(dst[:, c0:c0 + n], o16[i][:, 0:n], queue='pool' if False else 'sp')
                    c0 += n
            for l in range(1):
                for k in range(8):
                    cast_rows(w_in[l, k * 128:(k + 1) * 128, :], wbf_in[l, k * 128:(k + 1) * 128, :], OFF['gate'])
                for k in range(8):
                    gi = k % 2
                    P.dma(gstg[gi][:], w_in[l, k * 128:(k + 1) * 128, OFF['gate']:INW])
                    P.copy('dve' if gi == 0 else 'act', g16[gi][:], gstg[gi][:])
                    for r in range(3):
                        P.dma(wbf_g[l].rearrange("m p k r c -> p k r m c")[:, k, r],
                              g16[gi][:, r * 1024:(r + 1) * 1024].rearrange("p (m c) -> p m c", m=8))
                for k in range(12):
                    cast_rows(w_branch[l, k * 128:(k + 1) * 128, :], wbf_br[l, k * 128:(k + 1) * 128, :], D)
                for k in range(8):
                    cast_rows(w_out[l, k * 128:(k + 1) * 128, :], wbf_out[l, k * 128:(k + 1) * 128, :], D)

            if stop == 'W':
                barrier()
                P.emit()
                return nc, P
            cT = sb("cT", [128, 8, 3], F32)
            scT = sb("scT", [128, 8, 3], F32)
            crow = sb("crow", [3, D], F32)
            P.dma(crow[0:2, :], c_in)
            P.dma(crow[2:3, :], cctx_in)
            pt = psum()
            for k in range(8):
                P.transpose(pt[:, k * 3:(k + 1) * 3], crow[0:3, k * 128:(k + 1) * 128], identf[0:3, 0:3])
            P.copy('dve', cT[:].rearrange("p k r -> p (k r)"), pt[:, 0:24])
            P.act(scT[:], cT[:], AF.Silu)
            wa = [sb("wa%d" % i, [128, 3 * D], F32) for i in range(2)]
            bada = sb("bada", [3, 3 * D], F32)
            mrow = sb("mrow", [3, 3 * D], F32)
            for l in range(n_layers):
                P.dma(bada[:], b_ada[l:l + 1, :].partition_broadcast(3))
                banks = [psums[i] for i in range(6)]
                for k in range(8):
                    w = wa[k % 2]
                    P.dma(w[:], w_ada[l, k * 128:(k + 1) * 128, :])
                    for j in range(6):
                        P.mm(banks[j][0:3, :], scT[:, k, :], w[:, j * 512:(j + 1) * 512], start=(k == 0), stop=(k == 7))
                for j in range(6):
                    P.tt('dve', mrow[:, j * 512:(j + 1) * 512], banks[j][0:3, :], bada[:, j * 512:(j + 1) * 512], ALU.add)
                P.dma(modrow_d[l], mrow[:])
                pt = psum()
                for j in range(24):
                    P.transpose(pt[:, j * 3:(j + 1) * 3], mrow[0:3, j * 128:(j + 1) * 128], identf[0:3, 0:3])
                P.copy('dve', modT[:, l].rearrange("p j r -> p (j r)"), pt[:, 0:72])
                P.ts('dve', modT[:, l, 8:16, :], modT[:, l, 8:16, :], 1.0, None, ALU.add)
            barrier()

        if stop == '0':
            P.emit()
            return nc, P
        wq_v = [wbf_in[l].rearrange("(k p) c -> p k c", p=128) for l in range(4)]
        wbr_v = [wbf_br[l].rearrange("(k p) c -> p k c", p=128) for l in range(4)]
        wout_v = [wbf_out[l].rearrange("(k p) c -> p k c", p=128) for l in range(4)]

        def load_T(st_sb, dst, src, n, tag):
            tmp = st_sb("ltT_" + tag, [n, 128], F32)
            P.dma(tmp[:], src)
            pt = psum()
            P.transpose(pt[:, 0:n], tmp[0:n, :], identf[0:n, 0:n])
            P.copy('dve', dst, pt[:, 0:n])

        def bgcast_gen(lw, stg_, o16_):
            cntb = 0

            def piece(src, dst):
                nonlocal cntb
                i = cntb % 2
                cntb += 1
                n = src.shape[-1]
                P.dma(stg_[i][:, 0:n], src)
                P.copy('pool', o16_[i][:, 0:n], stg_[i][:, 0:n])
                P.dma(dst, o16_[i][:, 0:n] if len(dst.shape) == 2 else o16_[i][:, 0:n].rearrange("p (m c) -> p m c", m=4), queue='pool')
            for k in range(8):
                rs = slice(k * 128, (k + 1) * 128)
                c0 = 0
                while c0 < OFF['gate']:
                    n = min(512, OFF['gate'] - c0)
                    piece(w_in[lw, rs, c0:c0 + n], wbf_in[lw, rs, c0:c0 + n])
                    c0 += n
                    yield
                for r in range(3):
                    for mh in range(2):
                        g0 = OFF['gate'] + r * 1024 + mh * 512
                        piece(w_in[lw, rs, g0:g0 + 512],
                              wbf_g[lw].rearrange("m p k r c -> p k r m c")[:, k, r][:, mh * 4:(mh + 1) * 4, :])
                        yield
            for k in range(12):
                rs = slice(k * 128, (k + 1) * 128)
                for hh in range(2):
                    piece(w_branch[lw, rs, hh * 512:(hh + 1) * 512], wbf_br[lw, rs, hh * 512:(hh + 1) * 512])
                    yield
            for k in range(8):
                rs = slice(k * 128, (k + 1) * 128)
                for hh in range(2):
                    piece(w_out[lw, rs, hh * 512:(hh + 1) * 512], wbf_out[lw, rs, hh * 512:(hh + 1) * 512])
                    yield

        def dump(name, src):
            import os as _os
            if _os.environ.get('NODUMP'):
                return
            if name in dbg_out:
                P.dma(dbg_out[name], src)

        for l in range(n_layers):
            for s in range(nseq):
                last = (l == 3)
                with ExitStack() as st:
                    def sb(name, shape, dt):
                        return st.enter_context(nc.sbuf_tensor(uniq(name), list(shape), dt))
                    set_pool(range(8))
                    xt = [sb("xt%d" % i, [128, D], F32) for i in range(3)]
                    for tb in range(NBLK):
                        r = 2 if tb < 2 else s
                        x_ = xt[tb % 3]
                        P.dma(x_[:], xres[s, tb * 128:(tb + 1) * 128, :])
                        for half in range(2):
                            pt = psum()
                            for kk in range(4):
                                k = half * 4 + kk
                                P.transpose(pt[:, kk * 128:(kk + 1) * 128], x_[:, k * 128:(k + 1) * 128], identf[:])
                            for kk in range(4):
                                k = half * 4 + kk
                                e = 'dve' if half == 0 else 'act'
                                dst = uT[:, k, tb * 128:(tb + 1) * 128]
                                src = pt[:, kk * 128:(kk + 1) * 128]
                                if e == 'dve':
                                    P.ts('dve', dst, src, modT[:, l, 8 + k, r:r + 1], modT[:, l, k, r:r + 1], ALU.mult, ALU.add)
                                else:
                                    P.act(dst, src, AF.Identity, bias=modT[:, l, k, r:r + 1], scale=modT[:, l, 8 + k, r:r + 1])
                    barrier()
                if l == 0 and s == 0:
                    dump('uT', uT[:])
                if stop == 'U':
                    break

                with ExitStack() as st:
                    def sb(name, shape, dt):
                        return st.enter_context(nc.sbuf_tensor(uniq(name), list(shape), dt))
                    set_pool(range(4))
                    cosT = sb("cosT", [128, NLAT], F32)
                    sinT = sb("sinT", [128, NLAT], F32)
                    rmat = sb("rmat", [128, 128], F32)
                    bmf = sb("bmf", [128, 384], F32)
                    bmask = sb("bmask", [128, 384], BF16)
                    P.dma(cosT[:], k_cosT)
                    P.dma(sinT[:], k_sinT)
                    P.dma(rmat[:], k_rmat)
                    P.dma(bmf[:], k_bandmask)
                    P.copy('pool', bmask[:], bmf[:])
                    Wq = sb("Wq", [128, 8, 512], BF16)
                    Wk = sb("Wk", [128, 8, 128], BF16)
                    Wv = sb("Wv", [128, 8, 128], BF16)
                    Wz = sb("Wz", [128, 8, 512], BF16)
                    for t in range(4):
                        P.dma(Wq[:, :, t * 128:t * 128 + 64], wq_v[l][:, :, t * 64:(t + 1) * 64])
                        P.dma(Wq[:, :, t * 128 + 64:(t + 1) * 128], wq_v[l][:, :, (4 + t) * 64:(5 + t) * 64])
                    P.dma(Wk[:], wq_v[l][:, :, OFF['a_k']:OFF['a_k'] + 128])
                    P.dma(Wv[:], wq_v[l][:, :, OFF['a_v']:OFF['a_v'] + 128])
                    P.dma(Wz[:], wq_v[l][:, :, OFF['a_z']:OFF['a_z'] + 512])
                    qT = sb("qT", [128, 4, T], BF16)
                    kT = sb("kT", [128, T], BF16)
                    kTz = [sb("kTz%d" % i, [128, T], BF16) for i in range(2)]
                    P.memset('pool', kTz[0][64:128, :], 0.0)
                    P.memset('pool', kTz[1][0:64, :], 0.0)
                    Vaug = sb("Vaug", [128, NBLK, 2, 66], BF16)
                    esink = sb("esink", [128, 8], F32)
                    P.dma(esink[:], a_sink[l:l + 1, :].partition_broadcast(128))
                    P.act(esink[:], esink[:], AF.Exp)
                    P.memset('pool', Vaug[:, :, :, 64:66], 1.0)
                    qraw = [sb("qraw%d" % i, [128, 512], F32) for i in range(2)]
                    t1s = [sb("t1s%d" % i, [128, 512], F32) for i in range(2)]
                    t2s = [sb("t2s%d" % i, [128, 512], F32) for i in range(2)]
                    it = 0
                    for (off, n) in TTILES:
                        for t in range(5):
                            ps = psum()
                            for k in range(8):
                                lhsT = Wq[:, k, t * 128:(t + 1) * 128] if t < 4 else Wk[:, k, :]
                                P.mm(ps[:, 0:n], lhsT, uT[:, k, off:off + n], start=(k == 0), stop=(k == 7))
                            dst = qT[:, t, off:off + n] if t < 4 else kT[:, off:off + n]
                            if off < NCTX:
                                P.copy('act', dst, ps[:, 0:n])
                                if t == 4:
                                    P.copy('act', kTz[0][0:64, off:off + n], ps[0:64, 0:n])
                                    P.copy('act', kTz[1][64:128, off:off + n], ps[64:128, 0:n])
                            else:
                                pos = off - NCTX
                                qr = qraw[it % 2]
                                a1 = t1s[it % 2]
                                a2 = t2s[it % 2]
                                it += 1
                                P.copy('act', qr[:, 0:n], ps[:, 0:n])
                                ps2 = psum()
                                P.mm(ps2[:, 0:n], rmat[:], qr[:, 0:n])
                                P.tt('pool', a1[:, 0:n], qr[:, 0:n], cosT[:, pos:pos + n], ALU.mult)
                                P.tt('dve', a2[:, 0:n], ps2[:, 0:n], sinT[:, pos:pos + n], ALU.mult)
                                P.tt('dve', dst, a1[:, 0:n], a2[:, 0:n], ALU.add)
                                if t == 4:
                                    P.tt('dve', kTz[0][0:64, off:off + n], a1[0:64, 0:n], a2[0:64, 0:n], ALU.add)
                                    P.tt('dve', kTz[1][64:128, off:off + n], a1[64:128, 0:n], a2[64:128, 0:n], ALU.add)
                    if stop == 'A1':
                        barrier()
                        P.emit()
                        return nc, P
                    for tb in range(NBLK):
                        ps = psum()
                        for k in range(8):
                            P.mm(ps[:, 0:128], uT[:, k, tb * 128:(tb + 1) * 128], Wv[:, k, :], start=(k == 0), stop=(k == 7))
                        P.copy(rot(('dve', 'act')), Vaug[:, tb, :, 0:64], ps[:, 0:128].rearrange("p (h d) -> p h d", h=2))
                    if l == 0 and s == 0:
                        dump('qT', qT[:])
                        dump('kT', kT[:])
                    if stop == 'A2':
                        barrier()
                        P.emit()
                        return nc, P
                    pTs = [sb("pT%d" % i, [128, 2, 5, 128], BF16) for i in range(2)]
                    az = [sb("az%d" % i, [128, 512], BF16) for i in range(2)]
                    den = [sb("den%d" % i, [128, 8], F32) for i in range(2)]
                    yaf = [sb("yaf%d" % i, [128, 512], F32) for i in range(2)]
                    yab = [sb("yab%d" % i, [128, 512], BF16) for i in range(2)]
                    yTt = [sb("yTt%d" % i, [128, 4, 128], BF16) for i in range(2)]
                    ya_dst = yT_d[0].rearrange("(c p) t -> p c t", p=128)
                    ip = 0
                    for n in range(NBLK):
                        q0 = n * 128
                        if n < 2:
                            slots = []
                        else:
                            slots = [sl for sl in range(3) if 2 <= n - 1 + sl <= 17]
                        ctxt = [0, 1]
                        ops = [psums[4 + 2 * (n % 2)], psums[5 + 2 * (n % 2)]]
                        for t in range(4):
                            pT = pTs[ip % 2]
                            ip += 1
                            if slots:
                                for hh in range(2):
                                    psb = psum()
                                    pr = slice(hh * 64, (hh + 1) * 64)
                                    for sl in slots:
                                        kb = n - 1 + sl
                                        P.mm(psb[:, sl * 128:(sl + 1) * 128], kTz[hh][:, kb * 128:(kb + 1) * 128], qT[:, t, q0:q0 + 128])
                                    s0, s1 = slots[0], slots[-1] + 1
                                    P.act(pT[:, hh, s0:s1, :], psb[:, s0 * 128:s1 * 128].rearrange("p (a b) -> p a b", b=128), AF.Exp, scale=0.125)
                                    P.tt('dve', pT[:, hh, s0:s1, :], pT[:, hh, s0:s1, :], bmask[:, s0 * 128:s1 * 128].rearrange("p (a b) -> p a b", b=128), ALU.mult)
                            psc = psum()
                            for hh in range(2):
                                pr = slice(hh * 64, (hh + 1) * 64)
                                for ci in ctxt:
                                    P.mm(psc[:, (hh * 2 + ci) * 128:(hh * 2 + ci + 1) * 128], kTz[hh][:, ci * 128:(ci + 1) * 128], qT[:, t, q0:q0 + 128])
                            P.act(pT[:, :, 3:5, :], psc[:, :].rearrange("p (h c b) -> p h c b", h=2, c=2), AF.Exp, scale=0.125)
                            import os as _os
                            CORE = int(_os.environ.get('CORE', '9'))
                            for hh in range(2 if CORE >= 2 else 0):
                                tiles = [(sl, n - 1 + sl) for sl in slots] + [(3 + ci, ci) for ci in ctxt]
                                for ii, (sl, kb) in enumerate(tiles):
                                    P.mm(ops[hh][:, t * 65:(t + 1) * 65], pT[:, hh, sl, :], Vaug[:, kb, hh, 0:65],
                                         start=(ii == 0), stop=(ii == len(tiles) - 1))
                        if CORE < 3:
                            continue
                        psz = psum()
                        for k in range(8):
                            P.mm(psz[:, :], uT[:, k, q0:q0 + 128], Wz[:, k, :], start=(k == 0), stop=(k == 7))
                        az_ = az[n % 2]
                        P.act(az_[:], psz[:, :], AF.Silu)
                        dn_ = den[n % 2]
                        yf = yaf[n % 2]
                        yb = yab[n % 2]
                        for hh in range(2):
                            ov = ops[hh][:, 0:260].rearrange("p (t e) -> p t e", e=65)
                            P.tt('dve', dn_[:, hh * 4:(hh + 1) * 4], ov[:, :, 64], esink[:, hh * 4:(hh + 1) * 4], ALU.add)
                        P.recip(dn_[:], dn_[:])
                        for hh in range(2):
                            ov = ops[hh][:, 0:260].rearrange("p (t e) -> p t e", e=65)
                            P.tt('dve', yf[:, hh * 256:(hh + 1) * 256].rearrange("p (t d) -> p t d", d=64), ov[:, :, 0:64],
                                 dn_[:, hh * 4:(hh + 1) * 4].unsqueeze(2).to_broadcast([128, 4, 64]), ALU.mult)
                        P.tt('pool', yb[:], yf[:], az_[:], ALU.mult)
                        if CORE < 4:
                            continue
                        ptb = psum()[:].bitcast(BF16)
                        for c in range(4):
                            P.transpose(ptb[:, c * 128:(c + 1) * 128], yb[:, c * 128:(c + 1) * 128], identb[:])
                        yt = yTt[n % 2]
                        P.copy('act', yt[:].rearrange("p c t -> p (c t)"), ptb[:, 0:512])
                        P.dma(ya_dst[:, :, q0:q0 + 128], yt[:], queue='pool')
                    barrier()
                if stop == 'A':
                    break

                with ExitStack() as st:
                    def sb(name, shape, dt):
                        return st.enter_context(nc.sbuf_tensor(uniq(name), list(shape), dt))
                    set_pool(range(8))
                    HW = 2364
                    hpad = sb("hpad", [128, 4, HW], BF16)
                    zT = sb("zT", [128, 4, T], BF16)
                    Wa = sb("Wa", [128, 8, 512], BF16)
                    Wb = sb("Wb", [128, 8, 512], BF16)
                    Wzb = sb("Wzb", [128, 8, 512], BF16)
                    P.dma(Wa[:], wq_v[l][:, :, OFF['b_a']:OFF['b_a'] + 512])
                    P.dma(Wb[:], wq_v[l][:, :, OFF['b_b']:OFF['b_b'] + 512])
                    P.dma(Wzb[:], wq_v[l][:, :, OFF['b_z']:OFF['b_z'] + 512])
                    cw = sb("cw", [128, 4, 31], F32)
                    cb = sb("cb", [128, 4], F32)
                    ng = sb("ng", [128, 4], F32)
                    nb_ = sb("nb", [128, 4], F32)
                    for c in range(4):
                        load_T(sb, cw[:, c, :], b_conv_w[l, :, c * 128:(c + 1) * 128], 31, "cw%d" % c)
                    load_T(sb, cb[:], b_conv_b[l].rearrange("(c p) -> c p", p=128), 4, "cb")
                    load_T(sb, ng[:], b_norm_g[l].rearrange("(c p) -> c p", p=128), 4, "ng")
                    load_T(sb, nb_[:], b_norm_b[l].rearrange("(c p) -> c p", p=128), 4, "nb")
                    P.memset('pool', hpad[:, :, 0:15], 0.0)
                    P.memset('pool', hpad[:, :, 271:301], 0.0)
                    P.memset('pool', hpad[:, :, 2349:2364], 0.0)

                    def hcol(tau):
                        return tau + 15 if tau < NCTX else tau + 45
                    sg = [sb("sg%d" % i, [128, 512], BF16) for i in range(2)]
                    i2 = 0
                    for (off, n) in TTILES:
                        for c in range(4):
                            psa = psum()
                            psb = psum()
                            psz = psum()
                            for k in range(8):
                                P.mm(psa[:, 0:n], Wa[:, k, c * 128:(c + 1) * 128], uT[:, k, off:off + n], start=(k == 0), stop=(k == 7))
                            for k in range(8):
                                P.mm(psb[:, 0:n], Wb[:, k, c * 128:(c + 1) * 128], uT[:, k, off:off + n], start=(k == 0), stop=(k == 7))
                            for k in range(8):
                                P.mm(psz[:, 0:n], Wzb[:, k, c * 128:(c + 1) * 128], uT[:, k, off:off + n], start=(k == 0), stop=(k == 7))
                            g_ = sg[i2 % 2]
                            i2 += 1
                            P.act(g_[:, 0:n], psb[:, 0:n], AF.Sigmoid)
                            P.tt('dve', hpad[:, c, hcol(off):hcol(off) + n], psa[:, 0:n], g_[:, 0:n], ALU.mult)
                            P.act(zT[:, c, off:off + n], psz[:, 0:n], AF.Silu)
                    accA = sb("accA", [128, 4, 512], F32)
                    sq = sb("sq", [128, 4, 512], F32)
                    mean = sb("mean", [128, 512], F32)
                    msq = sb("msq", [128, 512], F32)
                    rstd = sb("rstd", [128, 512], F32)
                    xc = [sb("xc%d" % i, [128, 512], F32) for i in range(2)]
                    sa = [sb("sa%d" % i, [128, 512], F32) for i in range(2)]
                    ybt = [sb("ybt%d" % i, [128, 4, 512], BF16) for i in range(2)]
                    yb_dst = yT_d[1].rearrange("(c p) t -> p c t", p=128)
                    dg = sb("dg", [128, 4, 31, 128], BF16)
                    for c in range(4):
                        for k in range(31):
                            if (c * 31 + k) % 2 == 0:
                                P.ts('dve', dg[:, c, k, :], identf[:], cw[:, c, k:k + 1], None, ALU.mult)
                            else:
                                P.act(dg[:, c, k, :], identf[:], AF.Identity, scale=cw[:, c, k:k + 1])
                    for ti, (off, n) in enumerate(TTILES):
                        h0 = hcol(off) - 15
                        for c in range(4):
                            pcv = psum()
                            for k in range(31):
                                P.mm(pcv[:, 0:n], dg[:, c, k, :], hpad[:, c, h0 + k:h0 + k + n], start=(k == 0), stop=(k == 30))
                            P.act(accA[:, c, 0:n], pcv[:, 0:n], AF.Identity, bias=cb[:, c:c + 1])
                            P.act(sq[:, c, 0:n], pcv[:, 0:n], AF.Square, bias=cb[:, c:c + 1])
                        p1 = psum()
                        p2 = psum()
                        for c in range(4):
                            P.mm(p1[:, 0:n], onesf[:], accA[:, c, 0:n], start=(c == 0), stop=(c == 3))
                        for c in range(4):
                            P.mm(p2[:, 0:n], onesf[:], sq[:, c, 0:n], start=(c == 0), stop=(c == 3))
                        P.ts('dve', mean[:, 0:n], p1[:, 0:n], 1.0 / 512, None, ALU.mult)
                        P.tt('dve', msq[:, 0:n], mean[:, 0:n], mean[:, 0:n], ALU.mult)
                        P.stt('dve', rstd[:, 0:n], p2[:, 0:n], 1.0 / 512, msq[:, 0:n], ALU.mult, ALU.subtract)
                        P.ts('dve', rstd[:, 0:n], rstd[:, 0:n], LN_EPS, None, ALU.add)
                        P.act(rstd[:, 0:n], rstd[:, 0:n], AF.Ln)
                        P.act(rstd[:, 0:n], rstd[:, 0:n], AF.Exp, scale=-0.5)
                        yt = ybt[ti % 2]
                        for c in range(4):
                            x_ = xc[c % 2]
                            a_ = sa[c % 2]
                            P.tt('pool', x_[:, 0:n], accA[:, c, 0:n], mean[:, 0:n], ALU.subtract)
                            P.tt('dve', x_[:, 0:n], x_[:, 0:n], rstd[:, 0:n], ALU.mult)
                            P.act(a_[:, 0:n], x_[:, 0:n], AF.Silu, bias=nb_[:, c:c + 1], scale=ng[:, c:c + 1])
                            P.tt('pool', yt[:, c, 0:n], a_[:, 0:n], zT[:, c, off:off + n], ALU.mult)
                        P.dma(yb_dst[:, :, off:off + n], yt[:, :, 0:n], queue='pool')
                    barrier()
                if stop == 'B':
                    break

                with ExitStack() as st:
                    def sb(name, shape, dt):
                        return st.enter_context(nc.sbuf_tensor(uniq(name), list(shape), dt))
                    set_pool(range(6))
                    dnm = sb("dnm", [128, 7, 128], F32)
                    P.dma(dnm[:].rearrange("p a b -> p (a b)"), k_dnmask)
                    lvl = sb("lvl", [128, 7, 128], F32)
                    P.dma(lvl[:].rearrange("p a b -> p (a b)"), k_lvl)
                    mcum = [dnm[:, 0, :], dnm[:, 1, :]]
                    mtot = dnm[:, 2, :]
                    mstr = [dnm[:, 3, :], dnm[:, 5, :]]
                    minc = [dnm[:, 4, :], dnm[:, 6, :]]
                    czT = sb("czT", [128, 4, T], BF16)
                    abt = sb("abt", [128, NBLK, 16], F32)
                    with ExitStack() as st2:
                        Wcz = st2.enter_context(nc.sbuf_tensor(uniq("Wcz"), [128, 8, 512], BF16))
                        Wab = st2.enter_context(nc.sbuf_tensor(uniq("Wab"), [128, 8, 16], BF16))
                        P.dma(Wcz[:], wq_v[l][:, :, OFF['c_z']:OFF['c_z'] + 512])
                        P.dma(Wab[:], wq_v[l][:, :, OFF['c_a']:OFF['c_a'] + 16])
                        for (off, n) in TTILES:
                            for c in range(4):
                                ps = psum()
                                for k in range(8):
                                    P.mm(ps[:, 0:n], Wcz[:, k, c * 128:(c + 1) * 128], uT[:, k, off:off + n], start=(k == 0), stop=(k == 7))
                                P.act(czT[:, c, off:off + n], ps[:, 0:n], AF.Silu)
                        for tb in range(NBLK):
                            ps = psum()
                            for k in range(8):
                                P.mm(ps[:, 0:16], uT[:, k, tb * 128:(tb + 1) * 128], Wab[:, k, :], start=(k == 0), stop=(k == 7))
                            P.copy('dve', abt[:, tb, :], ps[:, 0:16])
                        barrier()
                    dtb = sb("dtb", [128, 8], F32)
                    negA = sb("negA", [128, 8], F32)
                    P.dma(dtb[:], c_dt_bias[l:l + 1, :].partition_broadcast(128))
                    P.dma(negA[:], c_a_log[l:l + 1, :].partition_broadcast(128))
                    P.act(negA[:], negA[:], AF.Exp)
                    P.ts('dve', negA[:], negA[:], -1.0, None, ALU.mult)
                    g_tok = sb("g_tok", [128, NBLK, 8], F32)
                    beta = sb("beta", [128, NBLK, 8], F32)
                    nbeta = sb("nbeta", [128, NBLK, 8], F32)
                    gc_tok = sb("gc_tok", [128, NBLK, 8], F32)
                    gl_tok = sb("gl_tok", [128, NBLK, 8], F32)
                    edk = sb("edk", [128, NBLK, 8], F32)
                    P.tt('dve', g_tok[:], abt[:, :, 0:8], dtb[:].unsqueeze(1).to_broadcast([128, NBLK, 8]), ALU.add)
                    P.act(g_tok[:], g_tok[:], AF.Exp)
                    P.ts('dve', g_tok[:], g_tok[:], 1.0, None, ALU.add)
                    P.act(g_tok[:], g_tok[:], AF.Ln)
                    P.tt('dve', g_tok[:], g_tok[:], negA[:].unsqueeze(1).to_broadcast([128, NBLK, 8]), ALU.mult)
                    P.act(beta[:], abt[:, :, 8:16], AF.Sigmoid)
                    P.ts('dve', nbeta[:], beta[:], -1.0, None, ALU.mult)
                    for d in range(2):
                        ps = psum()
                        P.mm(ps[:, 0:72].rearrange("p (b h) -> p b h", h=4), mcum[d], g_tok[:, :, d * 4:(d + 1) * 4])
                        P.copy('dve', gc_tok[:, :, d * 4:(d + 1) * 4], ps[:, 0:72].rearrange("p (b h) -> p b h", h=4))
                        ps = psum()
                        P.mm(ps[:, 0:72].rearrange("p (b h) -> p b h", h=4), mtot, g_tok[:, :, d * 4:(d + 1) * 4])
                        P.copy('dve', gl_tok[:, :, d * 4:(d + 1) * 4], ps[:, 0:72].rearrange("p (b h) -> p b h", h=4))
                    P.tt('dve', edk[:], gl_tok[:], gc_tok[:], ALU.subtract)
                    P.act(edk[:], edk[:], AF.Exp)
                    if l == 0 and s == 0:
                        dump('g_tok', g_tok[:])
                        dump('gc_tok', gc_tok[:])

                    ccw = sb("ccw", [128, 12, 3], F32)
                    for c in range(12):
                        load_T(sb, ccw[:, c, :], c_conv_w[l, :, c * 128:(c + 1) * 128], 3, "ccw%d" % c)
                    cng = sb("cng", [128, 1], F32)
                    P.dma(cng[:], c_norm_g[l].rearrange("(p o) -> p o", o=1))
                    RW = T + 4
                    raw = sb("raw", [128, RW], F32)
                    acc = sb("acc", [128, T], F32)
                    kTf = sb("kTf", [128, T], F32)
                    k_tok = sb("k_tok", [128, NBLK, 128], F32)
                    v_tok = sb("v_tok", [128, NBLK, 128], F32)
                    oT = sb("oT", [128, T], F32)
                    qTb = sb("qTb", [128, T], BF16)
                    kTb = sb("kTb", [128, T], BF16)
                    Wh = [sb("Wh%d" % i, [128, 8, 384], BF16) for i in range(2)]
                    P.memset('pool', raw[:, 0:1], 0.0)
                    P.memset('pool', raw[:, 257:259], 0.0)
                    P.memset('pool', raw[:, RW - 1:RW], 0.0)

                    def rcol(tau):
                        return tau + 1 if tau < NCTX else tau + 3
                    ring = {}
                    for d in range(2):
                        for nm in ('TT', 'aqk', 'kg', 'qd', 'kdec', 'egc'):
                            ring[(d, nm)] = [sb("rg_%s%d_%d" % (nm, d, i), [128, 128], BF16 if nm in ('aqk', 'qd') else F32) for i in range(NSLOT)]
                    tmpn = {}

                    def tmp(nm, i=0, dt=F32, shape=(128, 128)):
                        key = (nm, i)
                        if key not in tmpn:
                            tmpn[key] = sb("tp_%s" % nm, list(shape), dt)
                        return tmpn[key]
                    S = [sb("S%d" % d, [128, 128], F32) for d in range(2)]
                    Xz = [[sb("Xz%d_%d" % (d, i), [128, 128], F32) for i in range(1)] for d in range(2)]
                    vnz = [[sb("vnz%d_%d" % (d, i), [128, 128], F32) for i in range(1)] for d in range(2)]
                    S16 = [sb("S16_%d" % d, [128, 128], BF16) for d in range(2)]
                    vn16 = [[sb("vn16_%d_%d" % (d, i), [128, 128], BF16) for i in range(1)] for d in range(2)]
                    for d in range(2):
                        for i in range(1):
                            P.memset('pool', Xz[d][i][:], 0.0)
                            P.memset('pool', vnz[d][i][:], 0.0)
                            P.memset('pool', vn16[d][i][:], 0.0)
                    ycb = sb("ycb", [128, T], BF16)

                    bg = None
                    if s == 0 and l + 1 < n_layers:
                        bstg = [sb("bstg%d" % i, [128, 512], F32) for i in range(2)]
                        bo16 = [sb("bo16_%d" % i, [128, 512], BF16) for i in range(2)]
                        bg = bgcast_gen(l + 1, bstg, bo16)
                    rnd = 0
                    for h in range(4):
                        W_ = Wh[h % 2]
                        for j in range(3):
                            c0 = OFF['c_qkv'] + (j * 4 + h) * 128
                            P.dma(W_[:, :, j * 128:(j + 1) * 128], wq_v[l][:, :, c0:c0 + 128])
                        for j in range(3):
                            ct = j * 4 + h
                            for (off, n) in TTILES:
                                ps = psum()
                                for k in range(8):
                                    P.mm(ps[:, 0:n], W_[:, k, j * 128:(j + 1) * 128], uT[:, k, off:off + n], start=(k == 0), stop=(k == 7))
                                P.copy(rot(('act', 'dve')), raw[:, rcol(off):rcol(off) + n], ps[:, 0:n])
                            for (off, n) in ((0, NCTX), (NCTX, 1024), (NCTX + 1024, 1024)):
                                r0 = rcol(off)
                                e = 'dve'
                                P.ts(e, acc[:, off:off + n], raw[:, r0 - 1:r0 - 1 + n], ccw[:, ct, 0:1], None, ALU.mult)
                                P.stt(e, acc[:, off:off + n], raw[:, r0:r0 + n], ccw[:, ct, 1:2], acc[:, off:off + n], ALU.mult, ALU.add)
                                P.stt(e, acc[:, off:off + n], raw[:, r0 + 1:r0 + 1 + n], ccw[:, ct, 2:3], acc[:, off:off + n], ALU.mult, ALU.add)
                            P.act(acc[:], acc[:], AF.Silu)
                            if j < 2:
                                dstT = qTb if j == 0 else kTf
                                P.act(raw[:, 0:T], acc[:], AF.Square)
                                for (off, n) in TTILES:
                                    ps = psum()
                                    P.mm(ps[:, 0:n], onesf[:], raw[:, off:off + n])
                                    P.ts('dve', raw[:, off:off + n], ps[:, 0:n], RMS_EPS, None, ALU.add)
                                    P.act(raw[:, off:off + n], raw[:, off:off + n], AF.Ln)
                                    P.act(raw[:, off:off + n], raw[:, off:off + n], AF.Exp, scale=-0.5)
                                if j == 0:
                                    P.stt('dve', dstT[:], acc[:], 128 ** -0.5, raw[:, 0:T], ALU.mult, ALU.mult)
                                else:
                                    P.tt('dve', dstT[:], acc[:], raw[:, 0:T], ALU.mult)
                                if j == 1:
                                    P.copy('pool', kTb[:], dstT[:])
                                P.memset('pool', raw[:, 0:1], 0.0)
                                P.memset('pool', raw[:, 257:259], 0.0)
                            if j >= 1:
                                src = kTf if j == 1 else acc
                                dtok = k_tok if j == 1 else v_tok
                                for tb4 in range(0, NBLK, 4):
                                    nb4 = min(4, NBLK - tb4)
                                    pt = psum()
                                    for i in range(nb4):
                                        tb = tb4 + i
                                        P.transpose(pt[:, i * 128:(i + 1) * 128], src[:, tb * 128:(tb + 1) * 128], identf[:])
                                    P.copy(rot(('act', 'dve')), dtok[:, tb4:tb4 + nb4, :].rearrange("p b d -> p (b d)"), pt[:, 0:nb4 * 128])
                        if l == 0 and s == 0 and h == 0:
                            dump('dn_kT', kTf[:])
                            dump('dn_vtok', v_tok[:])
                        P.memset('pool', oT[:], 0.0)
                        orders = [list(range(NBLK)), [1, 0] + list(range(17, 1, -1))]

                        def prep_gen(d, blk, slot, tk):
                            dh = d * 4 + h
                            cs = slice(blk * 128, (blk + 1) * 128)
                            gbc = tmp('gbc', tk)
                            P.copy('act', gbc[:], g_tok[:, blk, dh:dh + 1].to_broadcast([128, 128]))
                            psg = psum()
                            P.mm(psg[:, 0:128], gbc[:], mcum[d])
                            egc = ring[(d, 'egc')][slot]
                            gcb = tmp('gcb', tk)
                            P.copy('dve', gcb[:], psg[:, 0:128])
                            P.act(egc[:], gcb[:], AF.Exp)
                            diff = tmp('diff', tk)
                            P.ts('dve', diff[:], gcb[:], gc_tok[:, blk, dh:dh + 1], 0.0, ALU.subtract, ALU.min)
                            P.act(diff[:], diff[:], AF.Exp)
                            yield
                            pkk = psum()
                            P.mm(pkk[:, 0:128], kTb[:, cs], kTb[:, cs])
                            pqk = psum()
                            P.mm(pqk[:, 0:128], kTb[:, cs], qTb[:, cs])
                            m12 = tmp('m12', tk, BF16, (128, 2, 128))
                            P.tt('pool', m12[:], diff[:].unsqueeze(1).to_broadcast([128, 2, 128]), dnm[:, 3 + 2 * d:5 + 2 * d, :], ALU.mult)
                            C = tmp('C0', tk, BF16)
                            P.stt('dve', C[:], pkk[:, 0:128], nbeta[:, blk, dh:dh + 1], m12[:, 0, :], ALU.mult, ALU.mult)
                            P.tt('dve', ring[(d, 'aqk')][slot][:], pqk[:, 0:128], m12[:, 1, :], ALU.mult)
                            yield
                            P.tt('pool', ring[(d, 'kg')][slot][:], kTf[:, cs], egc[:], ALU.mult)
                            P.tt('pool', ring[(d, 'qd')][slot][:], qTb[:, cs], egc[:], ALU.mult)
                            P.act(ring[(d, 'kdec')][slot][:], k_tok[:, blk, :], AF.Identity, scale=edk[:, blk, dh:dh + 1])
                            yield
                            pb = psum()[:].bitcast(BF16)
                            P.transpose(pb[:, 0:128], C[:], identb[:])
                            B0 = tmp('B0', tk, BF16)
                            P.copy('act', B0[:], pb[:, 0:128])
                            Tm = tmp('Tm', tk, BF16)
                            Um = tmp('Um', tk, BF16)
                            Gall = tmp('Gall', tk, BF16, (128, 7, 128))
                            H0 = tmp('H0', tk, BF16)
                            P.tt('pool', Gall[:], C[:].unsqueeze(1).to_broadcast([128, 7, 128]), lvl[:], ALU.mult)
                            P.tt('pool', Um[:], Gall[:, 0, :], identb[:], ALU.add)
                            P.tt('pool', H0[:], B0[:], lvl[:, 0, :], ALU.mult)
                            P.tt('pool', Tm[:], H0[:], identb[:], ALU.add)
                            yield
                            for li in range(1, 7):
                                lastl = (li == 6)
                                pX = psum()
                                P.mm(pX[:, 0:128], Gall[:, li, :], Tm[:])
                                X16 = tmp('X16', tk, BF16)
                                P.copy('act', X16[:], pX[:, 0:128])
                                yield
                                pYT = psum()
                                P.mm(pYT[:, 0:128], X16[:], Um[:])
                                if not lastl:
                                    pY = psum()
                                    P.mm(pY[:, 0:128], Um[:], X16[:])
                                    P.tt('dve', Tm[:], Tm[:], pY[:, 0:128], ALU.add)
                                    P.tt('dve', Um[:], Um[:], pYT[:, 0:128], ALU.add)
                                else:
                                    P.tt('dve', ring[(d, 'TT')][slot][:], Um[:], pYT[:, 0:128], ALU.add)
                                yield

                        def scan_gen(d, blk, slot):
                            dh = d * 4 + h
                            halves = (0, 1) if d == 0 else (1, 0)
                            kg = ring[(d, 'kg')][slot]
                            qd = ring[(d, 'qd')][slot]
                            TTs = ring[(d, 'TT')][slot]
                            aqk = ring[(d, 'aqk')][slot]
                            kdec = ring[(d, 'kdec')][slot]
                            egc = ring[(d, 'egc')][slot]
                            bank = psums[6 + d]
                            ps1 = bank[:, 0:128]
                            ps2 = bank[:, 128:256]
                            ps3 = bank[:, 256:384]
                            ps4 = bank[:, 384:512]
                            P.mm(ps1, kg[:], S[d][:])
                            yield
                            X = Xz[d][0]
                            P.tt('dve', X[:], v_tok[:, blk, :], ps1, ALU.subtract)
                            yield
                            P.mm(ps2, TTs[:, :], X[:, :])
                            yield
                            vn = vnz[d][0]
                            P.ts('dve', vn[:], ps2, beta[:, blk, dh:dh + 1], None, ALU.mult)
                            v16 = vn16[d][0]
                            P.copy('act', v16[:], vn[:])
                            yield
                            P.mm(ps4, kdec[:, :], vn[:, :])
                            P.mm(ps3, S16[d][:], qd[:, :], start=True, stop=False)
                            P.mm(ps3, v16[:, :], aqk[:, :], start=False, stop=True)
                            yield
                            ccol = 127 if d == 0 else 0
                            P.stt('dve', S[d][:], S[d][:], egc[:, ccol:ccol + 1], ps4, ALU.mult, ALU.add)
                            P.copy('act', S16[d][:], S[d][:])
                            oc = slice(blk * 128, (blk + 1) * 128)
                            P.tt('dve', oT[:, oc], oT[:, oc], ps3, ALU.add)
                            yield

                        for d in range(2):
                            P.memset('pool', S[d][:], 0.0)
                            P.memset('pool', S16[d][:], 0.0)
                        LEAD = NSLOT - 1
                        NPREP = 2
                        preps = {d: [] for d in range(2)}
                        pdone = {d: set() for d in range(2)}
                        scans = {d: None for d in range(2)}
                        pi = {d: 0 for d in range(2)}
                        si = {d: 0 for d in range(2)}
                        active = True
                        while active:
                            active = False
                            rnd += 1
                            if bg is not None and rnd % 3 == 0:
                                try:
                                    next(bg)
                                except StopIteration:
                                    bg = None
                            for d in range(2):
                                if len(preps[d]) < NPREP and pi[d] < NBLK and pi[d] - si[d] < NSLOT:
                                    preps[d].append((pi[d], prep_gen(d, orders[d][pi[d]], pi[d] % NSLOT, d * NPREP + pi[d] % NPREP)))
                                    pi[d] += 1
                                for item in list(preps[d]):
                                    active = True
                                    try:
                                        next(item[1])
                                    except StopIteration:
                                        preps[d].remove(item)
                                        pdone[d].add(item[0])
                                if scans[d] is None and si[d] < NBLK and si[d] in pdone[d]:
                                    scans[d] = scan_gen(d, orders[d][si[d]], si[d] % NSLOT)
                                if scans[d] is not None:
                                    active = True
                                    try:
                                        next(scans[d])
                                    except StopIteration:
                                        scans[d] = None
                                        si[d] += 1
                                if si[d] < NBLK or pi[d] < NBLK:
                                    active = True
                        if l == 0 and s == 0 and h == 0:
                            dump('dn_oT', oT[:])
                        P.act(acc[:], oT[:], AF.Square)
                        for (off, n) in TTILES:
                            ps = psum()
                            P.mm(ps[:, 0:n], onesf[:], acc[:, off:off + n])
                            P.ts('dve', acc[:, off:off + n], ps[:, 0:n], 1.0 / 128, RMS_EPS, ALU.mult, ALU.add)
                        P.act(acc[:], acc[:], AF.Ln)
                        P.act(acc[:], acc[:], AF.Exp, scale=-0.5)
                        P.stt('dve', acc[:], oT[:], cng[:, 0:1], acc[:], ALU.mult, ALU.mult)
                        P.tt('pool', ycb[:], acc[:], czT[:, h, :], ALU.mult)
                        P.dma(yT_d[2, h * 128:(h + 1) * 128, :], ycb[:], queue='pool')
                    if bg is not None:
                        for _ in bg:
                            pass
                    barrier()
                if stop == 'C':
                    break

                with ExitStack() as st:
                    def sb(name, shape, dt):
                        return st.enter_context(nc.sbuf_tensor(uniq(name), list(shape), dt))
                    set_pool(range(8))
                    Wbr = sb("Wbr", [128, 12, D], BF16)
                    Wo = sb("Wo", [128, 8, D], BF16)
                    P.dma(Wbr[:], wbr_v[l])
                    P.dma(Wo[:], wout_v[l])
                    lng = sb("lng", [128, D], F32)
                    lnb = sb("lnb", [128, D], F32)
                    P.dma(lng[:], ln_g[l:l + 1, :].partition_broadcast(128))
                    P.dma(lnb[:], ln_b[l:l + 1, :].partition_broadcast(128))
                    gbc_ = [sb("gatebc%d" % i, [128, D], F32) for i in range(2)]
                    P.dma(gbc_[0][:], modrow_d[l, 2:3, 2 * D:3 * D].partition_broadcast(128))
                    P.dma(gbc_[1][:], modrow_d[l, s:s + 1, 2 * D:3 * D].partition_broadcast(128))
                    Wg = [sb("Wg%d" % i, [128, 8, 3, 128], BF16) for i in range(3)]
                    yt3 = [sb("yt3_%d" % i, [128, 3, 4, 512], BF16) for i in range(2)]
                    sgm = [sb("sgm%d" % i, [128, 512], BF16) for i in range(3)]
                    mT = sb("mT", [128, 8, 512], BF16)
                    macc = sb("macc", [128, 512], F32)
                    mtmp = [sb("mtmp%d" % i, [128, 512], F32) for i in range(2)]
                    xtl = [sb("xtl%d" % i, [128, D], F32) for i in range(2)]
                    tt_ = [sb("ttl%d" % i, [128, D], F32) for i in range(2)]
                    stat = [sb("stat%d" % i, [128, 8], F32) for i in range(2)]
                    junk = sb("junk", [128, D], F32)
                    yv = yT_d.rearrange("r (c p) t -> p r c t", p=128)
                    iw = 0
                    for ti, (off, n) in enumerate(TTILES):
                        y3 = yt3[ti % 2]
                        for r in range(3):
                            P.dma(y3[:, r, :, 0:n], yv[:, r, :, off:off + n])
                        for m in range(8):
                            wg = Wg[iw % 3]
                            iw += 1
                            P.dma(wg[:], wbf_g[l, m])
                            for r in range(3):
                                pg = psum()
                                for k in range(8):
                                    P.mm(pg[:, 0:n], wg[:, k, r, :], uT[:, k, off:off + n], start=(k == 0), stop=(k == 7))
                                pb_ = psum()
                                for k in range(4):
                                    P.mm(pb_[:, 0:n], Wbr[:, r * 4 + k, m * 128:(m + 1) * 128], y3[:, r, k, 0:n], start=(k == 0), stop=(k == 3))
                                sg_ = sgm[r]
                                P.act(sg_[:, 0:n], pg[:, 0:n], AF.Sigmoid)
                                if r == 0:
                                    P.tt('dve', macc[:, 0:n], pb_[:, 0:n], sg_[:, 0:n], ALU.mult)
                                else:
                                    mt_ = mtmp[r % 2]
                                    P.tt('dve', mt_[:, 0:n], pb_[:, 0:n], sg_[:, 0:n], ALU.mult)
                                    if r == 1:
                                        P.tt('pool', macc[:, 0:n], macc[:, 0:n], mt_[:, 0:n], ALU.add)
                                    else:
                                        P.tt('pool', mT[:, m, 0:n], macc[:, 0:n], mt_[:, 0:n], ALU.add)
                        for sbk in range(n // 128):
                            tau0 = off + sbk * 128
                            gb = gbc_[0] if tau0 < NCTX else gbc_[1]
                            x_ = xtl[sbk % 2]
                            t_ = tt_[sbk % 2]
                            st_ = stat[sbk % 2]
                            P.dma(x_[:], xres[s, tau0:tau0 + 128, :])
                            for hc in range(2):
                                po = psum()
                                for k in range(8):
                                    P.mm(po[:, :], mT[:, k, sbk * 128:(sbk + 1) * 128], Wo[:, k, hc * 512:(hc + 1) * 512], start=(k == 0), stop=(k == 7))
                                hs = slice(hc * 512, (hc + 1) * 512)
                                P.tt('dve', t_[:, hs], po[:, :], gb[:, hs], ALU.mult)
                                P.stt('dve', t_[:, hs], x_[:, hs], ALU_ALPHA, t_[:, hs], ALU.mult, ALU.add)
                            P.reduce('dve', st_[:, 0:1], t_[:], ALU.add)
                            P.act(junk[:], t_[:], AF.Square)
                            P.reduce('dve', st_[:, 1:2], junk[:], ALU.add)
                            P.ts('dve', st_[:, 2:3], st_[:, 0:1], 1.0 / D, None, ALU.mult)
                            P.tt('dve', st_[:, 3:4], st_[:, 2:3], st_[:, 2:3], ALU.mult)
                            P.stt('dve', st_[:, 4:5], st_[:, 1:2], 1.0 / D, st_[:, 3:4], ALU.mult, ALU.subtract)
                            P.ts('dve', st_[:, 5:6], st_[:, 4:5], LN_EPS, None, ALU.add)
                            P.act(st_[:, 5:6], st_[:, 5:6], AF.Ln)
                            P.act(st_[:, 5:6], st_[:, 5:6], AF.Exp, scale=-0.5)
                            P.ts('dve', t_[:], t_[:], st_[:, 2:3], st_[:, 5:6], ALU.subtract, ALU.mult)
                            P.tt('pool', t_[:], t_[:], lng[:], ALU.mult)
                            P.tt('dve', t_[:], t_[:], lnb[:], ALU.add)
                            if last:
                                if tau0 >= NCTX:
                                    P.dma(y_out[s, tau0 - NCTX:tau0 - NCTX + 128, :], t_[:], queue='pool')
                            else:
                                P.dma(xres[s, tau0:tau0 + 128, :], t_[:], queue='pool')
                    barrier()
            if stop is not None:
                break
        import os as _os
        if dbg and 'xres' in dbg_out and not _os.environ.get('NODUMP'):
            P.dma(dbg_out['xres'], xres[0])
        if dbg and 'yT' in dbg_out:
            nbr = {'A': 1, 'B': 2}.get(stop, 3)
            if stop != 'U':
                P.dma(dbg_out['yT'][0:nbr], yT_d[0:nbr])
        barrier()
        P.emit()
    return nc, P


ALU_ALPHA = ALPHA

_CACHE = {}


def kernel(**inputs):
    n = 8
    if 'nc' not in _CACHE:
        _CACHE['nc'] = build()[0]
    nc = _CACHE['nc']
    consts = host_consts()
    f = lambda a: np.ascontiguousarray(np.asarray(a, dtype=np.float32))
    shared = {
        'c_ctx': f(inputs['c_ctx']).reshape(1, D),
        'w_ada': f(inputs['w_ada']), 'b_ada': f(inputs['b_ada']), 'w_in': f(inputs['w_in']),
        'a_sink': f(inputs['a_sink']), 'b_conv_w': f(inputs['b_conv_w']), 'b_conv_b': f(inputs['b_conv_b']),
        'b_norm_g': f(inputs['b_norm_g']), 'b_norm_b': f(inputs['b_norm_b']), 'c_conv_w': f(inputs['c_conv_w']),
        'c_a_log': f(inputs['c_a_log']).reshape(4, 8), 'c_dt_bias': f(inputs['c_dt_bias']).reshape(4, 8),
        'c_norm_g': f(inputs['c_norm_g']), 'w_branch': f(inputs['w_branch']).reshape(4, 1536, D),
        'w_out': f(inputs['w_out']), 'ln_g': f(inputs['ln_g']), 'ln_b': f(inputs['ln_b']),
    }
    shared.update(consts)
    x = f(inputs['x'])
    ctx = f(inputs['ctx'])
    c = f(inputs['c'])
    in_maps = []
    for i in range(n):
        m = dict(shared)
        m['x'] = x[2 * i:2 * i + 2]
        m['ctx'] = ctx[2 * i:2 * i + 2]
        m['c'] = c[2 * i:2 * i + 2]
        in_maps.append(m)
    res = run_bass_kernel_spmd(nc, in_maps, core_ids=list(range(n)))
    return np.concatenate([r['y'] for r in res.results], axis=0).astype(np.float32)
```

```python
import numpy as np
import concourse.bass as bass
import concourse.mybir as mybir

F32 = mybir.dt.float32
BF16 = mybir.dt.bfloat16
AF = mybir.ActivationFunctionType
ALU = mybir.AluOpType
AX = mybir.AxisListType

_DT_SIZE = {F32: 4, BF16: 2}


def _dsize(dt):
    if dt in _DT_SIZE:
        return _DT_SIZE[dt]
    s = str(dt)
    if '32' in s:
        return 4
    if '16' in s:
        return 2
    if '8' in s:
        return 1
    if '64' in s:
        return 8
    raise ValueError(s)


def _prod(xs):
    r = 1
    for x in xs:
        r *= int(x)
    return r


def box(ap):
    t = ap.tensor
    name = t.name
    esz = _dsize(ap.dtype)
    off = int(ap.offset) * esz
    aps = ap.ap
    sp = str(ap.space)
    if sp == 'PSUM':
        return (name, 0, 128, 0, 2048)
    if sp in ('SB', 'PSUM'):
        pbytes = _prod(t.shape[1:]) * _dsize(t.dtype)
        plo = off // pbytes
        flo = off % pbytes
        pcnt = aps[0][1] if aps[0][0] != 0 else 1
        ext = sum((c - 1) * abs(s) for s, c in aps[1:]) + 1
        return (name, plo, plo + pcnt, flo, flo + ext * esz)
    ext = sum((c - 1) * abs(s) for s, c in aps) + 1
    return (name, 0, 1, off, off + ext * esz)


def _ovl(a, b):
    return a[1] < b[2] and b[1] < a[2] and a[3] < b[4] and b[3] < a[4]


def _covers(a, b):
    return a[1] <= b[1] and a[2] >= b[2] and a[3] <= b[3] and a[4] >= b[4]


ENGINES = ('pe', 'act', 'dve', 'pool', 'sp')
SEM_EPOCH = 30000
NDMA_SEMS = 12
SAME_ENGINE_DIST = 3


class Prog:
    def __init__(self, nc):
        self.nc = nc
        self.streams = {e: [] for e in ENGINES}
        self.count = {e: 0 for e in ENGINES}
        self.known = {e: {} for e in ENGINES}
        self.wr = {}
        self.rd = {}
        self.dma_val = {}
        self.dma_next = {e: 0 for e in ENGINES}
        self.n_waits = 0
        self.sem_handles = {}
        self.needed_sems = set()
        self.ninst = 0

    def _deps_for(self, reads, writes):
        deps = {}

        def add(key, val):
            if deps.get(key, -1) < val:
                deps[key] = val
        rboxes = [box(a) for a in reads]
        wboxes = [box(a) for a in writes]
        for b in rboxes:
            w = self.wr.get(b[0])
            if w:
                for ob, (k, v) in w.items():
                    if _ovl(b, ob):
                        add(k, v)
        for b in wboxes:
            w = self.wr.get(b[0])
            if w:
                for ob, (k, v) in w.items():
                    if _ovl(b, ob):
                        add(k, v)
            r = self.rd.get(b[0])
            if r:
                for (ob, k), v in r.items():
                    if _ovl(b, ob):
                        add(k, v)
        return deps, rboxes, wboxes

    def _commit(self, rboxes, wboxes, key, val):
        for b in wboxes:
            w = self.wr.setdefault(b[0], {})
            for ob in [ob for ob in w if _covers(b, ob)]:
                del w[ob]
            w[b] = (key, val)
            r = self.rd.get(b[0])
            if r:
                for rk in [rk for rk in r if _covers(b, rk[0])]:
                    del r[rk]
        for b in rboxes:
            self.rd.setdefault(b[0], {})[(b, key)] = val

    def _emit_waits(self, eng, deps, self_seq=None):
        kn = self.known[eng]
        for key, val in deps.items():
            if key == ('e', eng):
                if eng == 'pe':
                    continue
            if kn.get(key, 0) >= val:
                continue
            kn[key] = val
            self.n_waits += 1
            self.needed_sems.add(self._semname(key, val))
            self.streams[eng].append(('wait', key, val))

    def _semname(self, key, val):
        if key[0] == 'e':
            return ('e', key[1], (val - 1) // SEM_EPOCH)
        return key

    def op(self, eng, fn, reads=(), writes=()):
        deps, rb, wb = self._deps_for(reads, writes)
        seq = self.count[eng] + 1
        self._emit_waits(eng, deps, seq)
        self.count[eng] = seq
        self.ninst += 1
        self.needed_sems.add(('e', eng, (seq - 1) // SEM_EPOCH))
        self.streams[eng].append(('op', fn, seq))
        self._commit(rb, wb, ('e', eng), seq)

    def dma(self, out, in_, queue='sp', **kw):
        deps, rb, wb = self._deps_for([in_], [out])
        i = self.dma_next[queue]
        self.dma_next[queue] = (i + 1) % NDMA_SEMS
        key = ('d', queue, i)
        prev = self.dma_val.get(key, 0)
        if prev:
            deps[key] = max(deps.get(key, 0), prev)
        self._emit_waits(queue, deps)
        val = prev + 16
        self.dma_val[key] = val
        self.ninst += 1
        self.needed_sems.add(key)
        self.streams[queue].append(('dma', out, in_, key, kw))
        self._commit(rb, wb, key, val)
        return key, val

    def wait_all(self, eng):
        deps = {}
        for key, val in self.dma_val.items():
            deps[key] = val
        for e in ENGINES:
            if e != eng and self.count[e]:
                deps[('e', e)] = self.count[e]
        self._emit_waits(eng, deps)

    def emit(self):
        nc = self.nc
        from contextlib import ExitStack
        with ExitStack() as st:
            sems = {}
            for sn in sorted(self.needed_sems, key=str):
                sems[sn] = st.enter_context(nc.semaphore("s_" + "_".join(str(x) for x in sn)))
            block = st.enter_context(nc.Block())
            streams = self.streams

            def run(eng_name):
                def body(e):
                    for item in streams[eng_name]:
                        if item[0] == 'wait':
                            _, key, val = item
                            if key[0] == 'e':
                                ep = (val - 1) // SEM_EPOCH
                                e.wait_ge(sems[('e', key[1], ep)], val - ep * SEM_EPOCH)
                            else:
                                e.wait_ge(sems[key], val)
                        elif item[0] == 'op':
                            _, fn, seq = item
                            ins = fn(e)
                            ep = (seq - 1) // SEM_EPOCH
                            ins.then_inc(sems[('e', eng_name, ep)], 1)
                        else:
                            _, out, in_, key, kw = item
                            e.dma_start(out=out, in_=in_, **kw).then_inc(sems[key], 16)
                return body
            if streams['sp']:
                block.sync(run('sp'))
            if streams['pe']:
                block.tensor(run('pe'))
            if streams['act']:
                block.scalar(run('act'))
            if streams['dve']:
                block.vector(run('dve'))
            if streams['pool']:
                block.gpsimd(run('pool'))

    def mm(self, out, lhsT, rhs, start=True, stop=True):
        self.op('pe', lambda e: e.matmul(out, lhsT, rhs, start=start, stop=stop),
                reads=[lhsT, rhs] + ([] if start else [out]), writes=[out])

    def transpose(self, out, in_, ident):
        self.op('pe', lambda e: e.transpose(out, in_, ident), reads=[in_, ident], writes=[out])

    def act(self, out, in_, func, bias=None, scale=1.0, accum_out=None):
        reads = [in_]
        writes = [out]
        kw = {}
        if bias is not None:
            kw['bias'] = bias
            if not isinstance(bias, (int, float)):
                reads.append(bias)
        if not isinstance(scale, (int, float)):
            reads.append(scale)
        kw['scale'] = scale
        if accum_out is not None:
            kw['accum_out'] = accum_out
            writes.append(accum_out)
        self.op('act', lambda e: e.activation(out, in_, func, **kw), reads=reads, writes=writes)

    def tt(self, eng, out, in0, in1, op):
        self.op(eng, lambda e: e.tensor_tensor(out, in0, in1, op), reads=[in0, in1], writes=[out])

    def ts(self, eng, out, in0, s1, s2, op0, op1=None):
        reads = [in0] + [s for s in (s1, s2) if s is not None and not isinstance(s, (int, float))]
        if op1 is None:
            self.op(eng, lambda e: e.tensor_scalar(out, in0, s1, None, op0), reads=reads, writes=[out])
        else:
            self.op(eng, lambda e: e.tensor_scalar(out, in0, s1, s2, op0, op1), reads=reads, writes=[out])

    def stt(self, eng, out, in0, scalar, in1, op0, op1):
        reads = [in0, in1] + ([] if isinstance(scalar, (int, float)) else [scalar])
        self.op(eng, lambda e: e.scalar_tensor_tensor(out, in0, scalar, in1, op0, op1), reads=reads, writes=[out])

    def copy(self, eng, out, in_):
        if eng == 'act':
            self.op('act', lambda e: e.copy(out, in_), reads=[in_], writes=[out])
        else:
            self.op(eng, lambda e: e.tensor_copy(out, in_), reads=[in_], writes=[out])

    def memset(self, eng, ap, val):
        self.op(eng, lambda e: e.memset(ap, val), reads=[], writes=[ap])

    def recip(self, out, in_):
        self.op('dve', lambda e: e.reciprocal(out, in_), reads=[in_], writes=[out])

    def reduce(self, eng, out, in_, op, axis=None):
        axis = axis if axis is not None else AX.X
        self.op(eng, lambda e: e.tensor_reduce(out, in_, axis, op), reads=[in_], writes=[out])

import numpy as np
from concourse.bass_utils import run_bass_kernel_spmd

from contextlib import ExitStack
import math

D = 1024
T = 2304
NBLK = 18
NLAT = 2048
NCTX = 256
INW = 7952
OFF = {'a_q': 0, 'a_k': 512, 'a_v': 640, 'a_z': 768, 'b_a': 1280, 'b_b': 1792, 'b_z': 2304,
       'c_qkv': 2816, 'c_z': 4352, 'c_a': 4864, 'c_b': 4872, 'gate': 4880}
TTILES = [(0, 256)] + [(256 + 512 * i, 512) for i in range(4)]
ALPHA = 8 ** 0.25
LN_EPS = 1e-5
RMS_EPS = 1e-6
NSLOT = 4


def host_consts():
    c = {}
    c['identf'] = np.eye(128, dtype=np.float32)
    inv = 10000.0 ** (-np.arange(16, dtype=np.float32) / 16)
    pos = np.arange(NLAT)
    row = (pos // 64).astype(np.float32)
    col = (pos % 64).astype(np.float32)
    cosT = np.zeros((128, NLAT), np.float32)
    sinT = np.zeros((128, NLAT), np.float32)
    rmat = np.zeros((128, 128), np.float32)
    for p in range(128):
        d = p % 64
        axis = d // 32
        half = (d % 32) // 16
        pr = d % 16
        ang = (row if axis == 0 else col) * inv[pr]
        cosT[p] = np.cos(ang)
        sinT[p] = np.sin(ang) * (-1.0 if half == 0 else 1.0)
        partner = p + 16 if half == 0 else p - 16
        rmat[partner, p] = 1.0
    c['cosT'] = cosT
    c['sinT'] = sinT
    c['rmat'] = rmat
    k = np.arange(128)[:, None]
    q = np.arange(128)[None, :]
    bm = np.zeros((128, 3, 128), np.float32)
    bm[:, 0, :] = (k >= q)
    bm[:, 1, :] = 1.0
    bm[:, 2, :] = (k <= q)
    c['bandmask'] = bm.reshape(128, 384)
    j = np.arange(128)[:, None]
    i = np.arange(128)[None, :]
    same = np.ones((128, 128), bool)
    dn = np.zeros((7, 128, 128), np.float32)
    dn[0] = same & (j <= i)
    dn[1] = same & (j >= i)
    dn[2] = same
    dn[3] = same & (i > j)
    dn[4] = same & (i >= j)
    dn[5] = same & (i < j)
    dn[6] = same & (i <= j)
    c['dnmask'] = np.ascontiguousarray(dn.transpose(1, 0, 2)).reshape(128, 7 * 128)
    lv = np.zeros((128, 7, 128), np.float32)
    for li, sz in enumerate((1, 2, 4, 8, 16, 32, 64)):
        lv[:, li, :] = ((j // (2 * sz)) == (i // (2 * sz))) & ((j // sz) != (i // sz))
    c['lvlmask'] = lv.reshape(128, 7 * 128)
    return c


def build(n_layers=4, nseq=2, dbg=None, stop=None):
    nc = bass.Bass("TRN2", target_bir_lowering=False)
    P = Prog(nc)

    def din(name, shape, dt=F32):
        return nc.dram_tensor(name, list(shape), dt, kind="ExternalInput").ap()

    x_in = din("x", [2, NLAT, D])
    ctx_in = din("ctx", [2, NCTX, D])
    c_in = din("c", [2, D])
    cctx_in = din("c_ctx", [1, D])
    w_ada = din("w_ada", [4, D, 3 * D])
    b_ada = din("b_ada", [4, 3 * D])
    w_in = din("w_in", [4, D, INW])
    a_sink = din("a_sink", [4, 8])
    b_conv_w = din("b_conv_w", [4, 31, 512])
    b_conv_b = din("b_conv_b", [4, 512])
    b_norm_g = din("b_norm_g", [4, 512])
    b_norm_b = din("b_norm_b", [4, 512])
    c_conv_w = din("c_conv_w", [4, 3, 1536])
    c_a_log = din("c_a_log", [4, 8])
    c_dt_bias = din("c_dt_bias", [4, 8])
    c_norm_g = din("c_norm_g", [4, 128])
    w_branch = din("w_branch", [4, 1536, D])
    w_out = din("w_out", [4, D, D])
    ln_g = din("ln_g", [4, D])
    ln_b = din("ln_b", [4, D])
    k_identf = din("identf", [128, 128])
    k_cosT = din("cosT", [128, NLAT])
    k_sinT = din("sinT", [128, NLAT])
    k_rmat = din("rmat", [128, 128])
    k_bandmask = din("bandmask", [128, 384])
    k_dnmask = din("dnmask", [128, 7 * 128])
    k_lvl = din("lvlmask", [128, 7 * 128])
    y_out = nc.dram_tensor("y", [2, NLAT, D], F32, kind="ExternalOutput").ap()

    def dscr(name, shape, dt):
        return nc.dram_tensor(name, list(shape), dt, kind="Internal").ap()

    xres = dscr("xres", [2, T, D], F32)
    wbf_in = dscr("wbf_in", [4, D, INW], BF16)
    wbf_br = dscr("wbf_br", [4, 1536, D], BF16)
    wbf_out = dscr("wbf_out", [4, D, D], BF16)
    modrow_d = dscr("modrow", [4, 3, 3 * D], F32)
    yT_d = dscr("yT", [3, 512, T], BF16)
    wbf_g = dscr("wbf_g", [4, 8, 128, 8, 3, 128], BF16)

    _uq = {'n': 0}

    def uniq(name):
        _uq['n'] += 1
        return "%s_u%d" % (name, _uq['n'])

    dbg_out = {}
    if dbg:
        for name, (shape, dt) in dbg.items():
            dbg_out[name] = nc.dram_tensor("dbg_" + name, list(shape), dt, kind="ExternalOutput").ap()

    with ExitStack() as gst:
        def gsb(name, shape, dt):
            return gst.enter_context(nc.sbuf_tensor(uniq(name), list(shape), dt))

        psums = [gst.enter_context(nc.psum_tensor("ps%d" % i, [128, 512], F32)) for i in range(8)]
        ps_state = {'i': 0, 'pool': list(range(8))}

        def set_pool(pool):
            ps_state['pool'] = list(pool)
            ps_state['i'] = 0

        def psum():
            pool = ps_state['pool']
            t = psums[pool[ps_state['i'] % len(pool)]]
            ps_state['i'] += 1
            return t

        identf = gsb("identf", [128, 128], F32)
        identb = gsb("identb", [128, 128], BF16)
        onesf = gsb("onesf", [128, 128], F32)
        modT = gsb("modT", [128, 4, 24, 3], F32)
        uT = gsb("uT", [128, 8, T], BF16)

        P.dma(identf[:], k_identf)
        P.copy('dve', identb[:], identf[:])
        P.memset('dve', onesf[:], 1.0)

        P.marks = []

        def barrier():
            P.marks.append(dict(P.count))
            for e in ('sp', 'pe', 'act', 'dve', 'pool'):
                P.wait_all(e)

        rr = {'i': 0}

        def rot(engs=('dve', 'act', 'pool')):
            rr['i'] += 1
            return engs[rr['i'] % len(engs)]

        for s in range(nseq):
            P.dma(xres[s, 0:NCTX, :], ctx_in[s])
            P.dma(xres[s, NCTX:T, :], x_in[s])

        with ExitStack() as st:
            def sb(name, shape, dt):
                return st.enter_context(nc.sbuf_tensor(uniq(name), list(shape), dt))
            stg = [sb("stg%d" % i, [128, 2048], F32) for i in range(3)]
            o16 = [sb("o16_%d" % i, [128, 2048], BF16) for i in range(3)]
            engs = ['dve', 'act', 'pool']
            cnt = 0
            gstg = [sb("gstg%d" % i, [128, 3072], F32) for i in range(2)]
            g16 = [sb("g16_%d" % i, [128, 3072], BF16) for i in range(2)]

            def cast_rows(src, dst, ncols):
                nonlocal cnt
                c0 = 0
                while c0 < ncols:
                    n = min(2048, ncols - c0)
                    i = cnt % 3
                    cnt += 1
                    P.dma(stg[i][:, 0:n], src[:, c0:c0 + n])
                    P.copy(engs[i], o16[i][:, 0:n], stg[i][:, 0:n])
                    P.dma(dst[:, c0:c0 + n], o16[i][:, 0:n], queue='pool' if False else 'sp')
                    c0 += n
            for l in range(1):
                for k in range(8):
                    cast_rows(w_in[l, k * 128:(k + 1) * 128, :], wbf_in[l, k * 128:(k + 1) * 128, :], OFF['gate'])
                for k in range(8):
                    gi = k % 2
                    P.dma(gstg[gi][:], w_in[l, k * 128:(k + 1) * 128, OFF['gate']:INW])
                    P.copy('dve' if gi == 0 else 'act', g16[gi][:], gstg[gi][:])
                    for r in range(3):
                        P.dma(wbf_g[l].rearrange("m p k r c -> p k r m c")[:, k, r],
                              g16[gi][:, r * 1024:(r + 1) * 1024].rearrange("p (m c) -> p m c", m=8))
                for k in range(12):
                    cast_rows(w_branch[l, k * 128:(k + 1) * 128, :], wbf_br[l, k * 128:(k + 1) * 128, :], D)
                for k in range(8):
                    cast_rows(w_out[l, k * 128:(k + 1) * 128, :], wbf_out[l, k * 128:(k + 1) * 128, :], D)

            if stop == 'W':
                barrier()
                P.emit()
                return nc, P
            cT = sb("cT", [128, 8, 3], F32)
            scT = sb("scT", [128, 8, 3], F32)
            crow = sb("crow", [3, D], F32)
            P.dma(crow[0:2, :], c_in)
            P.dma(crow[2:3, :], cctx_in)
            pt = psum()
            for k in range(8):
                P.transpose(pt[:, k * 3:(k + 1) * 3], crow[0:3, k * 128:(k + 1) * 128], identf[0:3, 0:3])
            P.copy('dve', cT[:].rearrange("p k r -> p (k r)"), pt[:, 0:24])
            P.act(scT[:], cT[:], AF.Silu)
            wa = [sb("wa%d" % i, [128, 3 * D], F32) for i in range(2)]
            bada = sb("bada", [3, 3 * D], F32)
            mrow = sb("mrow", [3, 3 * D], F32)
            for l in range(n_layers):
                P.dma(bada[:], b_ada[l:l + 1, :].partition_broadcast(3))
                banks = [psums[i] for i in range(6)]
                for k in range(8):
                    w = wa[k % 2]
                    P.dma(w[:], w_ada[l, k * 128:(k + 1) * 128, :])
                    for j in range(6):
                        P.mm(banks[j][0:3, :], scT[:, k, :], w[:, j * 512:(j + 1) * 512], start=(k == 0), stop=(k == 7))
                for j in range(6):
                    P.tt('dve', mrow[:, j * 512:(j + 1) * 512], banks[j][0:3, :], bada[:, j * 512:(j + 1) * 512], ALU.add)
                P.dma(modrow_d[l], mrow[:])
                pt = psum()
                for j in range(24):
                    P.transpose(pt[:, j * 3:(j + 1) * 3], mrow[0:3, j * 128:(j + 1) * 128], identf[0:3, 0:3])
                P.copy('dve', modT[:, l].rearrange("p j r -> p (j r)"), pt[:, 0:72])
                P.ts('dve', modT[:, l, 8:16, :], modT[:, l, 8:16, :], 1.0, None, ALU.add)
            barrier()

        if stop == '0':
            P.emit()
            return nc, P
        wq_v = [wbf_in[l].rearrange("(k p) c -> p k c", p=128) for l in range(4)]
        wbr_v = [wbf_br[l].rearrange("(k p) c -> p k c", p=128) for l in range(4)]
        wout_v = [wbf_out[l].rearrange("(k p) c -> p k c", p=128) for l in range(4)]

        def load_T(st_sb, dst, src, n, tag):
            tmp = st_sb("ltT_" + tag, [n, 128], F32)
            P.dma(tmp[:], src)
            pt = psum()
            P.transpose(pt[:, 0:n], tmp[0:n, :], identf[0:n, 0:n])
            P.copy('dve', dst, pt[:, 0:n])

        def bgcast_gen(lw, stg_, o16_):
            cntb = 0

            def piece(src, dst):
                nonlocal cntb
                i = cntb % 2
                cntb += 1
                n = src.shape[-1]
                P.dma(stg_[i][:, 0:n], src)
                P.copy('pool', o16_[i][:, 0:n], stg_[i][:, 0:n])
                P.dma(dst, o16_[i][:, 0:n] if len(dst.shape) == 2 else o16_[i][:, 0:n].rearrange("p (m c) -> p m c", m=8), queue='pool')
            for k in range(8):
                rs = slice(k * 128, (k + 1) * 128)
                c0 = 0
                while c0 < OFF['gate']:
                    n = min(1024, OFF['gate'] - c0)
                    piece(w_in[lw, rs, c0:c0 + n], wbf_in[lw, rs, c0:c0 + n])
                    c0 += n
                    yield
                for r in range(3):
                    g0 = OFF['gate'] + r * 1024
                    piece(w_in[lw, rs, g0:g0 + 1024], wbf_g[lw].rearrange("m p k r c -> p k r m c")[:, k, r])
                    yield
            for k in range(12):
                rs = slice(k * 128, (k + 1) * 128)
                piece(w_branch[lw, rs, :], wbf_br[lw, rs, :])
                yield
            for k in range(8):
                rs = slice(k * 128, (k + 1) * 128)
                piece(w_out[lw, rs, :], wbf_out[lw, rs, :])
                yield

        def dump(name, src):
            import os as _os
            if _os.environ.get('NODUMP'):
                return
            if name in dbg_out:
                P.dma(dbg_out[name], src)

        for l in range(n_layers):
            for s in range(nseq):
                last = (l == 3)
                with ExitStack() as st:
                    def sb(name, shape, dt):
                        return st.enter_context(nc.sbuf_tensor(uniq(name), list(shape), dt))
                    set_pool(range(8))
                    xt = [sb("xt%d" % i, [128, D], F32) for i in range(3)]
                    for tb in range(NBLK):
                        r = 2 if tb < 2 else s
                        x_ = xt[tb % 3]
                        P.dma(x_[:], xres[s, tb * 128:(tb + 1) * 128, :])
                        for half in range(2):
                            pt = psum()
                            for kk in range(4):
                                k = half * 4 + kk
                                P.transpose(pt[:, kk * 128:(kk + 1) * 128], x_[:, k * 128:(k + 1) * 128], identf[:])
                            for kk in range(4):
                                k = half * 4 + kk
                                e = 'dve' if half == 0 else 'act'
                                dst = uT[:, k, tb * 128:(tb + 1) * 128]
                                src = pt[:, kk * 128:(kk + 1) * 128]
                                if e == 'dve':
                                    P.ts('dve', dst, src, modT[:, l, 8 + k, r:r + 1], modT[:, l, k, r:r + 1], ALU.mult, ALU.add)
                                else:
                                    P.act(dst, src, AF.Identity, bias=modT[:, l, k, r:r + 1], scale=modT[:, l, 8 + k, r:r + 1])
                    barrier()
                if l == 0 and s == 0:
                    dump('uT', uT[:])
                if stop == 'U':
                    break

                with ExitStack() as st:
                    def sb(name, shape, dt):
                        return st.enter_context(nc.sbuf_tensor(uniq(name), list(shape), dt))
                    set_pool(range(4))
                    cosT = sb("cosT", [128, NLAT], F32)
                    sinT = sb("sinT", [128, NLAT], F32)
                    rmat = sb("rmat", [128, 128], F32)
                    bmf = sb("bmf", [128, 384], F32)
                    bmask = sb("bmask", [128, 384], BF16)
                    P.dma(cosT[:], k_cosT)
                    P.dma(sinT[:], k_sinT)
                    P.dma(rmat[:], k_rmat)
                    P.dma(bmf[:], k_bandmask)
                    P.copy('pool', bmask[:], bmf[:])
                    Wq = sb("Wq", [128, 8, 512], BF16)
                    Wk = sb("Wk", [128, 8, 128], BF16)
                    Wv = sb("Wv", [128, 8, 128], BF16)
                    Wz = sb("Wz", [128, 8, 512], BF16)
                    for t in range(4):
                        P.dma(Wq[:, :, t * 128:t * 128 + 64], wq_v[l][:, :, t * 64:(t + 1) * 64])
                        P.dma(Wq[:, :, t * 128 + 64:(t + 1) * 128], wq_v[l][:, :, (4 + t) * 64:(5 + t) * 64])
                    P.dma(Wk[:], wq_v[l][:, :, OFF['a_k']:OFF['a_k'] + 128])
                    P.dma(Wv[:], wq_v[l][:, :, OFF['a_v']:OFF['a_v'] + 128])
                    P.dma(Wz[:], wq_v[l][:, :, OFF['a_z']:OFF['a_z'] + 512])
                    qT = sb("qT", [128, 4, T], BF16)
                    kT = sb("kT", [128, T], BF16)
                    kTz = [sb("kTz%d" % i, [128, T], BF16) for i in range(2)]
                    P.memset('pool', kTz[0][64:128, :], 0.0)
                    P.memset('pool', kTz[1][0:64, :], 0.0)
                    Vaug = sb("Vaug", [128, NBLK, 2, 66], BF16)
                    esink = sb("esink", [128, 8], F32)
                    P.dma(esink[:], a_sink[l:l + 1, :].partition_broadcast(128))
                    P.act(esink[:], esink[:], AF.Exp)
                    P.memset('pool', Vaug[:, :, :, 64:66], 1.0)
                    qraw = [sb("qraw%d" % i, [128, 512], F32) for i in range(2)]
                    t1s = [sb("t1s%d" % i, [128, 512], F32) for i in range(2)]
                    t2s = [sb("t2s%d" % i, [128, 512], F32) for i in range(2)]
                    it = 0
                    for (off, n) in TTILES:
                        for t in range(5):
                            ps = psum()
                            for k in range(8):
                                lhsT = Wq[:, k, t * 128:(t + 1) * 128] if t < 4 else Wk[:, k, :]
                                P.mm(ps[:, 0:n], lhsT, uT[:, k, off:off + n], start=(k == 0), stop=(k == 7))
                            dst = qT[:, t, off:off + n] if t < 4 else kT[:, off:off + n]
                            if off < NCTX:
                                P.copy('act', dst, ps[:, 0:n])
                                if t == 4:
                                    P.copy('act', kTz[0][0:64, off:off + n], ps[0:64, 0:n])
                                    P.copy('act', kTz[1][64:128, off:off + n], ps[64:128, 0:n])
                            else:
                                pos = off - NCTX
                                qr = qraw[it % 2]
                                a1 = t1s[it % 2]
                                a2 = t2s[it % 2]
                                it += 1
                                P.copy('act', qr[:, 0:n], ps[:, 0:n])
                                ps2 = psum()
                                P.mm(ps2[:, 0:n], rmat[:], qr[:, 0:n])
                                P.tt('pool', a1[:, 0:n], qr[:, 0:n], cosT[:, pos:pos + n], ALU.mult)
                                P.tt('dve', a2[:, 0:n], ps2[:, 0:n], sinT[:, pos:pos + n], ALU.mult)
                                P.tt('dve', dst, a1[:, 0:n], a2[:, 0:n], ALU.add)
                                if t == 4:
                                    P.tt('dve', kTz[0][0:64, off:off + n], a1[0:64, 0:n], a2[0:64, 0:n], ALU.add)
                                    P.tt('dve', kTz[1][64:128, off:off + n], a1[64:128, 0:n], a2[64:128, 0:n], ALU.add)
                    if stop == 'A1':
                        barrier()
                        P.emit()
                        return nc, P
                    for tb in range(NBLK):
                        ps = psum()
                        for k in range(8):
                            P.mm(ps[:, 0:128], uT[:, k, tb * 128:(tb + 1) * 128], Wv[:, k, :], start=(k == 0), stop=(k == 7))
                        P.copy(rot(('dve', 'act')), Vaug[:, tb, :, 0:64], ps[:, 0:128].rearrange("p (h d) -> p h d", h=2))
                    if l == 0 and s == 0:
                        dump('qT', qT[:])
                        dump('kT', kT[:])
                    if stop == 'A2':
                        barrier()
                        P.emit()
                        return nc, P
                    pTs = [sb("pT%d" % i, [128, 2, 5, 128], BF16) for i in range(2)]
                    az = [sb("az%d" % i, [128, 512], BF16) for i in range(2)]
                    den = [sb("den%d" % i, [128, 8], F32) for i in range(2)]
                    yaf = [sb("yaf%d" % i, [128, 512], F32) for i in range(2)]
                    yab = [sb("yab%d" % i, [128, 512], BF16) for i in range(2)]
                    yTt = [sb("yTt%d" % i, [128, 4, 128], BF16) for i in range(2)]
                    ya_dst = yT_d[0].rearrange("(c p) t -> p c t", p=128)
                    ip = 0
                    for n in range(NBLK):
                        q0 = n * 128
                        if n < 2:
                            slots = []
                        else:
                            slots = [sl for sl in range(3) if 2 <= n - 1 + sl <= 17]
                        ctxt = [0, 1]
                        ops = [psums[4 + 2 * (n % 2)], psums[5 + 2 * (n % 2)]]
                        for t in range(4):
                            pT = pTs[ip % 2]
                            ip += 1
                            if slots:
                                for hh in range(2):
                                    psb = psum()
                                    pr = slice(hh * 64, (hh + 1) * 64)
                                    for sl in slots:
                                        kb = n - 1 + sl
                                        P.mm(psb[:, sl * 128:(sl + 1) * 128], kTz[hh][:, kb * 128:(kb + 1) * 128], qT[:, t, q0:q0 + 128])
                                    s0, s1 = slots[0], slots[-1] + 1
                                    P.act(pT[:, hh, s0:s1, :], psb[:, s0 * 128:s1 * 128].rearrange("p (a b) -> p a b", b=128), AF.Exp, scale=0.125)
                                    P.tt('dve', pT[:, hh, s0:s1, :], pT[:, hh, s0:s1, :], bmask[:, s0 * 128:s1 * 128].rearrange("p (a b) -> p a b", b=128), ALU.mult)
                            psc = psum()
                            for hh in range(2):
                                pr = slice(hh * 64, (hh + 1) * 64)
                                for ci in ctxt:
                                    P.mm(psc[:, (hh * 2 + ci) * 128:(hh * 2 + ci + 1) * 128], kTz[hh][:, ci * 128:(ci + 1) * 128], qT[:, t, q0:q0 + 128])
                            P.act(pT[:, :, 3:5, :], psc[:, :].rearrange("p (h c b) -> p h c b", h=2, c=2), AF.Exp, scale=0.125)
                            import os as _os
                            CORE = int(_os.environ.get('CORE', '9'))
                            for hh in range(2 if CORE >= 2 else 0):
                                tiles = [(sl, n - 1 + sl) for sl in slots] + [(3 + ci, ci) for ci in ctxt]
                                for ii, (sl, kb) in enumerate(tiles):
                                    P.mm(ops[hh][:, t * 65:(t + 1) * 65], pT[:, hh, sl, :], Vaug[:, kb, hh, 0:65],
                                         start=(ii == 0), stop=(ii == len(tiles) - 1))
                        if CORE < 3:
                            continue
                        psz = psum()
                        for k in range(8):
                            P.mm(psz[:, :], uT[:, k, q0:q0 + 128], Wz[:, k, :], start=(k == 0), stop=(k == 7))
                        az_ = az[n % 2]
                        P.act(az_[:], psz[:, :], AF.Silu)
                        dn_ = den[n % 2]
                        yf = yaf[n % 2]
                        yb = yab[n % 2]
                        for hh in range(2):
                            ov = ops[hh][:, 0:260].rearrange("p (t e) -> p t e", e=65)
                            P.tt('dve', dn_[:, hh * 4:(hh + 1) * 4], ov[:, :, 64], esink[:, hh * 4:(hh + 1) * 4], ALU.add)
                        P.recip(dn_[:], dn_[:])
                        for hh in range(2):
                            ov = ops[hh][:, 0:260].rearrange("p (t e) -> p t e", e=65)
                            P.tt('dve', yf[:, hh * 256:(hh + 1) * 256].rearrange("p (t d) -> p t d", d=64), ov[:, :, 0:64],
                                 dn_[:, hh * 4:(hh + 1) * 4].unsqueeze(2).to_broadcast([128, 4, 64]), ALU.mult)
                        P.tt('pool', yb[:], yf[:], az_[:], ALU.mult)
                        if CORE < 4:
                            continue
                        ptb = psum()[:].bitcast(BF16)
                        for c in range(4):
                            P.transpose(ptb[:, c * 128:(c + 1) * 128], yb[:, c * 128:(c + 1) * 128], identb[:])
                        yt = yTt[n % 2]
                        P.copy('act', yt[:].rearrange("p c t -> p (c t)"), ptb[:, 0:512])
                        P.dma(ya_dst[:, :, q0:q0 + 128], yt[:], queue='pool')
                    barrier()
                if stop == 'A':
                    break

                with ExitStack() as st:
                    def sb(name, shape, dt):
                        return st.enter_context(nc.sbuf_tensor(uniq(name), list(shape), dt))
                    set_pool(range(8))
                    HW = 2364
                    hpad = sb("hpad", [128, 4, HW], BF16)
                    zT = sb("zT", [128, 4, T], BF16)
                    Wa = sb("Wa", [128, 8, 512], BF16)
                    Wb = sb("Wb", [128, 8, 512], BF16)
                    Wzb = sb("Wzb", [128, 8, 512], BF16)
                    P.dma(Wa[:], wq_v[l][:, :, OFF['b_a']:OFF['b_a'] + 512])
                    P.dma(Wb[:], wq_v[l][:, :, OFF['b_b']:OFF['b_b'] + 512])
                    P.dma(Wzb[:], wq_v[l][:, :, OFF['b_z']:OFF['b_z'] + 512])
                    cw = sb("cw", [128, 4, 31], F32)
                    cb = sb("cb", [128, 4], F32)
                    ng = sb("ng", [128, 4], F32)
                    nb_ = sb("nb", [128, 4], F32)
                    for c in range(4):
                        load_T(sb, cw[:, c, :], b_conv_w[l, :, c * 128:(c + 1) * 128], 31, "cw%d" % c)
                    load_T(sb, cb[:], b_conv_b[l].rearrange("(c p) -> c p", p=128), 4, "cb")
                    load_T(sb, ng[:], b_norm_g[l].rearrange("(c p) -> c p", p=128), 4, "ng")
                    load_T(sb, nb_[:], b_norm_b[l].rearrange("(c p) -> c p", p=128), 4, "nb")
                    P.memset('pool', hpad[:, :, 0:15], 0.0)
                    P.memset('pool', hpad[:, :, 271:301], 0.0)
                    P.memset('pool', hpad[:, :, 2349:2364], 0.0)

                    def hcol(tau):
                        return tau + 15 if tau < NCTX else tau + 45
                    sg = [sb("sg%d" % i, [128, 512], BF16) for i in range(2)]
                    i2 = 0
                    for (off, n) in TTILES:
                        for c in range(4):
                            psa = psum()
                            psb = psum()
                            psz = psum()
                            for k in range(8):
                                P.mm(psa[:, 0:n], Wa[:, k, c * 128:(c + 1) * 128], uT[:, k, off:off + n], start=(k == 0), stop=(k == 7))
                            for k in range(8):
                                P.mm(psb[:, 0:n], Wb[:, k, c * 128:(c + 1) * 128], uT[:, k, off:off + n], start=(k == 0), stop=(k == 7))
                            for k in range(8):
                                P.mm(psz[:, 0:n], Wzb[:, k, c * 128:(c + 1) * 128], uT[:, k, off:off + n], start=(k == 0), stop=(k == 7))
                            g_ = sg[i2 % 2]
                            i2 += 1
                            P.act(g_[:, 0:n], psb[:, 0:n], AF.Sigmoid)
                            P.tt('dve', hpad[:, c, hcol(off):hcol(off) + n], psa[:, 0:n], g_[:, 0:n], ALU.mult)
                            P.act(zT[:, c, off:off + n], psz[:, 0:n], AF.Silu)
                    accA = sb("accA", [128, 4, 512], F32)
                    sq = sb("sq", [128, 4, 512], F32)
                    mean = sb("mean", [128, 512], F32)
                    msq = sb("msq", [128, 512], F32)
                    rstd = sb("rstd", [128, 512], F32)
                    xc = [sb("xc%d" % i, [128, 512], F32) for i in range(2)]
                    sa = [sb("sa%d" % i, [128, 512], F32) for i in range(2)]
                    ybt = [sb("ybt%d" % i, [128, 4, 512], BF16) for i in range(2)]
                    yb_dst = yT_d[1].rearrange("(c p) t -> p c t", p=128)
                    dg = sb("dg", [128, 4, 31, 128], BF16)
                    for c in range(4):
                        for k in range(31):
                            if (c * 31 + k) % 2 == 0:
                                P.ts('dve', dg[:, c, k, :], identf[:], cw[:, c, k:k + 1], None, ALU.mult)
                            else:
                                P.act(dg[:, c, k, :], identf[:], AF.Identity, scale=cw[:, c, k:k + 1])
                    for ti, (off, n) in enumerate(TTILES):
                        h0 = hcol(off) - 15
                        for c in range(4):
                            pcv = psum()
                            for k in range(31):
                                P.mm(pcv[:, 0:n], dg[:, c, k, :], hpad[:, c, h0 + k:h0 + k + n], start=(k == 0), stop=(k == 30))
                            P.act(accA[:, c, 0:n], pcv[:, 0:n], AF.Identity, bias=cb[:, c:c + 1])
                            P.act(sq[:, c, 0:n], pcv[:, 0:n], AF.Square, bias=cb[:, c:c + 1])
                        p1 = psum()
                        p2 = psum()
                        for c in range(4):
                            P.mm(p1[:, 0:n], onesf[:], accA[:, c, 0:n], start=(c == 0), stop=(c == 3))
                        for c in range(4):
                            P.mm(p2[:, 0:n], onesf[:], sq[:, c, 0:n], start=(c == 0), stop=(c == 3))
                        P.ts('dve', mean[:, 0:n], p1[:, 0:n], 1.0 / 512, None, ALU.mult)
                        P.tt('dve', msq[:, 0:n], mean[:, 0:n], mean[:, 0:n], ALU.mult)
                        P.stt('dve', rstd[:, 0:n], p2[:, 0:n], 1.0 / 512, msq[:, 0:n], ALU.mult, ALU.subtract)
                        P.ts('dve', rstd[:, 0:n], rstd[:, 0:n], LN_EPS, None, ALU.add)
                        P.act(rstd[:, 0:n], rstd[:, 0:n], AF.Ln)
                        P.act(rstd[:, 0:n], rstd[:, 0:n], AF.Exp, scale=-0.5)
                        yt = ybt[ti % 2]
                        for c in range(4):
                            x_ = xc[c % 2]
                            a_ = sa[c % 2]
                            P.tt('pool', x_[:, 0:n], accA[:, c, 0:n], mean[:, 0:n], ALU.subtract)
                            P.tt('dve', x_[:, 0:n], x_[:, 0:n], rstd[:, 0:n], ALU.mult)
                            P.act(a_[:, 0:n], x_[:, 0:n], AF.Silu, bias=nb_[:, c:c + 1], scale=ng[:, c:c + 1])
                            P.tt('pool', yt[:, c, 0:n], a_[:, 0:n], zT[:, c, off:off + n], ALU.mult)
                        P.dma(yb_dst[:, :, off:off + n], yt[:, :, 0:n], queue='pool')
                    barrier()
                if stop == 'B':
                    break

                with ExitStack() as st:
                    def sb(name, shape, dt):
                        return st.enter_context(nc.sbuf_tensor(uniq(name), list(shape), dt))
                    set_pool(range(6))
                    dnm = sb("dnm", [128, 7, 128], F32)
                    P.dma(dnm[:].rearrange("p a b -> p (a b)"), k_dnmask)
                    lvl = sb("lvl", [128, 7, 128], F32)
                    P.dma(lvl[:].rearrange("p a b -> p (a b)"), k_lvl)
                    mcum = [dnm[:, 0, :], dnm[:, 1, :]]
                    mtot = dnm[:, 2, :]
                    mstr = [dnm[:, 3, :], dnm[:, 5, :]]
                    minc = [dnm[:, 4, :], dnm[:, 6, :]]
                    czT = sb("czT", [128, 4, T], BF16)
                    abt = sb("abt", [128, NBLK, 16], F32)
                    with ExitStack() as st2:
                        Wcz = st2.enter_context(nc.sbuf_tensor(uniq("Wcz"), [128, 8, 512], BF16))
                        Wab = st2.enter_context(nc.sbuf_tensor(uniq("Wab"), [128, 8, 16], BF16))
                        P.dma(Wcz[:], wq_v[l][:, :, OFF['c_z']:OFF['c_z'] + 512])
                        P.dma(Wab[:], wq_v[l][:, :, OFF['c_a']:OFF['c_a'] + 16])
                        for (off, n) in TTILES:
                            for c in range(4):
                                ps = psum()
                                for k in range(8):
                                    P.mm(ps[:, 0:n], Wcz[:, k, c * 128:(c + 1) * 128], uT[:, k, off:off + n], start=(k == 0), stop=(k == 7))
                                P.act(czT[:, c, off:off + n], ps[:, 0:n], AF.Silu)
                        for tb in range(NBLK):
                            ps = psum()
                            for k in range(8):
                                P.mm(ps[:, 0:16], uT[:, k, tb * 128:(tb + 1) * 128], Wab[:, k, :], start=(k == 0), stop=(k == 7))
                            P.copy('dve', abt[:, tb, :], ps[:, 0:16])
                        barrier()
                    dtb = sb("dtb", [128, 8], F32)
                    negA = sb("negA", [128, 8], F32)
                    P.dma(dtb[:], c_dt_bias[l:l + 1, :].partition_broadcast(128))
                    P.dma(negA[:], c_a_log[l:l + 1, :].partition_broadcast(128))
                    P.act(negA[:], negA[:], AF.Exp)
                    P.ts('dve', negA[:], negA[:], -1.0, None, ALU.mult)
                    g_tok = sb("g_tok", [128, NBLK, 8], F32)
                    beta = sb("beta", [128, NBLK, 8], F32)
                    nbeta = sb("nbeta", [128, NBLK, 8], F32)
                    gc_tok = sb("gc_tok", [128, NBLK, 8], F32)
                    gl_tok = sb("gl_tok", [128, NBLK, 8], F32)
                    edk = sb("edk", [128, NBLK, 8], F32)
                    P.tt('dve', g_tok[:], abt[:, :, 0:8], dtb[:].unsqueeze(1).to_broadcast([128, NBLK, 8]), ALU.add)
                    P.act(g_tok[:], g_tok[:], AF.Exp)
                    P.ts('dve', g_tok[:], g_tok[:], 1.0, None, ALU.add)
                    P.act(g_tok[:], g_tok[:], AF.Ln)
                    P.tt('dve', g_tok[:], g_tok[:], negA[:].unsqueeze(1).to_broadcast([128, NBLK, 8]), ALU.mult)
                    P.act(beta[:], abt[:, :, 8:16], AF.Sigmoid)
                    P.ts('dve', nbeta[:], beta[:], -1.0, None, ALU.mult)
                    for d in range(2):
                        ps = psum()
                        P.mm(ps[:, 0:72].rearrange("p (b h) -> p b h", h=4), mcum[d], g_tok[:, :, d * 4:(d + 1) * 4])
                        P.copy('dve', gc_tok[:, :, d * 4:(d + 1) * 4], ps[:, 0:72].rearrange("p (b h) -> p b h", h=4))
                        ps = psum()
                        P.mm(ps[:, 0:72].rearrange("p (b h) -> p b h", h=4), mtot, g_tok[:, :, d * 4:(d + 1) * 4])
                        P.copy('dve', gl_tok[:, :, d * 4:(d + 1) * 4], ps[:, 0:72].rearrange("p (b h) -> p b h", h=4))
                    P.tt('dve', edk[:], gl_tok[:], gc_tok[:], ALU.subtract)
                    P.act(edk[:], edk[:], AF.Exp)
                    if l == 0 and s == 0:
                        dump('g_tok', g_tok[:])
                        dump('gc_tok', gc_tok[:])

                    ccw = sb("ccw", [128, 12, 3], F32)
                    for c in range(12):
                        load_T(sb, ccw[:, c, :], c_conv_w[l, :, c * 128:(c + 1) * 128], 3, "ccw%d" % c)
                    cng = sb("cng", [128, 1], F32)
                    P.dma(cng[:], c_norm_g[l].rearrange("(p o) -> p o", o=1))
                    RW = T + 4
                    raw = sb("raw", [128, RW], F32)
                    acc = sb("acc", [128, T], F32)
                    kTf = sb("kTf", [128, T], F32)
                    k_tok = sb("k_tok", [128, NBLK, 128], F32)
                    v_tok = sb("v_tok", [128, NBLK, 128], F32)
                    oT = sb("oT", [128, T], F32)
                    qTb = sb("qTb", [128, T], BF16)
                    kTb = sb("kTb", [128, T], BF16)
                    Wh = [sb("Wh%d" % i, [128, 8, 384], BF16) for i in range(2)]
                    P.memset('pool', raw[:, 0:1], 0.0)
                    P.memset('pool', raw[:, 257:259], 0.0)
                    P.memset('pool', raw[:, RW - 1:RW], 0.0)

                    def rcol(tau):
                        return tau + 1 if tau < NCTX else tau + 3
                    ring = {}
                    for d in range(2):
                        for nm in ('TT', 'aqk', 'kg', 'qd', 'kdec', 'egc'):
                            ring[(d, nm)] = [sb("rg_%s%d_%d" % (nm, d, i), [128, 128], BF16 if nm in ('aqk', 'qd') else F32) for i in range(NSLOT)]
                    tmpn = {}

                    def tmp(nm, i=0, dt=F32):
                        key = (nm, i)
                        if key not in tmpn:
                            tmpn[key] = sb("tp_%s" % nm, [128, 128], dt)
                        return tmpn[key]
                    S = [sb("S%d" % d, [128, 128], F32) for d in range(2)]
                    Xz = [[sb("Xz%d_%d" % (d, i), [128, 128], F32) for i in range(1)] for d in range(2)]
                    vnz = [[sb("vnz%d_%d" % (d, i), [128, 128], F32) for i in range(1)] for d in range(2)]
                    S16 = [sb("S16_%d" % d, [128, 128], BF16) for d in range(2)]
                    vn16 = [[sb("vn16_%d_%d" % (d, i), [128, 128], BF16) for i in range(1)] for d in range(2)]
                    for d in range(2):
                        for i in range(1):
                            P.memset('pool', Xz[d][i][:], 0.0)
                            P.memset('pool', vnz[d][i][:], 0.0)
                            P.memset('pool', vn16[d][i][:], 0.0)
                    ycb = sb("ycb", [128, T], BF16)

                    bg = None
                    if s == 0 and l + 1 < n_layers:
                        bstg = [sb("bstg%d" % i, [128, 1024], F32) for i in range(2)]
                        bo16 = [sb("bo16_%d" % i, [128, 1024], BF16) for i in range(2)]
                        bg = bgcast_gen(l + 1, bstg, bo16)
                    rnd = 0
                    for h in range(4):
                        W_ = Wh[h % 2]
                        for j in range(3):
                            c0 = OFF['c_qkv'] + (j * 4 + h) * 128
                            P.dma(W_[:, :, j * 128:(j + 1) * 128], wq_v[l][:, :, c0:c0 + 128])
                        for j in range(3):
                            ct = j * 4 + h
                            for (off, n) in TTILES:
                                ps = psum()
                                for k in range(8):
                                    P.mm(ps[:, 0:n], W_[:, k, j * 128:(j + 1) * 128], uT[:, k, off:off + n], start=(k == 0), stop=(k == 7))
                                P.copy(rot(('act', 'dve')), raw[:, rcol(off):rcol(off) + n], ps[:, 0:n])
                            for (off, n) in ((0, NCTX), (NCTX, 1024), (NCTX + 1024, 1024)):
                                r0 = rcol(off)
                                e = 'dve'
                                P.ts(e, acc[:, off:off + n], raw[:, r0 - 1:r0 - 1 + n], ccw[:, ct, 0:1], None, ALU.mult)
                                P.stt(e, acc[:, off:off + n], raw[:, r0:r0 + n], ccw[:, ct, 1:2], acc[:, off:off + n], ALU.mult, ALU.add)
                                P.stt(e, acc[:, off:off + n], raw[:, r0 + 1:r0 + 1 + n], ccw[:, ct, 2:3], acc[:, off:off + n], ALU.mult, ALU.add)
                            P.act(acc[:], acc[:], AF.Silu)
                            if j < 2:
                                dstT = qTb if j == 0 else kTf
                                P.act(raw[:, 0:T], acc[:], AF.Square)
                                for (off, n) in TTILES:
                                    ps = psum()
                                    P.mm(ps[:, 0:n], onesf[:], raw[:, off:off + n])
                                    P.ts('dve', raw[:, off:off + n], ps[:, 0:n], RMS_EPS, None, ALU.add)
                                    P.act(raw[:, off:off + n], raw[:, off:off + n], AF.Ln)
                                    P.act(raw[:, off:off + n], raw[:, off:off + n], AF.Exp, scale=-0.5)
                                if j == 0:
                                    P.stt('dve', dstT[:], acc[:], 128 ** -0.5, raw[:, 0:T], ALU.mult, ALU.mult)
                                else:
                                    P.tt('dve', dstT[:], acc[:], raw[:, 0:T], ALU.mult)
                                if j == 1:
                                    P.copy('pool', kTb[:], dstT[:])
                                P.memset('pool', raw[:, 0:1], 0.0)
                                P.memset('pool', raw[:, 257:259], 0.0)
                            if j >= 1:
                                src = kTf if j == 1 else acc
                                dtok = k_tok if j == 1 else v_tok
                                for tb4 in range(0, NBLK, 4):
                                    nb4 = min(4, NBLK - tb4)
                                    pt = psum()
                                    for i in range(nb4):
                                        tb = tb4 + i
                                        P.transpose(pt[:, i * 128:(i + 1) * 128], src[:, tb * 128:(tb + 1) * 128], identf[:])
                                    P.copy(rot(('act', 'dve')), dtok[:, tb4:tb4 + nb4, :].rearrange("p b d -> p (b d)"), pt[:, 0:nb4 * 128])
                        if l == 0 and s == 0 and h == 0:
                            dump('dn_kT', kTf[:])
                            dump('dn_vtok', v_tok[:])
                        P.memset('pool', oT[:], 0.0)
                        orders = [list(range(NBLK)), [1, 0] + list(range(17, 1, -1))]

                        def prep_gen(d, blk, slot, tk):
                            dh = d * 4 + h
                            cs = slice(blk * 128, (blk + 1) * 128)
                            gbc = tmp('gbc', tk)
                            P.copy('pool', gbc[:], g_tok[:, blk, dh:dh + 1].to_broadcast([128, 128]))
                            psg = psum()
                            P.mm(psg[:, 0:128], gbc[:], mcum[d])
                            egc = ring[(d, 'egc')][slot]
                            gcb = tmp('gcb', tk)
                            P.copy('dve', gcb[:], psg[:, 0:128])
                            P.act(egc[:], gcb[:], AF.Exp)
                            diff = tmp('diff', tk)
                            P.ts('dve', diff[:], gcb[:], gc_tok[:, blk, dh:dh + 1], 0.0, ALU.subtract, ALU.min)
                            P.act(diff[:], diff[:], AF.Exp)
                            yield
                            pkk = psum()
                            P.mm(pkk[:, 0:128], kTb[:, cs], kTb[:, cs])
                            pqk = psum()
                            P.mm(pqk[:, 0:128], kTb[:, cs], qTb[:, cs])
                            m1 = tmp('m1', tk, BF16)
                            m2 = tmp('m2', tk, BF16)
                            P.tt('dve', m1[:], diff[:], mstr[d], ALU.mult)
                            P.tt('dve', m2[:], diff[:], minc[d], ALU.mult)
                            C = tmp('C0', tk, BF16)
                            P.stt('dve', C[:], pkk[:, 0:128], nbeta[:, blk, dh:dh + 1], m1[:], ALU.mult, ALU.mult)
                            P.tt('dve', ring[(d, 'aqk')][slot][:], pqk[:, 0:128], m2[:], ALU.mult)
                            yield
                            P.tt('pool', ring[(d, 'kg')][slot][:], kTf[:, cs], egc[:], ALU.mult)
                            P.tt('pool', ring[(d, 'qd')][slot][:], qTb[:, cs], egc[:], ALU.mult)
                            P.act(ring[(d, 'kdec')][slot][:], k_tok[:, blk, :], AF.Identity, scale=edk[:, blk, dh:dh + 1])
                            yield
                            pb = psum()[:].bitcast(BF16)
                            P.transpose(pb[:, 0:128], C[:], identb[:])
                            B0 = tmp('B0', tk, BF16)
                            P.copy('act', B0[:], pb[:, 0:128])
                            Tm = tmp('Tm', tk, BF16)
                            Um = tmp('Um', tk, BF16)
                            G0 = tmp('G0', tk, BF16)
                            H0 = tmp('H0', tk, BF16)
                            P.tt('pool', G0[:], C[:], lvl[:, 0, :], ALU.mult)
                            P.tt('dve', Um[:], G0[:], identb[:], ALU.add)
                            P.tt('pool', H0[:], B0[:], lvl[:, 0, :], ALU.mult)
                            P.tt('dve', Tm[:], H0[:], identb[:], ALU.add)
                            yield
                            for li in range(1, 7):
                                lastl = (li == 6)
                                Gs = tmp('Gs', tk * 2 + (li % 2), BF16)
                                P.tt('pool', Gs[:], C[:], lvl[:, li, :], ALU.mult)
                                pX = psum()
                                P.mm(pX[:, 0:128], Gs[:], Tm[:])
                                X16 = tmp('X16', tk, BF16)
                                P.copy('act', X16[:], pX[:, 0:128])
                                yield
                                pYT = psum()
                                P.mm(pYT[:, 0:128], X16[:], Um[:])
                                if not lastl:
                                    pY = psum()
                                    P.mm(pY[:, 0:128], Um[:], X16[:])
                                    P.tt('dve', Tm[:], Tm[:], pY[:, 0:128], ALU.add)
                                    P.tt('dve', Um[:], Um[:], pYT[:, 0:128], ALU.add)
                                else:
                                    P.tt('dve', ring[(d, 'TT')][slot][:], Um[:], pYT[:, 0:128], ALU.add)
                                yield

                        def scan_gen(d, blk, slot):
                            dh = d * 4 + h
                            halves = (0, 1) if d == 0 else (1, 0)
                            kg = ring[(d, 'kg')][slot]
                            qd = ring[(d, 'qd')][slot]
                            TTs = ring[(d, 'TT')][slot]
                            aqk = ring[(d, 'aqk')][slot]
                            kdec = ring[(d, 'kdec')][slot]
                            egc = ring[(d, 'egc')][slot]
                            bank = psums[6 + d]
                            ps1 = bank[:, 0:128]
                            ps2 = bank[:, 128:256]
                            ps3 = bank[:, 256:384]
                            ps4 = bank[:, 384:512]
                            P.mm(ps1, kg[:], S[d][:])
                            yield
                            X = Xz[d][0]
                            P.tt('dve', X[:], v_tok[:, blk, :], ps1, ALU.subtract)
                            yield
                            P.mm(ps2, TTs[:, :], X[:, :])
                            yield
                            vn = vnz[d][0]
                            P.ts('dve', vn[:], ps2, beta[:, blk, dh:dh + 1], None, ALU.mult)
                            v16 = vn16[d][0]
                            P.copy('act', v16[:], vn[:])
                            yield
                            P.mm(ps4, kdec[:, :], vn[:, :])
                            P.mm(ps3, S16[d][:], qd[:, :], start=True, stop=False)
                            P.mm(ps3, v16[:, :], aqk[:, :], start=False, stop=True)
                            yield
                            ccol = 127 if d == 0 else 0
                            P.stt('dve', S[d][:], S[d][:], egc[:, ccol:ccol + 1], ps4, ALU.mult, ALU.add)
                            P.copy('act', S16[d][:], S[d][:])
                            oc = slice(blk * 128, (blk + 1) * 128)
                            P.tt('dve', oT[:, oc], oT[:, oc], ps3, ALU.add)
                            yield

                        for d in range(2):
                            P.memset('pool', S[d][:], 0.0)
                            P.memset('pool', S16[d][:], 0.0)
                        LEAD = NSLOT - 1
                        NPREP = 2
                        preps = {d: [] for d in range(2)}
                        pdone = {d: set() for d in range(2)}
                        scans = {d: None for d in range(2)}
                        pi = {d: 0 for d in range(2)}
                        si = {d: 0 for d in range(2)}
                        active = True
                        while active:
                            active = False
                            rnd += 1
                            if bg is not None and rnd % 6 == 0:
                                try:
                                    next(bg)
                                except StopIteration:
                                    bg = None
                            for d in range(2):
                                if len(preps[d]) < NPREP and pi[d] < NBLK and pi[d] - si[d] < NSLOT:
                                    preps[d].append((pi[d], prep_gen(d, orders[d][pi[d]], pi[d] % NSLOT, d * NPREP + pi[d] % NPREP)))
                                    pi[d] += 1
                                for item in list(preps[d]):
                                    active = True
                                    try:
                                        next(item[1])
                                    except StopIteration:
                                        preps[d].remove(item)
                                        pdone[d].add(item[0])
                                if scans[d] is None and si[d] < NBLK and si[d] in pdone[d]:
                                    scans[d] = scan_gen(d, orders[d][si[d]], si[d] % NSLOT)
                                if scans[d] is not None:
                                    active = True
                                    try:
                                        next(scans[d])
                                    except StopIteration:
                                        scans[d] = None
                                        si[d] += 1
                                if si[d] < NBLK or pi[d] < NBLK:
                                    active = True
                        if l == 0 and s == 0 and h == 0:
                            dump('dn_oT', oT[:])
                        P.act(acc[:], oT[:], AF.Square)
                        for (off, n) in TTILES:
                            ps = psum()
                            P.mm(ps[:, 0:n], onesf[:], acc[:, off:off + n])
                            P.ts('dve', acc[:, off:off + n], ps[:, 0:n], 1.0 / 128, RMS_EPS, ALU.mult, ALU.add)
                        P.act(acc[:], acc[:], AF.Ln)
                        P.act(acc[:], acc[:], AF.Exp, scale=-0.5)
                        P.stt('dve', acc[:], oT[:], cng[:, 0:1], acc[:], ALU.mult, ALU.mult)
                        P.tt('pool', ycb[:], acc[:], czT[:, h, :], ALU.mult)
                        P.dma(yT_d[2, h * 128:(h + 1) * 128, :], ycb[:], queue='pool')
                    if bg is not None:
                        for _ in bg:
                            pass
                    barrier()
                if stop == 'C':
                    break

                with ExitStack() as st:
                    def sb(name, shape, dt):
                        return st.enter_context(nc.sbuf_tensor(uniq(name), list(shape), dt))
                    set_pool(range(8))
                    Wbr = sb("Wbr", [128, 12, D], BF16)
                    Wo = sb("Wo", [128, 8, D], BF16)
                    P.dma(Wbr[:], wbr_v[l])
                    P.dma(Wo[:], wout_v[l])
                    lng = sb("lng", [128, D], F32)
                    lnb = sb("lnb", [128, D], F32)
                    P.dma(lng[:], ln_g[l:l + 1, :].partition_broadcast(128))
                    P.dma(lnb[:], ln_b[l:l + 1, :].partition_broadcast(128))
                    gbc_ = [sb("gatebc%d" % i, [128, D], F32) for i in range(2)]
                    P.dma(gbc_[0][:], modrow_d[l, 2:3, 2 * D:3 * D].partition_broadcast(128))
                    P.dma(gbc_[1][:], modrow_d[l, s:s + 1, 2 * D:3 * D].partition_broadcast(128))
                    Wg = [sb("Wg%d" % i, [128, 8, 3, 128], BF16) for i in range(3)]
                    yt3 = [sb("yt3_%d" % i, [128, 3, 4, 512], BF16) for i in range(2)]
                    sgm = [sb("sgm%d" % i, [128, 512], BF16) for i in range(3)]
                    mT = sb("mT", [128, 8, 512], BF16)
                    macc = sb("macc", [128, 512], F32)
                    mtmp = [sb("mtmp%d" % i, [128, 512], F32) for i in range(2)]
                    xtl = [sb("xtl%d" % i, [128, D], F32) for i in range(2)]
                    tt_ = [sb("ttl%d" % i, [128, D], F32) for i in range(2)]
                    stat = [sb("stat%d" % i, [128, 8], F32) for i in range(2)]
                    junk = sb("junk", [128, D], F32)
                    yv = yT_d.rearrange("r (c p) t -> p r c t", p=128)
                    iw = 0
                    for ti, (off, n) in enumerate(TTILES):
                        y3 = yt3[ti % 2]
                        for r in range(3):
                            P.dma(y3[:, r, :, 0:n], yv[:, r, :, off:off + n])
                        for m in range(8):
                            wg = Wg[iw % 3]
                            iw += 1
                            P.dma(wg[:], wbf_g[l, m])
                            for r in range(3):
                                pg = psum()
                                for k in range(8):
                                    P.mm(pg[:, 0:n], wg[:, k, r, :], uT[:, k, off:off + n], start=(k == 0), stop=(k == 7))
                                pb_ = psum()
                                for k in range(4):
                                    P.mm(pb_[:, 0:n], Wbr[:, r * 4 + k, m * 128:(m + 1) * 128], y3[:, r, k, 0:n], start=(k == 0), stop=(k == 3))
                                sg_ = sgm[r]
                                P.act(sg_[:, 0:n], pg[:, 0:n], AF.Sigmoid)
                                if r == 0:
                                    P.tt('dve', macc[:, 0:n], pb_[:, 0:n], sg_[:, 0:n], ALU.mult)
                                else:
                                    mt_ = mtmp[r % 2]
                                    P.tt('dve', mt_[:, 0:n], pb_[:, 0:n], sg_[:, 0:n], ALU.mult)
                                    if r == 1:
                                        P.tt('pool', macc[:, 0:n], macc[:, 0:n], mt_[:, 0:n], ALU.add)
                                    else:
                                        P.tt('pool', mT[:, m, 0:n], macc[:, 0:n], mt_[:, 0:n], ALU.add)
                        for sbk in range(n // 128):
                            tau0 = off + sbk * 128
                            gb = gbc_[0] if tau0 < NCTX else gbc_[1]
                            x_ = xtl[sbk % 2]
                            t_ = tt_[sbk % 2]
                            st_ = stat[sbk % 2]
                            P.dma(x_[:], xres[s, tau0:tau0 + 128, :])
                            for hc in range(2):
                                po = psum()
                                for k in range(8):
                                    P.mm(po[:, :], mT[:, k, sbk * 128:(sbk + 1) * 128], Wo[:, k, hc * 512:(hc + 1) * 512], start=(k == 0), stop=(k == 7))
                                hs = slice(hc * 512, (hc + 1) * 512)
                                P.tt('dve', t_[:, hs], po[:, :], gb[:, hs], ALU.mult)
                                P.stt('dve', t_[:, hs], x_[:, hs], ALU_ALPHA, t_[:, hs], ALU.mult, ALU.add)
                            P.reduce('dve', st_[:, 0:1], t_[:], ALU.add)
                            P.act(junk[:], t_[:], AF.Square)
                            P.reduce('dve', st_[:, 1:2], junk[:], ALU.add)
                            P.ts('dve', st_[:, 2:3], st_[:, 0:1], 1.0 / D, None, ALU.mult)
                            P.tt('dve', st_[:, 3:4], st_[:, 2:3], st_[:, 2:3], ALU.mult)
                            P.stt('dve', st_[:, 4:5], st_[:, 1:2], 1.0 / D, st_[:, 3:4], ALU.mult, ALU.subtract)
                            P.ts('dve', st_[:, 5:6], st_[:, 4:5], LN_EPS, None, ALU.add)
                            P.act(st_[:, 5:6], st_[:, 5:6], AF.Ln)
                            P.act(st_[:, 5:6], st_[:, 5:6], AF.Exp, scale=-0.5)
                            P.ts('dve', t_[:], t_[:], st_[:, 2:3], st_[:, 5:6], ALU.subtract, ALU.mult)
                            P.tt('pool', t_[:], t_[:], lng[:], ALU.mult)
                            P.tt('dve', t_[:], t_[:], lnb[:], ALU.add)
                            if last:
                                if tau0 >= NCTX:
                                    P.dma(y_out[s, tau0 - NCTX:tau0 - NCTX + 128, :], t_[:], queue='pool')
                            else:
                                P.dma(xres[s, tau0:tau0 + 128, :], t_[:], queue='pool')
                    barrier()
            if stop is not None:
                break
        import os as _os
        if dbg and 'xres' in dbg_out and not _os.environ.get('NODUMP'):
            P.dma(dbg_out['xres'], xres[0])
        if dbg and 'yT' in dbg_out:
            nbr = {'A': 1, 'B': 2}.get(stop, 3)
            if stop != 'U':
                P.dma(dbg_out['yT'][0:nbr], yT_d[0:nbr])
        barrier()
        P.emit()
    return nc, P


ALU_ALPHA = ALPHA

_CACHE = {}


def kernel(**inputs):
    n = 8
    if 'nc' not in _CACHE:
        _CACHE['nc'] = build()[0]
    nc = _CACHE['nc']
    consts = host_consts()
    f = lambda a: np.ascontiguousarray(np.asarray(a, dtype=np.float32))
    shared = {
        'c_ctx': f(inputs['c_ctx']).reshape(1, D),
        'w_ada': f(inputs['w_ada']), 'b_ada': f(inputs['b_ada']), 'w_in': f(inputs['w_in']),
        'a_sink': f(inputs['a_sink']), 'b_conv_w': f(inputs['b_conv_w']), 'b_conv_b': f(inputs['b_conv_b']),
        'b_norm_g': f(inputs['b_norm_g']), 'b_norm_b': f(inputs['b_norm_b']), 'c_conv_w': f(inputs['c_conv_w']),
        'c_a_log': f(inputs['c_a_log']).reshape(4, 8), 'c_dt_bias': f(inputs['c_dt_bias']).reshape(4, 8),
        'c_norm_g': f(inputs['c_norm_g']), 'w_branch': f(inputs['w_branch']).reshape(4, 1536, D),
        'w_out': f(inputs['w_out']), 'ln_g': f(inputs['ln_g']), 'ln_b': f(inputs['ln_b']),
    }
    shared.update(consts)
    x = f(inputs['x'])
    ctx = f(inputs['ctx'])
    c = f(inputs['c'])
    in_maps = []
    for i in range(n):
        m = dict(shared)
        m['x'] = x[2 * i:2 * i + 2]
        m['ctx'] = ctx[2 * i:2 * i + 2]
        m['c'] = c[2 * i:2 * i + 2]
        in_maps.append(m)
    res = run_bass_kernel_spmd(nc, in_maps, core_ids=list(range(n)))
    return np.concatenate([r['y'] for r in res.results], axis=0).astype(np.float32)
```

```python
import numpy as np
import concourse.bass as bass
import concourse.mybir as mybir

F32 = mybir.dt.float32
BF16 = mybir.dt.bfloat16
AF = mybir.ActivationFunctionType
ALU = mybir.AluOpType
AX = mybir.AxisListType

_DT_SIZE = {F32: 4, BF16: 2}


def _dsize(dt):
    if dt in _DT_SIZE:
        return _DT_SIZE[dt]
    s = str(dt)
    if '32' in s:
        return 4
    if '16' in s:
        return 2
    if '8' in s:
        return 1
    if '64' in s:
        return 8
    raise ValueError(s)


def _prod(xs):
    r = 1
    for x in xs:
        r *= int(x)
    return r


def box(ap):
    t = ap.tensor
    name = t.name
    esz = _dsize(ap.dtype)
    off = int(ap.offset) * esz
    aps = ap.ap
    sp = str(ap.space)
    if sp == 'PSUM':
        return (name, 0, 128, 0, 2048)
    if sp in ('SB', 'PSUM'):
        pbytes = _prod(t.shape[1:]) * _dsize(t.dtype)
        plo = off // pbytes
        flo = off % pbytes
        pcnt = aps[0][1] if aps[0][0] != 0 else 1
        ext = sum((c - 1) * abs(s) for s, c in aps[1:]) + 1
        return (name, plo, plo + pcnt, flo, flo + ext * esz)
    ext = sum((c - 1) * abs(s) for s, c in aps) + 1
    return (name, 0, 1, off, off + ext * esz)


def _ovl(a, b):
    return a[1] < b[2] and b[1] < a[2] and a[3] < b[4] and b[3] < a[4]


def _covers(a, b):
    return a[1] <= b[1] and a[2] >= b[2] and a[3] <= b[3] and a[4] >= b[4]


ENGINES = ('pe', 'act', 'dve', 'pool', 'sp')
SEM_EPOCH = 30000
NDMA_SEMS = 12
SAME_ENGINE_DIST = 3


class Prog:
    def __init__(self, nc):
        self.nc = nc
        self.streams = {e: [] for e in ENGINES}
        self.count = {e: 0 for e in ENGINES}
        self.known = {e: {} for e in ENGINES}
        self.wr = {}
        self.rd = {}
        self.dma_val = {}
        self.dma_next = {e: 0 for e in ENGINES}
        self.n_waits = 0
        self.sem_handles = {}
        self.needed_sems = set()
        self.ninst = 0

    def _deps_for(self, reads, writes):
        deps = {}

        def add(key, val):
            if deps.get(key, -1) < val:
                deps[key] = val
        rboxes = [box(a) for a in reads]
        wboxes = [box(a) for a in writes]
        for b in rboxes:
            w = self.wr.get(b[0])
            if w:
                for ob, (k, v) in w.items():
                    if _ovl(b, ob):
                        add(k, v)
        for b in wboxes:
            w = self.wr.get(b[0])
            if w:
                for ob, (k, v) in w.items():
                    if _ovl(b, ob):
                        add(k, v)
            r = self.rd.get(b[0])
            if r:
                for (ob, k), v in r.items():
                    if _ovl(b, ob):
                        add(k, v)
        return deps, rboxes, wboxes

    def _commit(self, rboxes, wboxes, key, val):
        for b in wboxes:
            w = self.wr.setdefault(b[0], {})
            for ob in [ob for ob in w if _covers(b, ob)]:
                del w[ob]
            w[b] = (key, val)
            r = self.rd.get(b[0])
            if r:
                for rk in [rk for rk in r if _covers(b, rk[0])]:
                    del r[rk]
        for b in rboxes:
            self.rd.setdefault(b[0], {})[(b, key)] = val

    def _emit_waits(self, eng, deps, self_seq=None):
        kn = self.known[eng]
        for key, val in deps.items():
            if key == ('e', eng):
                if eng == 'pe':
                    continue
            if kn.get(key, 0) >= val:
                continue
            kn[key] = val
            self.n_waits += 1
            self.needed_sems.add(self._semname(key, val))
            self.streams[eng].append(('wait', key, val))

    def _semname(self, key, val):
        if key[0] == 'e':
            return ('e', key[1], (val - 1) // SEM_EPOCH)
        return key

    def op(self, eng, fn, reads=(), writes=()):
        deps, rb, wb = self._deps_for(reads, writes)
        seq = self.count[eng] + 1
        self._emit_waits(eng, deps, seq)
        self.count[eng] = seq
        self.ninst += 1
        self.needed_sems.add(('e', eng, (seq - 1) // SEM_EPOCH))
        self.streams[eng].append(('op', fn, seq))
        self._commit(rb, wb, ('e', eng), seq)

    def dma(self, out, in_, queue='sp', **kw):
        deps, rb, wb = self._deps_for([in_], [out])
        i = self.dma_next[queue]
        self.dma_next[queue] = (i + 1) % NDMA_SEMS
        key = ('d', queue, i)
        prev = self.dma_val.get(key, 0)
        if prev:
            deps[key] = max(deps.get(key, 0), prev)
        self._emit_waits(queue, deps)
        val = prev + 16
        self.dma_val[key] = val
        self.ninst += 1
        self.needed_sems.add(key)
        self.streams[queue].append(('dma', out, in_, key, kw))
        self._commit(rb, wb, key, val)
        return key, val

    def wait_all(self, eng):
        deps = {}
        for key, val in self.dma_val.items():
            deps[key] = val
        for e in ENGINES:
            if e != eng and self.count[e]:
                deps[('e', e)] = self.count[e]
        self._emit_waits(eng, deps)

    def emit(self):
        nc = self.nc
        from contextlib import ExitStack
        with ExitStack() as st:
            sems = {}
            for sn in sorted(self.needed_sems, key=str):
                sems[sn] = st.enter_context(nc.semaphore("s_" + "_".join(str(x) for x in sn)))
            block = st.enter_context(nc.Block())
            streams = self.streams

            def run(eng_name):
                def body(e):
                    for item in streams[eng_name]:
                        if item[0] == 'wait':
                            _, key, val = item
                            if key[0] == 'e':
                                ep = (val - 1) // SEM_EPOCH
                                e.wait_ge(sems[('e', key[1], ep)], val - ep * SEM_EPOCH)
                            else:
                                e.wait_ge(sems[key], val)
                        elif item[0] == 'op':
                            _, fn, seq = item
                            ins = fn(e)
                            ep = (seq - 1) // SEM_EPOCH
                            ins.then_inc(sems[('e', eng_name, ep)], 1)
                        else:
                            _, out, in_, key, kw = item
                            e.dma_start(out=out, in_=in_, **kw).then_inc(sems[key], 16)
                return body
            if streams['sp']:
                block.sync(run('sp'))
            if streams['pe']:
                block.tensor(run('pe'))
            if streams['act']:
                block.scalar(run('act'))
            if streams['dve']:
                block.vector(run('dve'))
            if streams['pool']:
                block.gpsimd(run('pool'))

    def mm(self, out, lhsT, rhs, start=True, stop=True):
        self.op('pe', lambda e: e.matmul(out, lhsT, rhs, start=start, stop=stop),
                reads=[lhsT, rhs] + ([] if start else [out]), writes=[out])

    def transpose(self, out, in_, ident):
        self.op('pe', lambda e: e.transpose(out, in_, ident), reads=[in_, ident], writes=[out])

    def act(self, out, in_, func, bias=None, scale=1.0, accum_out=None):
        reads = [in_]
        writes = [out]
        kw = {}
        if bias is not None:
            kw['bias'] = bias
            if not isinstance(bias, (int, float)):
                reads.append(bias)
        if not isinstance(scale, (int, float)):
            reads.append(scale)
        kw['scale'] = scale
        if accum_out is not None:
            kw['accum_out'] = accum_out
            writes.append(accum_out)
        self.op('act', lambda e: e.activation(out, in_, func, **kw), reads=reads, writes=writes)

    def tt(self, eng, out, in0, in1, op):
        self.op(eng, lambda e: e.tensor_tensor(out, in0, in1, op), reads=[in0, in1], writes=[out])

    def ts(self, eng, out, in0, s1, s2, op0, op1=None):
        reads = [in0] + [s for s in (s1, s2) if s is not None and not isinstance(s, (int, float))]
        if op1 is None:
            self.op(eng, lambda e: e.tensor_scalar(out, in0, s1, None, op0), reads=reads, writes=[out])
        else:
            self.op(eng, lambda e: e.tensor_scalar(out, in0, s1, s2, op0, op1), reads=reads, writes=[out])

    def stt(self, eng, out, in0, scalar, in1, op0, op1):
        reads = [in0, in1] + ([] if isinstance(scalar, (int, float)) else [scalar])
        self.op(eng, lambda e: e.scalar_tensor_tensor(out, in0, scalar, in1, op0, op1), reads=reads, writes=[out])

    def copy(self, eng, out, in_):
        if eng == 'act':
            self.op('act', lambda e: e.copy(out, in_), reads=[in_], writes=[out])
        else:
            self.op(eng, lambda e: e.tensor_copy(out, in_), reads=[in_], writes=[out])

    def memset(self, eng, ap, val):
        self.op(eng, lambda e: e.memset(ap, val), reads=[], writes=[ap])

    def recip(self, out, in_):
        self.op('dve', lambda e: e.reciprocal(out, in_), reads=[in_], writes=[out])

    def reduce(self, eng, out, in_, op, axis=None):
        axis = axis if axis is not None else AX.X
        self.op(eng, lambda e: e.tensor_reduce(out, in_, axis, op), reads=[in_], writes=[out])

import numpy as np
from concourse.bass_utils import run_bass_kernel_spmd

from contextlib import ExitStack
import math

D = 1024
T = 2304
NBLK = 18
NLAT = 2048
NCTX = 256
INW = 7952
OFF = {'a_q': 0, 'a_k': 512, 'a_v': 640, 'a_z': 768, 'b_a': 1280, 'b_b': 1792, 'b_z': 2304,
       'c_qkv': 2816, 'c_z': 4352, 'c_a': 4864, 'c_b': 4872, 'gate': 4880}
TTILES = [(0, 256)] + [(256 + 512 * i, 512) for i in range(4)]
ALPHA = 8 ** 0.25
LN_EPS = 1e-5
RMS_EPS = 1e-6
NSLOT = 4


def host_consts():
    c = {}
    c['identf'] = np.eye(128, dtype=np.float32)
    inv = 10000.0 ** (-np.arange(16, dtype=np.float32) / 16)
    pos = np.arange(NLAT)
    row = (pos // 64).astype(np.float32)
    col = (pos % 64).astype(np.float32)
    cosT = np.zeros((128, NLAT), np.float32)
    sinT = np.zeros((128, NLAT), np.float32)
    rmat = np.zeros((128, 128), np.float32)
    for p in range(128):
        d = p % 64
        axis = d // 32
        half = (d % 32) // 16
        pr = d % 16
        ang = (row if axis == 0 else col) * inv[pr]
        cosT[p] = np.cos(ang)
        sinT[p] = np.sin(ang) * (-1.0 if half == 0 else 1.0)
        partner = p + 16 if half == 0 else p - 16
        rmat[partner, p] = 1.0
    c['cosT'] = cosT
    c['sinT'] = sinT
    c['rmat'] = rmat
    k = np.arange(128)[:, None]
    q = np.arange(128)[None, :]
    bm = np.zeros((128, 3, 128), np.float32)
    bm[:, 0, :] = (k >= q)
    bm[:, 1, :] = 1.0
    bm[:, 2, :] = (k <= q)
    c['bandmask'] = bm.reshape(128, 384)
    j = np.arange(128)[:, None]
    i = np.arange(128)[None, :]
    same = np.ones((128, 128), bool)
    dn = np.zeros((7, 128, 128), np.float32)
    dn[0] = same & (j <= i)
    dn[1] = same & (j >= i)
    dn[2] = same
    dn[3] = same & (i > j)
    dn[4] = same & (i >= j)
    dn[5] = same & (i < j)
    dn[6] = same & (i <= j)
    c['dnmask'] = np.ascontiguousarray(dn.transpose(1, 0, 2)).reshape(128, 7 * 128)
    lv = np.zeros((128, 7, 128), np.float32)
    for li, sz in enumerate((1, 2, 4, 8, 16, 32, 64)):
        lv[:, li, :] = ((j // (2 * sz)) == (i // (2 * sz))) & ((j // sz) != (i // sz))
    c['lvlmask'] = lv.reshape(128, 7 * 128)
    return c


def build(n_layers=4, nseq=2, dbg=None, stop=None):
    nc = bass.Bass("TRN2", target_bir_lowering=False)
    P = Prog(nc)

    def din(name, shape, dt=F32):
        return nc.dram_tensor(name, list(shape), dt, kind="ExternalInput").ap()

    x_in = din("x", [2, NLAT, D])
    ctx_in = din("ctx", [2, NCTX, D])
    c_in = din("c", [2, D])
    cctx_in = din("c_ctx", [1, D])
    w_ada = din("w_ada", [4, D, 3 * D])
    b_ada = din("b_ada", [4, 3 * D])
    w_in = din("w_in", [4, D, INW])
    a_sink = din("a_sink", [4, 8])
    b_conv_w = din("b_conv_w", [4, 31, 512])
    b_conv_b = din("b_conv_b", [4, 512])
    b_norm_g = din("b_norm_g", [4, 512])
    b_norm_b = din("b_norm_b", [4, 512])
    c_conv_w = din("c_conv_w", [4, 3, 1536])
    c_a_log = din("c_a_log", [4, 8])
    c_dt_bias = din("c_dt_bias", [4, 8])
    c_norm_g = din("c_norm_g", [4, 128])
    w_branch = din("w_branch", [4, 1536, D])
    w_out = din("w_out", [4, D, D])
    ln_g = din("ln_g", [4, D])
    ln_b = din("ln_b", [4, D])
    k_identf = din("identf", [128, 128])
    k_cosT = din("cosT", [128, NLAT])
    k_sinT = din("sinT", [128, NLAT])
    k_rmat = din("rmat", [128, 128])
    k_bandmask = din("bandmask", [128, 384])
    k_dnmask = din("dnmask", [128, 7 * 128])
    k_lvl = din("lvlmask", [128, 7 * 128])
    y_out = nc.dram_tensor("y", [2, NLAT, D], F32, kind="ExternalOutput").ap()

    def dscr(name, shape, dt):
        return nc.dram_tensor(name, list(shape), dt, kind="Internal").ap()

    xres = dscr("xres", [2, T, D], F32)
    wbf_in = dscr("wbf_in", [4, D, INW], BF16)
    wbf_br = dscr("wbf_br", [4, 1536, D], BF16)
    wbf_out = dscr("wbf_out", [4, D, D], BF16)
    modrow_d = dscr("modrow", [4, 3, 3 * D], F32)
    yT_d = dscr("yT", [3, 512, T], BF16)
    wbf_g = dscr("wbf_g", [4, 8, 128, 8, 3, 128], BF16)

    _uq = {'n': 0}

    def uniq(name):
        _uq['n'] += 1
        return "%s_u%d" % (name, _uq['n'])

    dbg_out = {}
    if dbg:
        for name, (shape, dt) in dbg.items():
            dbg_out[name] = nc.dram_tensor("dbg_" + name, list(shape), dt, kind="ExternalOutput").ap()

    with ExitStack() as gst:
        def gsb(name, shape, dt):
            return gst.enter_context(nc.sbuf_tensor(uniq(name), list(shape), dt))

        psums = [gst.enter_context(nc.psum_tensor("ps%d" % i, [128, 512], F32)) for i in range(8)]
        ps_state = {'i': 0, 'pool': list(range(8))}

        def set_pool(pool):
            ps_state['pool'] = list(pool)
            ps_state['i'] = 0

        def psum():
            pool = ps_state['pool']
            t = psums[pool[ps_state['i'] % len(pool)]]
            ps_state['i'] += 1
            return t

        identf = gsb("identf", [128, 128], F32)
        identb = gsb("identb", [128, 128], BF16)
        onesf = gsb("onesf", [128, 128], F32)
        modT = gsb("modT", [128, 4, 24, 3], F32)
        uT = gsb("uT", [128, 8, T], BF16)

        P.dma(identf[:], k_identf)
        P.copy('dve', identb[:], identf[:])
        P.memset('dve', onesf[:], 1.0)

        P.marks = []

        def barrier():
            P.marks.append(dict(P.count))
            for e in ('sp', 'pe', 'act', 'dve', 'pool'):
                P.wait_all(e)

        rr = {'i': 0}

        def rot(engs=('dve', 'act', 'pool')):
            rr['i'] += 1
            return engs[rr['i'] % len(engs)]

        for s in range(nseq):
            P.dma(xres[s, 0:NCTX, :], ctx_in[s])
            P.dma(xres[s, NCTX:T, :], x_in[s])

        with ExitStack() as st:
            def sb(name, shape, dt):
                return st.enter_context(nc.sbuf_tensor(uniq(name), list(shape), dt))
            stg = [sb("stg%d" % i, [128, 2048], F32) for i in range(3)]
            o16 = [sb("o16_%d" % i, [128, 2048], BF16) for i in range(3)]
            engs = ['dve', 'act', 'pool']
            cnt = 0
            gstg = [sb("gstg%d" % i, [128, 3072], F32) for i in range(2)]
            g16 = [sb("g16_%d" % i, [128, 3072], BF16) for i in range(2)]

            def cast_rows(src, dst, ncols):
                nonlocal cnt
                c0 = 0
                while c0 < ncols:
                    n = min(2048, ncols - c0)
                    i = cnt % 3
                    cnt += 1
                    P.dma(stg[i][:, 0:n], src[:, c0:c0 + n])
                    P.copy(engs[i], o16[i][:, 0:n], stg[i][:, 0:n])
                    P.dma(dst[:, c0:c0 + n], o16[i][:, 0:n], queue='pool' if False else 'sp')
                    c0 += n
            for l in range(1):
                for k in range(8):
                    cast_rows(w_in[l, k * 128:(k + 1) * 128, :], wbf_in[l, k * 128:(k + 1) * 128, :], OFF['gate'])
                for k in range(8):
                    gi = k % 2
                    P.dma(gstg[gi][:], w_in[l, k * 128:(k + 1) * 128, OFF['gate']:INW])
                    P.copy('dve' if gi == 0 else 'act', g16[gi][:], gstg[gi][:])
                    for r in range(3):
                        P.dma(wbf_g[l].rearrange("m p k r c -> p k r m c")[:, k, r],
                              g16[gi][:, r * 1024:(r + 1) * 1024].rearrange("p (m c) -> p m c", m=8))
                for k in range(12):
                    cast_rows(w_branch[l, k * 128:(k + 1) * 128, :], wbf_br[l, k * 128:(k + 1) * 128, :], D)
                for k in range(8):
                    cast_rows(w_out[l, k * 128:(k + 1) * 128, :], wbf_out[l, k * 128:(k + 1) * 128, :], D)

            if stop == 'W':
                barrier()
                P.emit()
                return nc, P
            cT = sb("cT", [128, 8, 3], F32)
            scT = sb("scT", [128, 8, 3], F32)
            crow = sb("crow", [3, D], F32)
            P.dma(crow[0:2, :], c_in)
            P.dma(crow[2:3, :], cctx_in)
            pt = psum()
            for k in range(8):
                P.transpose(pt[:, k * 3:(k + 1) * 3], crow[0:3, k * 128:(k + 1) * 128], identf[0:3, 0:3])
            P.copy('dve', cT[:].rearrange("p k r -> p (k r)"), pt[:, 0:24])
            P.act(scT[:], cT[:], AF.Silu)
            wa = [sb("wa%d" % i, [128, 3 * D], F32) for i in range(2)]
            bada = sb("bada", [3, 3 * D], F32)
            mrow = sb("mrow", [3, 3 * D], F32)
            for l in range(n_layers):
                P.dma(bada[:], b_ada[l:l + 1, :].partition_broadcast(3))
                banks = [psums[i] for i in range(6)]
                for k in range(8):
                    w = wa[k % 2]
                    P.dma(w[:], w_ada[l, k * 128:(k + 1) * 128, :])
                    for j in range(6):
                        P.mm(banks[j][0:3, :], scT[:, k, :], w[:, j * 512:(j + 1) * 512], start=(k == 0), stop=(k == 7))
                for j in range(6):
                    P.tt('dve', mrow[:, j * 512:(j + 1) * 512], banks[j][0:3, :], bada[:, j * 512:(j + 1) * 512], ALU.add)
                P.dma(modrow_d[l], mrow[:])
                pt = psum()
                for j in range(24):
                    P.transpose(pt[:, j * 3:(j + 1) * 3], mrow[0:3, j * 128:(j + 1) * 128], identf[0:3, 0:3])
                P.copy('dve', modT[:, l].rearrange("p j r -> p (j r)"), pt[:, 0:72])
                P.ts('dve', modT[:, l, 8:16, :], modT[:, l, 8:16, :], 1.0, None, ALU.add)
            barrier()

        if stop == '0':
            P.emit()
            return nc, P
        wq_v = [wbf_in[l].rearrange("(k p) c -> p k c", p=128) for l in range(4)]
        wbr_v = [wbf_br[l].rearrange("(k p) c -> p k c", p=128) for l in range(4)]
        wout_v = [wbf_out[l].rearrange("(k p) c -> p k c", p=128) for l in range(4)]

        def load_T(st_sb, dst, src, n, tag):
            tmp = st_sb("ltT_" + tag, [n, 128], F32)
            P.dma(tmp[:], src)
            pt = psum()
            P.transpose(pt[:, 0:n], tmp[0:n, :], identf[0:n, 0:n])
            P.copy('dve', dst, pt[:, 0:n])

        def bgcast_gen(lw, stg_, o16_):
            cntb = 0

            def piece(src, dst):
                nonlocal cntb
                i = cntb % 2
                cntb += 1
                n = src.shape[-1]
                P.dma(stg_[i][:, 0:n], src)
                P.copy('pool', o16_[i][:, 0:n], stg_[i][:, 0:n])
                P.dma(dst, o16_[i][:, 0:n] if len(dst.shape) == 2 else o16_[i][:, 0:n].rearrange("p (m c) -> p m c", m=8), queue='pool')
            for k in range(8):
                rs = slice(k * 128, (k + 1) * 128)
                c0 = 0
                while c0 < OFF['gate']:
                    n = min(1024, OFF['gate'] - c0)
                    piece(w_in[lw, rs, c0:c0 + n], wbf_in[lw, rs, c0:c0 + n])
                    c0 += n
                    yield
                for r in range(3):
                    g0 = OFF['gate'] + r * 1024
                    piece(w_in[lw, rs, g0:g0 + 1024], wbf_g[lw].rearrange("m p k r c -> p k r m c")[:, k, r])
                    yield
            for k in range(12):
                rs = slice(k * 128, (k + 1) * 128)
                piece(w_branch[lw, rs, :], wbf_br[lw, rs, :])
                yield
            for k in range(8):
                rs = slice(k * 128, (k + 1) * 128)
                piece(w_out[lw, rs, :], wbf_out[lw, rs, :])
                yield

        def dump(name, src):
            import os as _os
            if _os.environ.get('NODUMP'):
                return
            if name in dbg_out:
                P.dma(dbg_out[name], src)

        for l in range(n_layers):
            for s in range(nseq):
                last = (l == 3)
                with ExitStack() as st:
                    def sb(name, shape, dt):
                        return st.enter_context(nc.sbuf_tensor(uniq(name), list(shape), dt))
                    set_pool(range(8))
                    xt = [sb("xt%d" % i, [128, D], F32) for i in range(3)]
                    for tb in range(NBLK):
                        r = 2 if tb < 2 else s
                        x_ = xt[tb % 3]
                        P.dma(x_[:], xres[s, tb * 128:(tb + 1) * 128, :])
                        for half in range(2):
                            pt = psum()
                            for kk in range(4):
                                k = half * 4 + kk
                                P.transpose(pt[:, kk * 128:(kk + 1) * 128], x_[:, k * 128:(k + 1) * 128], identf[:])
                            for kk in range(4):
                                k = half * 4 + kk
                                e = 'dve' if half == 0 else 'act'
                                dst = uT[:, k, tb * 128:(tb + 1) * 128]
                                src = pt[:, kk * 128:(kk + 1) * 128]
                                if e == 'dve':
                                    P.ts('dve', dst, src, modT[:, l, 8 + k, r:r + 1], modT[:, l, k, r:r + 1], ALU.mult, ALU.add)
                                else:
                                    P.act(dst, src, AF.Identity, bias=modT[:, l, k, r:r + 1], scale=modT[:, l, 8 + k, r:r + 1])
                    barrier()
                if l == 0 and s == 0:
                    dump('uT', uT[:])
                if stop == 'U':
                    break

                with ExitStack() as st:
                    def sb(name, shape, dt):
                        return st.enter_context(nc.sbuf_tensor(uniq(name), list(shape), dt))
                    set_pool(range(4))
                    cosT = sb("cosT", [128, NLAT], F32)
                    sinT = sb("sinT", [128, NLAT], F32)
                    rmat = sb("rmat", [128, 128], F32)
                    bmf = sb("bmf", [128, 384], F32)
                    bmask = sb("bmask", [128, 384], BF16)
                    P.dma(cosT[:], k_cosT)
                    P.dma(sinT[:], k_sinT)
                    P.dma(rmat[:], k_rmat)
                    P.dma(bmf[:], k_bandmask)
                    P.copy('pool', bmask[:], bmf[:])
                    Wq = sb("Wq", [128, 8, 512], BF16)
                    Wk = sb("Wk", [128, 8, 128], BF16)
                    Wv = sb("Wv", [128, 8, 128], BF16)
                    Wz = sb("Wz", [128, 8, 512], BF16)
                    for t in range(4):
                        P.dma(Wq[:, :, t * 128:t * 128 + 64], wq_v[l][:, :, t * 64:(t + 1) * 64])
                        P.dma(Wq[:, :, t * 128 + 64:(t + 1) * 128], wq_v[l][:, :, (4 + t) * 64:(5 + t) * 64])
                    P.dma(Wk[:], wq_v[l][:, :, OFF['a_k']:OFF['a_k'] + 128])
                    P.dma(Wv[:], wq_v[l][:, :, OFF['a_v']:OFF['a_v'] + 128])
                    P.dma(Wz[:], wq_v[l][:, :, OFF['a_z']:OFF['a_z'] + 512])
                    qT = sb("qT", [128, 4, T], BF16)
                    kT = sb("kT", [128, T], BF16)
                    kTz = [sb("kTz%d" % i, [128, T], BF16) for i in range(2)]
                    P.memset('pool', kTz[0][64:128, :], 0.0)
                    P.memset('pool', kTz[1][0:64, :], 0.0)
                    Vaug = sb("Vaug", [128, NBLK, 2, 66], BF16)
                    esink = sb("esink", [128, 8], F32)
                    P.dma(esink[:], a_sink[l:l + 1, :].partition_broadcast(128))
                    P.act(esink[:], esink[:], AF.Exp)
                    P.memset('pool', Vaug[:, :, :, 64:66], 1.0)
                    qraw = [sb("qraw%d" % i, [128, 512], F32) for i in range(2)]
                    t1s = [sb("t1s%d" % i, [128, 512], F32) for i in range(2)]
                    t2s = [sb("t2s%d" % i, [128, 512], F32) for i in range(2)]
                    it = 0
                    for (off, n) in TTILES:
                        for t in range(5):
                            ps = psum()
                            for k in range(8):
                                lhsT = Wq[:, k, t * 128:(t + 1) * 128] if t < 4 else Wk[:, k, :]
                                P.mm(ps[:, 0:n], lhsT, uT[:, k, off:off + n], start=(k == 0), stop=(k == 7))
                            dst = qT[:, t, off:off + n] if t < 4 else kT[:, off:off + n]
                            if off < NCTX:
                                P.copy('act', dst, ps[:, 0:n])
                                if t == 4:
                                    P.copy('act', kTz[0][0:64, off:off + n], ps[0:64, 0:n])
                                    P.copy('act', kTz[1][64:128, off:off + n], ps[64:128, 0:n])
                            else:
                                pos = off - NCTX
                                qr = qraw[it % 2]
                                a1 = t1s[it % 2]
                                a2 = t2s[it % 2]
                                it += 1
                                P.copy('act', qr[:, 0:n], ps[:, 0:n])
                                ps2 = psum()
                                P.mm(ps2[:, 0:n], rmat[:], qr[:, 0:n])
                                P.tt('pool', a1[:, 0:n], qr[:, 0:n], cosT[:, pos:pos + n], ALU.mult)
                                P.tt('dve', a2[:, 0:n], ps2[:, 0:n], sinT[:, pos:pos + n], ALU.mult)
                                P.tt('dve', dst, a1[:, 0:n], a2[:, 0:n], ALU.add)
                                if t == 4:
                                    P.tt('dve', kTz[0][0:64, off:off + n], a1[0:64, 0:n], a2[0:64, 0:n], ALU.add)
                                    P.tt('dve', kTz[1][64:128, off:off + n], a1[64:128, 0:n], a2[64:128, 0:n], ALU.add)
                    if stop == 'A1':
                        barrier()
                        P.emit()
                        return nc, P
                    for tb in range(NBLK):
                        ps = psum()
                        for k in range(8):
                            P.mm(ps[:, 0:128], uT[:, k, tb * 128:(tb + 1) * 128], Wv[:, k, :], start=(k == 0), stop=(k == 7))
                        P.copy(rot(('dve', 'act')), Vaug[:, tb, :, 0:64], ps[:, 0:128].rearrange("p (h d) -> p h d", h=2))
                    if l == 0 and s == 0:
                        dump('qT', qT[:])
                        dump('kT', kT[:])
                    if stop == 'A2':
                        barrier()
                        P.emit()
                        return nc, P
                    pTs = [sb("pT%d" % i, [128, 2, 5, 128], BF16) for i in range(2)]
                    az = [sb("az%d" % i, [128, 512], BF16) for i in range(2)]
                    den = [sb("den%d" % i, [128, 8], F32) for i in range(2)]
                    yaf = [sb("yaf%d" % i, [128, 512], F32) for i in range(2)]
                    yab = [sb("yab%d" % i, [128, 512], BF16) for i in range(2)]
                    yTt = [sb("yTt%d" % i, [128, 4, 128], BF16) for i in range(2)]
                    ya_dst = yT_d[0].rearrange("(c p) t -> p c t", p=128)
                    ip = 0
                    for n in range(NBLK):
                        q0 = n * 128
                        if n < 2:
                            slots = []
                        else:
                            slots = [sl for sl in range(3) if 2 <= n - 1 + sl <= 17]
                        ctxt = [0, 1]
                        ops = [psums[4 + 2 * (n % 2)], psums[5 + 2 * (n % 2)]]
                        for t in range(4):
                            pT = pTs[ip % 2]
                            ip += 1
                            if slots:
                                for hh in range(2):
                                    psb = psum()
                                    pr = slice(hh * 64, (hh + 1) * 64)
                                    for sl in slots:
                                        kb = n - 1 + sl
                                        P.mm(psb[:, sl * 128:(sl + 1) * 128], kTz[hh][:, kb * 128:(kb + 1) * 128], qT[:, t, q0:q0 + 128])
                                    s0, s1 = slots[0], slots[-1] + 1
                                    P.act(pT[:, hh, s0:s1, :], psb[:, s0 * 128:s1 * 128].rearrange("p (a b) -> p a b", b=128), AF.Exp, scale=0.125)
                                    P.tt('dve', pT[:, hh, s0:s1, :], pT[:, hh, s0:s1, :], bmask[:, s0 * 128:s1 * 128].rearrange("p (a b) -> p a b", b=128), ALU.mult)
                            psc = psum()
                            for hh in range(2):
                                pr = slice(hh * 64, (hh + 1) * 64)
                                for ci in ctxt:
                                    P.mm(psc[:, (hh * 2 + ci) * 128:(hh * 2 + ci + 1) * 128], kTz[hh][:, ci * 128:(ci + 1) * 128], qT[:, t, q0:q0 + 128])
                            P.act(pT[:, :, 3:5, :], psc[:, :].rearrange("p (h c b) -> p h c b", h=2, c=2), AF.Exp, scale=0.125)
                            import os as _os
                            CORE = int(_os.environ.get('CORE', '9'))
                            for hh in range(2 if CORE >= 2 else 0):
                                tiles = [(sl, n - 1 + sl) for sl in slots] + [(3 + ci, ci) for ci in ctxt]
                                for ii, (sl, kb) in enumerate(tiles):
                                    P.mm(ops[hh][:, t * 65:(t + 1) * 65], pT[:, hh, sl, :], Vaug[:, kb, hh, 0:65],
                                         start=(ii == 0), stop=(ii == len(tiles) - 1))
                        if CORE < 3:
                            continue
                        psz = psum()
                        for k in range(8):
                            P.mm(psz[:, :], uT[:, k, q0:q0 + 128], Wz[:, k, :], start=(k == 0), stop=(k == 7))
                        az_ = az[n % 2]
                        P.act(az_[:], psz[:, :], AF.Silu)
                        dn_ = den[n % 2]
                        yf = yaf[n % 2]
                        yb = yab[n % 2]
                        for hh in range(2):
                            ov = ops[hh][:, 0:260].rearrange("p (t e) -> p t e", e=65)
                            P.tt('dve', dn_[:, hh * 4:(hh + 1) * 4], ov[:, :, 64], esink[:, hh * 4:(hh + 1) * 4], ALU.add)
                        P.recip(dn_[:], dn_[:])
                        for hh in range(2):
                            ov = ops[hh][:, 0:260].rearrange("p (t e) -> p t e", e=65)
                            P.tt('dve', yf[:, hh * 256:(hh + 1) * 256].rearrange("p (t d) -> p t d", d=64), ov[:, :, 0:64],
                                 dn_[:, hh * 4:(hh + 1) * 4].unsqueeze(2).to_broadcast([128, 4, 64]), ALU.mult)
                        P.tt('pool', yb[:], yf[:], az_[:], ALU.mult)
                        if CORE < 4:
                            continue
                        ptb = psum()[:].bitcast(BF16)
                        for c in range(4):
                            P.transpose(ptb[:, c * 128:(c + 1) * 128], yb[:, c * 128:(c + 1) * 128], identb[:])
                        yt = yTt[n % 2]
                        P.copy('act', yt[:].rearrange("p c t -> p (c t)"), ptb[:, 0:512])
                        P.dma(ya_dst[:, :, q0:q0 + 128], yt[:], queue='pool')
                    barrier()
                if stop == 'A':
                    break

                with ExitStack() as st:
                    def sb(name, shape, dt):
                        return st.enter_context(nc.sbuf_tensor(uniq(name), list(shape), dt))
                    set_pool(range(8))
                    HW = 2364
                    hpad = sb("hpad", [128, 4, HW], BF16)
                    zT = sb("zT", [128, 4, T], BF16)
                    Wa = sb("Wa", [128, 8, 512], BF16)
                    Wb = sb("Wb", [128, 8, 512], BF16)
                    Wzb = sb("Wzb", [128, 8, 512], BF16)
                    P.dma(Wa[:], wq_v[l][:, :, OFF['b_a']:OFF['b_a'] + 512])
                    P.dma(Wb[:], wq_v[l][:, :, OFF['b_b']:OFF['b_b'] + 512])
                    P.dma(Wzb[:], wq_v[l][:, :, OFF['b_z']:OFF['b_z'] + 512])
                    cw = sb("cw", [128, 4, 31], F32)
                    cb = sb("cb", [128, 4], F32)
                    ng = sb("ng", [128, 4], F32)
                    nb_ = sb("nb", [128, 4], F32)
                    for c in range(4):
                        load_T(sb, cw[:, c, :], b_conv_w[l, :, c * 128:(c + 1) * 128], 31, "cw%d" % c)
                    load_T(sb, cb[:], b_conv_b[l].rearrange("(c p) -> c p", p=128), 4, "cb")
                    load_T(sb, ng[:], b_norm_g[l].rearrange("(c p) -> c p", p=128), 4, "ng")
                    load_T(sb, nb_[:], b_norm_b[l].rearrange("(c p) -> c p", p=128), 4, "nb")
                    P.memset('pool', hpad[:, :, 0:15], 0.0)
                    P.memset('pool', hpad[:, :, 271:301], 0.0)
                    P.memset('pool', hpad[:, :, 2349:2364], 0.0)

                    def hcol(tau):
                        return tau + 15 if tau < NCTX else tau + 45
                    sg = [sb("sg%d" % i, [128, 512], BF16) for i in range(2)]
                    i2 = 0
                    for (off, n) in TTILES:
                        for c in range(4):
                            psa = psum()
                            psb = psum()
                            psz = psum()
                            for k in range(8):
                                P.mm(psa[:, 0:n], Wa[:, k, c * 128:(c + 1) * 128], uT[:, k, off:off + n], start=(k == 0), stop=(k == 7))
                            for k in range(8):
                                P.mm(psb[:, 0:n], Wb[:, k, c * 128:(c + 1) * 128], uT[:, k, off:off + n], start=(k == 0), stop=(k == 7))
                            for k in range(8):
                                P.mm(psz[:, 0:n], Wzb[:, k, c * 128:(c + 1) * 128], uT[:, k, off:off + n], start=(k == 0), stop=(k == 7))
                            g_ = sg[i2 % 2]
                            i2 += 1
                            P.act(g_[:, 0:n], psb[:, 0:n], AF.Sigmoid)
                            P.tt('dve', hpad[:, c, hcol(off):hcol(off) + n], psa[:, 0:n], g_[:, 0:n], ALU.mult)
                            P.act(zT[:, c, off:off + n], psz[:, 0:n], AF.Silu)
                    accA = sb("accA", [128, 4, 512], F32)
                    sq = sb("sq", [128, 4, 512], F32)
                    mean = sb("mean", [128, 512], F32)
                    msq = sb("msq", [128, 512], F32)
                    rstd = sb("rstd", [128, 512], F32)
                    xc = [sb("xc%d" % i, [128, 512], F32) for i in range(2)]
                    sa = [sb("sa%d" % i, [128, 512], F32) for i in range(2)]
                    ybt = [sb("ybt%d" % i, [128, 4, 512], BF16) for i in range(2)]
                    yb_dst = yT_d[1].rearrange("(c p) t -> p c t", p=128)
                    dg = sb("dg", [128, 4, 31, 128], BF16)
                    for c in range(4):
                        for k in range(31):
                            if (c * 31 + k) % 2 == 0:
                                P.ts('dve', dg[:, c, k, :], identf[:], cw[:, c, k:k + 1], None, ALU.mult)
                            else:
                                P.act(dg[:, c, k, :], identf[:], AF.Identity, scale=cw[:, c, k:k + 1])
                    for ti, (off, n) in enumerate(TTILES):
                        h0 = hcol(off) - 15
                        for c in range(4):
                            pcv = psum()
                            for k in range(31):
                                P.mm(pcv[:, 0:n], dg[:, c, k, :], hpad[:, c, h0 + k:h0 + k + n], start=(k == 0), stop=(k == 30))
                            P.act(accA[:, c, 0:n], pcv[:, 0:n], AF.Identity, bias=cb[:, c:c + 1])
                            P.act(sq[:, c, 0:n], pcv[:, 0:n], AF.Square, bias=cb[:, c:c + 1])
                        p1 = psum()
                        p2 = psum()
                        for c in range(4):
                            P.mm(p1[:, 0:n], onesf[:], accA[:, c, 0:n], start=(c == 0), stop=(c == 3))
                        for c in range(4):
                            P.mm(p2[:, 0:n], onesf[:], sq[:, c, 0:n], start=(c == 0), stop=(c == 3))
                        P.ts('dve', mean[:, 0:n], p1[:, 0:n], 1.0 / 512, None, ALU.mult)
                        P.tt('dve', msq[:, 0:n], mean[:, 0:n], mean[:, 0:n], ALU.mult)
                        P.stt('dve', rstd[:, 0:n], p2[:, 0:n], 1.0 / 512, msq[:, 0:n], ALU.mult, ALU.subtract)
                        P.ts('dve', rstd[:, 0:n], rstd[:, 0:n], LN_EPS, None, ALU.add)
                        P.act(rstd[:, 0:n], rstd[:, 0:n], AF.Ln)
                        P.act(rstd[:, 0:n], rstd[:, 0:n], AF.Exp, scale=-0.5)
                        yt = ybt[ti % 2]
                        for c in range(4):
                            x_ = xc[c % 2]
                            a_ = sa[c % 2]
                            P.tt('pool', x_[:, 0:n], accA[:, c, 0:n], mean[:, 0:n], ALU.subtract)
                            P.tt('dve', x_[:, 0:n], x_[:, 0:n], rstd[:, 0:n], ALU.mult)
                            P.act(a_[:, 0:n], x_[:, 0:n], AF.Silu, bias=nb_[:, c:c + 1], scale=ng[:, c:c + 1])
                            P.tt('pool', yt[:, c, 0:n], a_[:, 0:n], zT[:, c, off:off + n], ALU.mult)
                        P.dma(yb_dst[:, :, off:off + n], yt[:, :, 0:n], queue='pool')
                    barrier()
                if stop == 'B':
                    break

                with ExitStack() as st:
                    def sb(name, shape, dt):
                        return st.enter_context(nc.sbuf_tensor(uniq(name), list(shape), dt))
                    set_pool(range(6))
                    dnm = sb("dnm", [128, 7, 128], F32)
                    P.dma(dnm[:].rearrange("p a b -> p (a b)"), k_dnmask)
                    lvl = sb("lvl", [128, 7, 128], F32)
                    P.dma(lvl[:].rearrange("p a b -> p (a b)"), k_lvl)
                    mcum = [dnm[:, 0, :], dnm[:, 1, :]]
                    mtot = dnm[:, 2, :]
                    mstr = [dnm[:, 3, :], dnm[:, 5, :]]
                    minc = [dnm[:, 4, :], dnm[:, 6, :]]
                    czT = sb("czT", [128, 4, T], BF16)
                    abt = sb("abt", [128, NBLK, 16], F32)
                    with ExitStack() as st2:
                        Wcz = st2.enter_context(nc.sbuf_tensor(uniq("Wcz"), [128, 8, 512], BF16))
                        Wab = st2.enter_context(nc.sbuf_tensor(uniq("Wab"), [128, 8, 16], BF16))
                        P.dma(Wcz[:], wq_v[l][:, :, OFF['c_z']:OFF['c_z'] + 512])
                        P.dma(Wab[:], wq_v[l][:, :, OFF['c_a']:OFF['c_a'] + 16])
                        for (off, n) in TTILES:
                            for c in range(4):
                                ps = psum()
                                for k in range(8):
                                    P.mm(ps[:, 0:n], Wcz[:, k, c * 128:(c + 1) * 128], uT[:, k, off:off + n], start=(k == 0), stop=(k == 7))
                                P.act(czT[:, c, off:off + n], ps[:, 0:n], AF.Silu)
                        for tb in range(NBLK):
                            ps = psum()
                            for k in range(8):
                                P.mm(ps[:, 0:16], uT[:, k, tb * 128:(tb + 1) * 128], Wab[:, k, :], start=(k == 0), stop=(k == 7))
                            P.copy('dve', abt[:, tb, :], ps[:, 0:16])
                        barrier()
                    dtb = sb("dtb", [128, 8], F32)
                    negA = sb("negA", [128, 8], F32)
                    P.dma(dtb[:], c_dt_bias[l:l + 1, :].partition_broadcast(128))
                    P.dma(negA[:], c_a_log[l:l + 1, :].partition_broadcast(128))
                    P.act(negA[:], negA[:], AF.Exp)
                    P.ts('dve', negA[:], negA[:], -1.0, None, ALU.mult)
                    g_tok = sb("g_tok", [128, NBLK, 8], F32)
                    beta = sb("beta", [128, NBLK, 8], F32)
                    nbeta = sb("nbeta", [128, NBLK, 8], F32)
                    gc_tok = sb("gc_tok", [128, NBLK, 8], F32)
                    gl_tok = sb("gl_tok", [128, NBLK, 8], F32)
                    edk = sb("edk", [128, NBLK, 8], F32)
                    P.tt('dve', g_tok[:], abt[:, :, 0:8], dtb[:].unsqueeze(1).to_broadcast([128, NBLK, 8]), ALU.add)
                    P.act(g_tok[:], g_tok[:], AF.Exp)
                    P.ts('dve', g_tok[:], g_tok[:], 1.0, None, ALU.add)
                    P.act(g_tok[:], g_tok[:], AF.Ln)
                    P.tt('dve', g_tok[:], g_tok[:], negA[:].unsqueeze(1).to_broadcast([128, NBLK, 8]), ALU.mult)
                    P.act(beta[:], abt[:, :, 8:16], AF.Sigmoid)
                    P.ts('dve', nbeta[:], beta[:], -1.0, None, ALU.mult)
                    for d in range(2):
                        ps = psum()
                        P.mm(ps[:, 0:72].rearrange("p (b h) -> p b h", h=4), mcum[d], g_tok[:, :, d * 4:(d + 1) * 4])
                        P.copy('dve', gc_tok[:, :, d * 4:(d + 1) * 4], ps[:, 0:72].rearrange("p (b h) -> p b h", h=4))
                        ps = psum()
                        P.mm(ps[:, 0:72].rearrange("p (b h) -> p b h", h=4), mtot, g_tok[:, :, d * 4:(d + 1) * 4])
                        P.copy('dve', gl_tok[:, :, d * 4:(d + 1) * 4], ps[:, 0:72].rearrange("p (b h) -> p b h", h=4))
                    P.tt('dve', edk[:], gl_tok[:], gc_tok[:], ALU.subtract)
                    P.act(edk[:], edk[:], AF.Exp)
                    if l == 0 and s == 0:
                        dump('g_tok', g_tok[:])
                        dump('gc_tok', gc_tok[:])

                    ccw = sb("ccw", [128, 12, 3], F32)
                    for c in range(12):
                        load_T(sb, ccw[:, c, :], c_conv_w[l, :, c * 128:(c + 1) * 128], 3, "ccw%d" % c)
                    cng = sb("cng", [128, 1], F32)
                    P.dma(cng[:], c_norm_g[l].rearrange("(p o) -> p o", o=1))
                    RW = T + 4
                    raw = sb("raw", [128, RW], F32)
                    acc = sb("acc", [128, T], F32)
                    kTf = sb("kTf", [128, T], F32)
                    k_tok = sb("k_tok", [128, NBLK, 128], F32)
                    v_tok = sb("v_tok", [128, NBLK, 128], F32)
                    oT = sb("oT", [128, T], F32)
                    qTb = sb("qTb", [128, T], BF16)
                    kTb = sb("kTb", [128, T], BF16)
                    Wh = [sb("Wh%d" % i, [128, 8, 384], BF16) for i in range(2)]
                    P.memset('pool', raw[:, 0:1], 0.0)
                    P.memset('pool', raw[:, 257:259], 0.0)
                    P.memset('pool', raw[:, RW - 1:RW], 0.0)

                    def rcol(tau):
                        return tau + 1 if tau < NCTX else tau + 3
                    ring = {}
                    for d in range(2):
                        for nm in ('TT', 'aqk', 'kg', 'qd', 'kdec', 'egc'):
                            ring[(d, nm)] = [sb("rg_%s%d_%d" % (nm, d, i), [128, 128], BF16 if nm in ('aqk', 'qd') else F32) for i in range(NSLOT)]
                    tmpn = {}

                    def tmp(nm, i=0, dt=F32):
                        key = (nm, i)
                        if key not in tmpn:
                            tmpn[key] = sb("tp_%s" % nm, [128, 128], dt)
                        return tmpn[key]
                    S = [sb("S%d" % d, [128, 128], F32) for d in range(2)]
                    Xz = [[sb("Xz%d_%d" % (d, i), [128, 128], F32) for i in range(1)] for d in range(2)]
                    vnz = [[sb("vnz%d_%d" % (d, i), [128, 128], F32) for i in range(1)] for d in range(2)]
                    S16 = [sb("S16_%d" % d, [128, 128], BF16) for d in range(2)]
                    vn16 = [[sb("vn16_%d_%d" % (d, i), [128, 128], BF16) for i in range(1)] for d in range(2)]
                    for d in range(2):
                        for i in range(1):
                            P.memset('pool', Xz[d][i][:], 0.0)
                            P.memset('pool', vnz[d][i][:], 0.0)
                            P.memset('pool', vn16[d][i][:], 0.0)
                    ycb = sb("ycb", [128, T], BF16)

                    bg = None
                    if s == 0 and l + 1 < n_layers:
                        bstg = [sb("bstg%d" % i, [128, 1024], F32) for i in range(2)]
                        bo16 = [sb("bo16_%d" % i, [128, 1024], BF16) for i in range(2)]
                        bg = bgcast_gen(l + 1, bstg, bo16)
                    rnd = 0
                    for h in range(4):
                        W_ = Wh[h % 2]
                        for j in range(3):
                            c0 = OFF['c_qkv'] + (j * 4 + h) * 128
                            P.dma(W_[:, :, j * 128:(j + 1) * 128], wq_v[l][:, :, c0:c0 + 128])
                        for j in range(3):
                            ct = j * 4 + h
                            for (off, n) in TTILES:
                                ps = psum()
                                for k in range(8):
                                    P.mm(ps[:, 0:n], W_[:, k, j * 128:(j + 1) * 128], uT[:, k, off:off + n], start=(k == 0), stop=(k == 7))
                                P.copy(rot(('act', 'dve')), raw[:, rcol(off):rcol(off) + n], ps[:, 0:n])
                            for (off, n) in ((0, NCTX), (NCTX, 1024), (NCTX + 1024, 1024)):
                                r0 = rcol(off)
                                e = 'dve'
                                P.ts(e, acc[:, off:off + n], raw[:, r0 - 1:r0 - 1 + n], ccw[:, ct, 0:1], None, ALU.mult)
                                P.stt(e, acc[:, off:off + n], raw[:, r0:r0 + n], ccw[:, ct, 1:2], acc[:, off:off + n], ALU.mult, ALU.add)
                                P.stt(e, acc[:, off:off + n], raw[:, r0 + 1:r0 + 1 + n], ccw[:, ct, 2:3], acc[:, off:off + n], ALU.mult, ALU.add)
                            P.act(acc[:], acc[:], AF.Silu)
                            if j < 2:
                                dstT = qTb if j == 0 else kTf
                                P.act(raw[:, 0:T], acc[:], AF.Square)
                                for (off, n) in TTILES:
                                    ps = psum()
                                    P.mm(ps[:, 0:n], onesf[:], raw[:, off:off + n])
                                    P.ts('dve', raw[:, off:off + n], ps[:, 0:n], RMS_EPS, None, ALU.add)
                                    P.act(raw[:, off:off + n], raw[:, off:off + n], AF.Ln)
                                    P.act(raw[:, off:off + n], raw[:, off:off + n], AF.Exp, scale=-0.5)
                                if j == 0:
                                    P.stt('dve', dstT[:], acc[:], 128 ** -0.5, raw[:, 0:T], ALU.mult, ALU.mult)
                                else:
                                    P.tt('dve', dstT[:], acc[:], raw[:, 0:T], ALU.mult)
                                if j == 1:
                                    P.copy('pool', kTb[:], dstT[:])
                                P.memset('pool', raw[:, 0:1], 0.0)
                                P.memset('pool', raw[:, 257:259], 0.0)
                            if j >= 1:
                                src = kTf if j == 1 else acc
                                dtok = k_tok if j == 1 else v_tok
                                for tb4 in range(0, NBLK, 4):
                                    nb4 = min(4, NBLK - tb4)
                                    pt = psum()
                                    for i in range(nb4):
                                        tb = tb4 + i
                                        P.transpose(pt[:, i * 128:(i + 1) * 128], src[:, tb * 128:(tb + 1) * 128], identf[:])
                                    P.copy(rot(('act', 'dve')), dtok[:, tb4:tb4 + nb4, :].rearrange("p b d -> p (b d)"), pt[:, 0:nb4 * 128])
                        if l == 0 and s == 0 and h == 0:
                            dump('dn_kT', kTf[:])
                            dump('dn_vtok', v_tok[:])
                        P.memset('pool', oT[:], 0.0)
                        orders = [list(range(NBLK)), [1, 0] + list(range(17, 1, -1))]

                        def prep_gen(d, blk, slot, tk):
                            dh = d * 4 + h
                            cs = slice(blk * 128, (blk + 1) * 128)
                            gbc = tmp('gbc', tk)
                            P.copy('pool', gbc[:], g_tok[:, blk, dh:dh + 1].to_broadcast([128, 128]))
                            psg = psum()
                            P.mm(psg[:, 0:128], gbc[:], mcum[d])
                            egc = ring[(d, 'egc')][slot]
                            gcb = tmp('gcb', tk)
                            P.copy('dve', gcb[:], psg[:, 0:128])
                            P.act(egc[:], gcb[:], AF.Exp)
                            diff = tmp('diff', tk)
                            P.ts('dve', diff[:], gcb[:], gc_tok[:, blk, dh:dh + 1], 0.0, ALU.subtract, ALU.min)
                            P.act(diff[:], diff[:], AF.Exp)
                            yield
                            pkk = psum()
                            P.mm(pkk[:, 0:128], kTb[:, cs], kTb[:, cs])
                            pqk = psum()
                            P.mm(pqk[:, 0:128], kTb[:, cs], qTb[:, cs])
                            m1 = tmp('m1', tk, BF16)
                            m2 = tmp('m2', tk, BF16)
                            P.tt('dve', m1[:], diff[:], mstr[d], ALU.mult)
                            P.tt('dve', m2[:], diff[:], minc[d], ALU.mult)
                            C = tmp('C0', tk, BF16)
                            P.stt('dve', C[:], pkk[:, 0:128], nbeta[:, blk, dh:dh + 1], m1[:], ALU.mult, ALU.mult)
                            P.tt('dve', ring[(d, 'aqk')][slot][:], pqk[:, 0:128], m2[:], ALU.mult)
                            yield
                            P.tt('pool', ring[(d, 'kg')][slot][:], kTf[:, cs], egc[:], ALU.mult)
                            P.tt('pool', ring[(d, 'qd')][slot][:], qTb[:, cs], egc[:], ALU.mult)
                            P.act(ring[(d, 'kdec')][slot][:], k_tok[:, blk, :], AF.Identity, scale=edk[:, blk, dh:dh + 1])
                            yield
                            pb = psum()[:].bitcast(BF16)
                            P.transpose(pb[:, 0:128], C[:], identb[:])
                            B0 = tmp('B0', tk, BF16)
                            P.copy('act', B0[:], pb[:, 0:128])
                            Tm = tmp('Tm', tk, BF16)
                            Um = tmp('Um', tk, BF16)
                            G0 = tmp('G0', tk, BF16)
                            H0 = tmp('H0', tk, BF16)
                            P.tt('dve', G0[:], C[:], lvl[:, 0, :], ALU.mult)
                            P.tt('dve', Um[:], G0[:], identb[:], ALU.add)
                            P.tt('dve', H0[:], B0[:], lvl[:, 0, :], ALU.mult)
                            P.tt('dve', Tm[:], H0[:], identb[:], ALU.add)
                            yield
                            for li in range(1, 7):
                                lastl = (li == 6)
                                Gs = tmp('Gs', tk * 2 + (li % 2), BF16)
                                P.tt('pool', Gs[:], C[:], lvl[:, li, :], ALU.mult)
                                pX = psum()
                                P.mm(pX[:, 0:128], Gs[:], Tm[:])
                                X16 = tmp('X16', tk, BF16)
                                P.copy('act', X16[:], pX[:, 0:128])
                                yield
                                pYT = psum()
                                P.mm(pYT[:, 0:128], X16[:], Um[:])
                                if not lastl:
                                    pY = psum()
                                    P.mm(pY[:, 0:128], Um[:], X16[:])
                                    P.tt('dve', Tm[:], Tm[:], pY[:, 0:128], ALU.add)
                                    P.tt('dve', Um[:], Um[:], pYT[:, 0:128], ALU.add)
                                else:
                                    P.tt('dve', ring[(d, 'TT')][slot][:], Um[:], pYT[:, 0:128], ALU.add)
                                yield

                        def scan_gen(d, blk, slot):
                            dh = d * 4 + h
                            halves = (0, 1) if d == 0 else (1, 0)
                            kg = ring[(d, 'kg')][slot]
                            qd = ring[(d, 'qd')][slot]
                            TTs = ring[(d, 'TT')][slot]
                            aqk = ring[(d, 'aqk')][slot]
                            kdec = ring[(d, 'kdec')][slot]
                            egc = ring[(d, 'egc')][slot]
                            bank = psums[6 + d]
                            ps1 = bank[:, 0:128]
                            ps2 = bank[:, 128:256]
                            ps3 = bank[:, 256:384]
                            ps4 = bank[:, 384:512]
                            P.mm(ps1, kg[:], S[d][:])
                            yield
                            X = Xz[d][0]
                            P.tt('dve', X[:], v_tok[:, blk, :], ps1, ALU.subtract)
                            yield
                            P.mm(ps2, TTs[:, :], X[:, :])
                            yield
                            vn = vnz[d][0]
                            P.ts('dve', vn[:], ps2, beta[:, blk, dh:dh + 1], None, ALU.mult)
                            v16 = vn16[d][0]
                            P.copy('act', v16[:], vn[:])
                            yield
                            P.mm(ps4, kdec[:, :], vn[:, :])
                            P.mm(ps3, S16[d][:], qd[:, :], start=True, stop=False)
                            P.mm(ps3, v16[:, :], aqk[:, :], start=False, stop=True)
                            yield
                            ccol = 127 if d == 0 else 0
                            P.stt('dve', S[d][:], S[d][:], egc[:, ccol:ccol + 1], ps4, ALU.mult, ALU.add)
                            P.copy('act', S16[d][:], S[d][:])
                            oc = slice(blk * 128, (blk + 1) * 128)
                            P.tt('dve', oT[:, oc], oT[:, oc], ps3, ALU.add)
                            yield

                        for d in range(2):
                            P.memset('pool', S[d][:], 0.0)
                            P.memset('pool', S16[d][:], 0.0)
                        LEAD = NSLOT - 1
                        NPREP = 2
                        preps = {d: [] for d in range(2)}
                        pdone = {d: set() for d in range(2)}
                        scans = {d: None for d in range(2)}
                        pi = {d: 0 for d in range(2)}
                        si = {d: 0 for d in range(2)}
                        active = True
                        while active:
                            active = False
                            rnd += 1
                            if bg is not None and rnd % 6 == 0:
                                try:
                                    next(bg)
                                except StopIteration:
                                    bg = None
                            for d in range(2):
                                if len(preps[d]) < NPREP and pi[d] < NBLK and pi[d] - si[d] < NSLOT:
                                    preps[d].append((pi[d], prep_gen(d, orders[d][pi[d]], pi[d] % NSLOT, d * NPREP + pi[d] % NPREP)))
                                    pi[d] += 1
                                for item in list(preps[d]):
                                    active = True
                                    try:
                                        next(item[1])
                                    except StopIteration:
                                        preps[d].remove(item)
                                        pdone[d].add(item[0])
                                if scans[d] is None and si[d] < NBLK and si[d] in pdone[d]:
                                    scans[d] = scan_gen(d, orders[d][si[d]], si[d] % NSLOT)
                                if scans[d] is not None:
                                    active = True
                                    try:
                                        next(scans[d])
                                    except StopIteration:
                                        scans[d] = None
                                        si[d] += 1
                                if si[d] < NBLK or pi[d] < NBLK:
                                    active = True
                        if l == 0 and s == 0 and h == 0:
                            dump('dn_oT', oT[:])
                        P.act(acc[:], oT[:], AF.Square)
                        for (off, n) in TTILES:
                            ps = psum()
                            P.mm(ps[:, 0:n], onesf[:], acc[:, off:off + n])
                            P.ts('dve', acc[:, off:off + n], ps[:, 0:n], 1.0 / 128, RMS_EPS, ALU.mult, ALU.add)
                        P.act(acc[:], acc[:], AF.Ln)
                        P.act(acc[:], acc[:], AF.Exp, scale=-0.5)
                        P.stt('dve', acc[:], oT[:], cng[:, 0:1], acc[:], ALU.mult, ALU.mult)
                        P.tt('pool', ycb[:], acc[:], czT[:, h, :], ALU.mult)
                        P.dma(yT_d[2, h * 128:(h + 1) * 128, :], ycb[:], queue='pool')
                    if bg is not None:
                        for _ in bg:
                            pass
                    barrier()
                if stop == 'C':
                    break

                with ExitStack() as st:
                    def sb(name, shape, dt):
                        return st.enter_context(nc.sbuf_tensor(uniq(name), list(shape), dt))
                    set_pool(range(8))
                    Wbr = sb("Wbr", [128, 12, D], BF16)
                    Wo = sb("Wo", [128, 8, D], BF16)
                    P.dma(Wbr[:], wbr_v[l])
                    P.dma(Wo[:], wout_v[l])
                    lng = sb("lng", [128, D], F32)
                    lnb = sb("lnb", [128, D], F32)
                    P.dma(lng[:], ln_g[l:l + 1, :].partition_broadcast(128))
                    P.dma(lnb[:], ln_b[l:l + 1, :].partition_broadcast(128))
                    gbc_ = [sb("gatebc%d" % i, [128, D], F32) for i in range(2)]
                    P.dma(gbc_[0][:], modrow_d[l, 2:3, 2 * D:3 * D].partition_broadcast(128))
                    P.dma(gbc_[1][:], modrow_d[l, s:s + 1, 2 * D:3 * D].partition_broadcast(128))
                    Wg = [sb("Wg%d" % i, [128, 8, 3, 128], BF16) for i in range(3)]
                    yt3 = [sb("yt3_%d" % i, [128, 3, 4, 512], BF16) for i in range(2)]
                    sgm = [sb("sgm%d" % i, [128, 512], BF16) for i in range(3)]
                    mT = sb("mT", [128, 8, 512], BF16)
                    macc = sb("macc", [128, 512], F32)
                    mtmp = [sb("mtmp%d" % i, [128, 512], F32) for i in range(2)]
                    xtl = [sb("xtl%d" % i, [128, D], F32) for i in range(2)]
                    tt_ = [sb("ttl%d" % i, [128, D], F32) for i in range(2)]
                    stat = [sb("stat%d" % i, [128, 8], F32) for i in range(2)]
                    junk = sb("junk", [128, D], F32)
                    yv = yT_d.rearrange("r (c p) t -> p r c t", p=128)
                    iw = 0
                    for ti, (off, n) in enumerate(TTILES):
                        y3 = yt3[ti % 2]
                        for r in range(3):
                            P.dma(y3[:, r, :, 0:n], yv[:, r, :, off:off + n])
                        for m in range(8):
                            wg = Wg[iw % 3]
                            iw += 1
                            P.dma(wg[:], wbf_g[l, m])
                            for r in range(3):
                                pg = psum()
                                for k in range(8):
                                    P.mm(pg[:, 0:n], wg[:, k, r, :], uT[:, k, off:off + n], start=(k == 0), stop=(k == 7))
                                pb_ = psum()
                                for k in range(4):
                                    P.mm(pb_[:, 0:n], Wbr[:, r * 4 + k, m * 128:(m + 1) * 128], y3[:, r, k, 0:n], start=(k == 0), stop=(k == 3))
                                sg_ = sgm[r]
                                P.act(sg_[:, 0:n], pg[:, 0:n], AF.Sigmoid)
                                if r == 0:
                                    P.tt('dve', macc[:, 0:n], pb_[:, 0:n], sg_[:, 0:n], ALU.mult)
                                else:
                                    mt_ = mtmp[r % 2]
                                    P.tt('dve', mt_[:, 0:n], pb_[:, 0:n], sg_[:, 0:n], ALU.mult)
                                    if r == 1:
                                        P.tt('pool', macc[:, 0:n], macc[:, 0:n], mt_[:, 0:n], ALU.add)
                                    else:
                                        P.tt('pool', mT[:, m, 0:n], macc[:, 0:n], mt_[:, 0:n], ALU.add)
                        for sbk in range(n // 128):
                            tau0 = off + sbk * 128
                            gb = gbc_[0] if tau0 < NCTX else gbc_[1]
                            x_ = xtl[sbk % 2]
                            t_ = tt_[sbk % 2]
                            st_ = stat[sbk % 2]
                            P.dma(x_[:], xres[s, tau0:tau0 + 128, :])
                            for hc in range(2):
                                po = psum()
                                for k in range(8):
                                    P.mm(po[:, :], mT[:, k, sbk * 128:(sbk + 1) * 128], Wo[:, k, hc * 512:(hc + 1) * 512], start=(k == 0), stop=(k == 7))
                                hs = slice(hc * 512, (hc + 1) * 512)
                                P.tt('dve', t_[:, hs], po[:, :], gb[:, hs], ALU.mult)
                                P.stt('dve', t_[:, hs], x_[:, hs], ALU_ALPHA, t_[:, hs], ALU.mult, ALU.add)
                            P.reduce('dve', st_[:, 0:1], t_[:], ALU.add)
                            P.act(junk[:], t_[:], AF.Square)
                            P.reduce('dve', st_[:, 1:2], junk[:], ALU.add)
                            P.ts('dve', st_[:, 2:3], st_[:, 0:1], 1.0 / D, None, ALU.mult)
                            P.tt('dve', st_[:, 3:4], st_[:, 2:3], st_[:, 2:3], ALU.mult)
                            P.stt('dve', st_[:, 4:5], st_[:, 1:2], 1.0 / D, st_[:, 3:4], ALU.mult, ALU.subtract)
                            P.ts('dve', st_[:, 5:6], st_[:, 4:5], LN_EPS, None, ALU.add)
                            P.act(st_[:, 5:6], st_[:, 5:6], AF.Ln)
                            P.act(st_[:, 5:6], st_[:, 5:6], AF.Exp, scale=-0.5)
                            P.ts('dve', t_[:], t_[:], st_[:, 2:3], st_[:, 5:6], ALU.subtract, ALU.mult)
                            P.tt('pool', t_[:], t_[:], lng[:], ALU.mult)
                            P.tt('dve', t_[:], t_[:], lnb[:], ALU.add)
                            if last:
                                if tau0 >= NCTX:
                                    P.dma(y_out[s, tau0 - NCTX:tau0 - NCTX + 128, :], t_[:], queue='pool')
                            else:
                                P.dma(xres[s, tau0:tau0 + 128, :], t_[:], queue='pool')
                    barrier()
            if stop is not None:
                break
        import os as _os
        if dbg and 'xres' in dbg_out and not _os.environ.get('NODUMP'):
            P.dma(dbg_out['xres'], xres[0])
        if dbg and 'yT' in dbg_out:
            nbr = {'A': 1, 'B': 2}.get(stop, 3)
            if stop != 'U':
                P.dma(dbg_out['yT'][0:nbr], yT_d[0:nbr])
        barrier()
        P.emit()
    return nc, P


ALU_ALPHA = ALPHA

_CACHE = {}


def kernel(**inputs):
    n = 8
    if 'nc' not in _CACHE:
        _CACHE['nc'] = build()[0]
    nc = _CACHE['nc']
    consts = host_consts()
    f = lambda a: np.ascontiguousarray(np.asarray(a, dtype=np.float32))
    shared = {
        'c_ctx': f(inputs['c_ctx']).reshape(1, D),
        'w_ada': f(inputs['w_ada']), 'b_ada': f(inputs['b_ada']), 'w_in': f(inputs['w_in']),
        'a_sink': f(inputs['a_sink']), 'b_conv_w': f(inputs['b_conv_w']), 'b_conv_b': f(inputs['b_conv_b']),
        'b_norm_g': f(inputs['b_norm_g']), 'b_norm_b': f(inputs['b_norm_b']), 'c_conv_w': f(inputs['c_conv_w']),
        'c_a_log': f(inputs['c_a_log']).reshape(4, 8), 'c_dt_bias': f(inputs['c_dt_bias']).reshape(4, 8),
        'c_norm_g': f(inputs['c_norm_g']), 'w_branch': f(inputs['w_branch']).reshape(4, 1536, D),
        'w_out': f(inputs['w_out']), 'ln_g': f(inputs['ln_g']), 'ln_b': f(inputs['ln_b']),
    }
    shared.update(consts)
    x = f(inputs['x'])
    ctx = f(inputs['ctx'])
    c = f(inputs['c'])
    in_maps = []
    for i in range(n):
        m = dict(shared)
        m['x'] = x[2 * i:2 * i + 2]
        m['ctx'] = ctx[2 * i:2 * i + 2]
        m['c'] = c[2 * i:2 * i + 2]
        in_maps.append(m)
    res = run_bass_kernel_spmd(nc, in_maps, core_ids=list(range(n)))
    return np.concatenate([r['y'] for r in res.results], axis=0).astype(np.float32)
```
